# Optimizing a Trainium2 kernel written in Bass

```python
import jax
import jax.numpy as jnp
from jax import lax
import numpy as np

D_MODEL = 1024
BATCH = 2
SEQ = 8192
DEPTH = 2

GRID_W = 64
CTX_LEN = 256
N_EVEN = (DEPTH + 1) // 2
N_ODD = DEPTH // 2
D_MIX = D_MODEL
D_FF = 4 * D_MODEL
N_MOD = 6
EPS = 1e-6

ML_WIDTH = D_MIX // 2
ML_HEADS = 4
ML_DIM = ML_WIDTH // ML_HEADS
ML_CHUNK = 64
ML_CONV = 3
RW_WIDTH = D_MIX - ML_WIDTH
RW_HEAD_DIM = 64
RW_HEADS = RW_WIDTH // RW_HEAD_DIM
RW_DECAY_RANK = 64
RW_ICL_RANK = 64
RW_GATE_RANK = 128
RW_GN_EPS = 64e-5
RW_FEAT = 3 * RW_WIDTH + 2 * RW_DECAY_RANK + 2 * RW_ICL_RANK + RW_GATE_RANK
EV_SIZES = (ML_WIDTH, ML_WIDTH, ML_WIDTH, ML_WIDTH, 4 * ML_HEADS, RW_FEAT)
EV_PROJ = sum(EV_SIZES)
HG_WIDTH = D_MIX // 2
HG_HEADS = 4
HG_DIM = HG_WIDTH // HG_HEADS
HG_CHUNK = 32
MLA_WIDTH = D_MIX - HG_WIDTH
MLA_HEADS = 4
MLA_NOPE = 128
MLA_ROPE = 64
MLA_V = MLA_WIDTH // MLA_HEADS
MLA_Q_RANK = 256
MLA_KV_RANK = 128
OD_SIZES = (HG_WIDTH, HG_WIDTH, HG_WIDTH, HG_WIDTH, HG_WIDTH, MLA_Q_RANK, MLA_KV_RANK + MLA_ROPE)
OD_PROJ = sum(OD_SIZES)
ATTN_BLOCK = 128
ROPE_BASE = 10000.0

kernel_name = 'hybrid_mlstm_rwkv7_hgrn2_mla_dit'


def split_cols(t, sizes):
    cuts = [int(s) for s in np.cumsum(sizes)[:-1]]
    return jnp.split(t, cuts, axis=-1)


def rmsnorm(x, g):
    xf = x.astype(jnp.float32)
    y = xf * lax.rsqrt(jnp.mean(xf * xf, axis=-1, keepdims=True) + EPS)
    return (y * g.astype(jnp.float32)).astype(x.dtype)


def head_rmsnorm(y, g):
    B, T, H, d = y.shape
    yf = y.astype(jnp.float32)
    yf = yf * lax.rsqrt(jnp.mean(yf * yf, axis=-1, keepdims=True) + EPS)
    return yf.reshape(B, T, H * d) * g.astype(jnp.float32)


def head_groupnorm(y, w, b):
    H, d = y.shape[-2:]
    mu = jnp.mean(y, axis=-1, keepdims=True)
    var = jnp.mean(jnp.square(y - mu), axis=-1, keepdims=True)
    return (y - mu) * lax.rsqrt(var + RW_GN_EPS) * w.astype(jnp.float32).reshape(H, d) + b.astype(jnp.float32).reshape(H, d)


def modulate(h, shift, scale):
    return h * (1.0 + scale) + shift


def sqrelu_mlp(h, w1, w2):
    return jnp.square(jax.nn.relu(h @ w1)) @ w2


def two_seq(fn, t, n_ctx):
    return jnp.concatenate([fn(t[:, :n_ctx]), fn(t[:, n_ctx:])], axis=1)


def neighbour_mean(t):
    tp = jnp.pad(t, ((0, 0), (1, 1), (0, 0)))
    return 0.5 * (tp[:, :-2] + tp[:, 2:])


def centred_dwconv(t, w):
    k = w.shape[0]
    half = k // 2
    n = t.shape[1]
    tp = jnp.pad(t, ((0, 0), (half, half), (0, 0)))
    out = tp[:, 0:n] * w[0]
    for j in range(1, k):
        out = out + tp[:, j:j + n] * w[j]
    return out


def dir_stack(t_fw, t_bw, n_ctx):
    bw = jnp.concatenate([jnp.flip(t_bw[:, :n_ctx], 1), jnp.flip(t_bw[:, n_ctx:], 1)], axis=1)
    return jnp.concatenate([t_fw, bw], axis=0)


def dir_merge(y, n_batch, n_ctx):
    y_fw, y_bw = y[:n_batch], y[n_batch:]
    bw = jnp.concatenate([jnp.flip(y_bw[:, :n_ctx], 1), jnp.flip(y_bw[:, n_ctx:], 1)], axis=1)
    return y_fw + bw


def to_chunks(a, L):
    Z, T = a.shape[:2]
    a = a.reshape((Z, T // L, L) + a.shape[2:])
    perm = (1, 0, 3, 2) + tuple(range(4, a.ndim))
    return a.transpose(perm)


def from_chunks(a):
    nc, Z, H, L, d = a.shape
    return a.transpose((1, 0, 3, 2, 4)).reshape(Z, nc * L, H, d)


def axial_rope(x, row, col):
    half = x.shape[-1] // 2
    quarter = half // 2
    inv = ROPE_BASE ** (-jnp.arange(quarter, dtype=jnp.float32) / quarter)

    def rot(xa, pos):
        ang = pos.astype(jnp.float32)[:, None] * inv[None, :]
        ang = ang.reshape((ang.shape[0],) + (1,) * (xa.ndim - 3) + (quarter,))
        cos = jnp.cos(ang).astype(xa.dtype)
        sin = jnp.sin(ang).astype(xa.dtype)
        x1, x2 = xa[..., :quarter], xa[..., quarter:]
        return jnp.concatenate([x1 * cos - x2 * sin, x1 * sin + x2 * cos], axis=-1)

    return jnp.concatenate([rot(x[..., :half], row), rot(x[..., half:], col)], axis=-1)


def block_attention(q_nope, q_rope, k_nope, k_rope, v, scale):
    B, Nq, H, _ = q_nope.shape
    nb = Nq // ATTN_BLOCK

    def blocks(t):
        return jnp.moveaxis(t.reshape((B, nb, ATTN_BLOCK) + t.shape[2:]), 1, 0)

    def attend(qs):
        qn, qr = qs
        s = jnp.einsum('bqhd,bkhd->bhqk', qn, k_nope) + jnp.einsum('bqhd,bkd->bhqk', qr, k_rope)
        p = jax.nn.softmax(s.astype(jnp.float32) * scale, axis=-1)
        return jnp.einsum('bhqk,bkhd->bqhd', p.astype(v.dtype), v)

    o = lax.map(attend, (blocks(q_nope), blocks(q_rope)))
    return jnp.moveaxis(o, 0, 1).reshape(B, Nq, H * v.shape[-1])


def mlstm_chunkwise(q, k, v, ig, lf):
    f32 = jnp.float32
    q, k, v, ig, lf = (t.astype(f32) for t in (q, k, v, ig, lf))
    Z, T, H, dk = q.shape
    dv = v.shape[-1]
    L = ML_CHUNK
    k = k * dk ** -0.5
    xs = tuple(to_chunks(t, L) for t in (q, k, v, ig, lf))
    mask = jnp.tril(jnp.ones((L, L), dtype=bool))

    def step(carry, inp):
        C, n, m = carry
        qc, kc, vc, ic, fc = inp
        b = jnp.cumsum(fc, axis=-1)
        logd = jnp.where(mask, b[..., :, None] - b[..., None, :] + ic[..., None, :], -jnp.inf)
        m_inter = b + m[..., None]
        m_t = jnp.maximum(m_inter, jnp.max(logd, axis=-1))
        s = jnp.einsum('zhtd,zhsd->zhts', qc, kc) * jnp.exp(logd - m_t[..., None])
        w_inter = jnp.exp(m_inter - m_t)
        num = jnp.einsum('zhts,zhsv->zhtv', s, vc) + w_inter[..., None] * jnp.einsum('zhtd,zhvd->zhtv', qc, C)
        den = jnp.sum(s, axis=-1) + w_inter * jnp.einsum('zhtd,zhd->zht', qc, n)
        h = num / jnp.maximum(jnp.abs(den), jnp.exp(-m_t))[..., None]
        b_last = b[..., -1]
        logw = b_last[..., None] - b + ic
        m_new = jnp.maximum(b_last + m, jnp.max(logw, axis=-1))
        w = jnp.exp(logw - m_new[..., None])
        decay = jnp.exp(b_last + m - m_new)
        C = decay[..., None, None] * C + jnp.einsum('zhs,zhsv,zhsd->zhvd', w, vc, kc)
        n = decay[..., None] * n + jnp.einsum('zhs,zhsd->zhd', w, kc)
        return (C, n, m_new), h

    init = (jnp.zeros((Z, H, dv, dk), f32), jnp.zeros((Z, H, dk), f32), jnp.zeros((Z, H), f32))
    _, h = lax.scan(step, init, xs)
    return from_chunks(h)


def rwkv7_scan(r, w, k, v, a_vec, b_vec):
    Z, T, H, dk = k.shape
    dv = v.shape[-1]
    xs = tuple(jnp.moveaxis(t, 1, 0) for t in (r, w, k, v, a_vec, b_vec))

    def step(S, inp):
        rt, wt, kt, vt, at, bt = inp
        sa = jnp.einsum('zhvk,zhk->zhv', S, at)
        S = S * wt[:, :, None, :] + sa[..., None] * bt[:, :, None, :] + vt[..., None] * kt[:, :, None, :]
        return S, jnp.einsum('zhvk,zhk->zhv', S, rt)

    _, y = lax.scan(step, jnp.zeros((Z, H, dv, dk), jnp.float32), xs)
    return jnp.moveaxis(y, 0, 1)


def hgrn2_chunkwise(q, k, v, log_f):
    Z, T, H, dk = q.shape
    dv = v.shape[-1]
    L = HG_CHUNK
    q = q * dk ** -0.5
    xs = tuple(to_chunks(t, L) for t in (q, k, v, log_f))
    mask = jnp.tril(jnp.ones((L, L), dtype=bool))[:, :, None]

    def step(S, inp):
        qc, kc, vc, gc = inp
        bc = jnp.cumsum(gc, axis=2)
        rel = jnp.where(mask, bc[:, :, :, None, :] - bc[:, :, None, :, :], -jnp.inf)
        att = jnp.einsum('zhtd,zhsd,zhtsd->zhts', qc, kc, jnp.exp(rel))
        o = att @ vc + jnp.einsum('zhtd,zhdv->zhtv', qc * jnp.exp(bc), S)
        b_last = bc[:, :, -1, :]
        S = jnp.exp(b_last)[..., None] * S + jnp.einsum('zhsd,zhsv->zhdv', kc * jnp.exp(b_last[:, :, None, :] - bc), vc)
        return S, o

    _, o = lax.scan(step, jnp.zeros((Z, H, dk, dv), jnp.float32), xs)
    return from_chunks(o)


def even_mixer(hj, n_ctx, need_ctx, w_in, ml_conv, ml_gate_b, ml_norm, rw_mu, rw_w0, rw_w2, rw_a0, rw_a2,
               rw_g2, rw_kk, rw_ka, rw_rk, rw_ln_w, rw_ln_b):
    dt = hj.dtype
    f32 = jnp.float32
    B, T, _ = hj.shape

    def hd(t, d):
        return t.reshape(B, T, -1, d)

    def z(a, b=None):
        return dir_stack(a, a if b is None else b, n_ctx)

    mq, mk, mv, mo, mg, rfeat = split_cols(hj @ w_in, EV_SIZES)

    qk = two_seq(lambda t: centred_dwconv(t, ml_conv), jnp.concatenate([mq, mk], axis=-1), n_ctx)
    q, k = jnp.split(qk, 2, axis=-1)
    gates = mg.reshape(B, T, 4, ML_HEADS).astype(f32) + ml_gate_b.astype(f32)
    ig_fw, lf_fw = gates[:, :, 0], jax.nn.log_sigmoid(gates[:, :, 1])
    ig_bw, lf_bw = gates[:, :, 2], jax.nn.log_sigmoid(gates[:, :, 3])
    hm = mlstm_chunkwise(z(hd(q, ML_DIM)), z(hd(k, ML_DIM)), z(hd(mv, ML_DIM)), z(ig_fw, ig_bw), z(lf_fw, lf_bw))
    y_ml = head_rmsnorm(dir_merge(hm, B, n_ctx), ml_norm) * jax.nn.sigmoid(mo.astype(f32))

    rfeat = two_seq(lambda t: t + (neighbour_mean(t) - t) * rw_mu, rfeat, n_ctx)
    r, k, v, wl, al, gl = split_cols(rfeat, (RW_WIDTH, RW_WIDTH, RW_WIDTH, 2 * RW_DECAY_RANK, 2 * RW_ICL_RANK, RW_GATE_RANK))
    r = hd(r.astype(f32), RW_HEAD_DIM)
    k = k.astype(f32)
    v = hd(v.astype(f32), RW_HEAD_DIM)
    kk = hd(k * rw_kk, RW_HEAD_DIM)
    kk = kk * lax.rsqrt(jnp.sum(kk * kk, axis=-1, keepdims=True) + 1e-12)
    rk = rw_rk.reshape(RW_HEADS, RW_HEAD_DIM)
    ka = rw_ka.reshape(RW_HEADS, RW_HEAD_DIM)
    wl_dirs = jnp.split(wl.astype(f32), 2, axis=-1)
    al_dirs = jnp.split(al.astype(f32), 2, axis=-1)
    decay, key_d, icl = [], [], []
    bonus = 0.0
    for d in range(2):
        w_log = -jax.nn.softplus(-(rw_w0[d] + jnp.tanh(wl_dirs[d]) @ rw_w2[d])) - 0.5
        a = hd(jax.nn.sigmoid(rw_a0[d] + al_dirs[d] @ rw_a2[d]), RW_HEAD_DIM)
        k_d = hd(k, RW_HEAD_DIM) * (1.0 + (a - 1.0) * ka)
        decay.append(hd(jnp.exp(-jnp.exp(w_log)), RW_HEAD_DIM))
        key_d.append(k_d)
        icl.append(kk * a)
        bonus = bonus + jnp.sum(r * k_d * rk, axis=-1, keepdims=True) * v
    y = rwkv7_scan(z(r), z(decay[0], decay[1]), z(key_d[0], key_d[1]), z(v), z(-kk), z(icl[0], icl[1]))
    y = head_groupnorm(dir_merge(y, B, n_ctx), rw_ln_w, rw_ln_b) + bonus
    y_rw = y.reshape(B, T, RW_WIDTH) * (jax.nn.sigmoid(gl.astype(f32)) @ rw_g2)

    out = jnp.concatenate([y_ml, y_rw], axis=-1).astype(dt)
    y_ctx = out[:, :n_ctx] if need_ctx else None
    return y_ctx, out[:, n_ctx:]


def odd_mixer(hj, n_ctx, need_ctx, lb, row, col, w_in, hg_norm, mla_q_norm, mla_w_qb, mla_kv_norm, mla_w_kvb):
    dt = hj.dtype
    f32 = jnp.float32
    B, T, _ = hj.shape

    def hd(t, d):
        return t.reshape(B, T, -1, d)

    def z(a, b=None):
        return dir_stack(a, a if b is None else b, n_ctx)

    hq, hf_fw, hf_bw, hi, hg, cq, ckv = split_cols(hj @ w_in, OD_SIZES)

    def forget(pre):
        pre = pre.astype(f32)
        log_f = jnp.logaddexp(jnp.log(lb), jnp.log1p(-lb) + jax.nn.log_sigmoid(pre))
        one_minus_f = (1.0 - lb) * jax.nn.sigmoid(-pre)
        return hd(log_f, HG_DIM), hd(one_minus_f, HG_DIM)

    lf_fw, kf_fw = forget(hf_fw)
    lf_bw, kf_bw = forget(hf_bw)
    q = hd(jax.nn.silu(hq.astype(f32)), HG_DIM)
    i = hd(hi.astype(f32), HG_DIM)
    o = hgrn2_chunkwise(z(q), z(kf_fw, kf_bw), z(i), z(lf_fw, lf_bw))
    y_hg = (head_rmsnorm(dir_merge(o, B, n_ctx), hg_norm) * jax.nn.silu(hg.astype(f32))).astype(dt)

    q_all = hd(rmsnorm(cq, mla_q_norm) @ mla_w_qb, MLA_NOPE + MLA_ROPE)
    q_nope, q_rope = q_all[..., :MLA_NOPE], q_all[..., MLA_NOPE:]
    c_kv, k_rope = ckv[..., :MLA_KV_RANK], ckv[..., MLA_KV_RANK:]
    kv = hd(rmsnorm(c_kv, mla_kv_norm) @ mla_w_kvb, MLA_NOPE + MLA_V)
    k_nope, v = kv[..., :MLA_NOPE], kv[..., MLA_NOPE:]
    k_rope = jnp.concatenate([k_rope[:, :n_ctx], axial_rope(k_rope[:, n_ctx:], row, col)], axis=1)
    scale = (MLA_NOPE + MLA_ROPE) ** -0.5
    y_att = block_attention(q_nope[:, n_ctx:], axial_rope(q_rope[:, n_ctx:], row, col), k_nope, k_rope, v, scale)
    y_lat = jnp.concatenate([y_hg[:, n_ctx:], y_att.astype(dt)], axis=-1)
    y_ctx = None
    if need_ctx:
        y_c = block_attention(q_nope[:, :n_ctx], q_rope[:, :n_ctx], k_nope[:, :n_ctx], k_rope[:, :n_ctx], v[:, :n_ctx], scale)
        y_ctx = jnp.concatenate([y_hg[:, :n_ctx], y_c.astype(dt)], axis=-1)
    return y_ctx, y_lat


def setup_inputs(seed: int = 0) -> dict:
    key = jax.random.key(seed)
    ks = iter(jax.random.split(key, 48))

    def nrm(shape, s):
        return jax.random.normal(next(ks), shape, jnp.float32) * s

    def uni(shape, lo, hi):
        return jax.random.uniform(next(ks), shape, jnp.float32, lo, hi)

    D = D_MODEL
    ig_b = nrm((N_EVEN, 2, ML_HEADS), 0.1)
    fg_b = uni((N_EVEN, 2, ML_HEADS), 3.0, 6.0)
    return {
        'x': nrm((BATCH, SEQ, D), 1.0),
        'c': nrm((BATCH, D), 1.0),
        'ctx': nrm((BATCH, CTX_LEN, D), 1.0),
        'c_ctx': nrm((D,), 1.0),
        'ada_w': nrm((DEPTH, D, N_MOD * D), 0.5 * D ** -0.5),
        'ada_b': nrm((DEPTH, N_MOD * D), 0.02),
        'norm_g': 1.0 + nrm((DEPTH, 4, D), 0.02),
        'out_w': nrm((DEPTH, D_MIX, D), D_MIX ** -0.5),
        'mlp_w1': nrm((DEPTH, D, D_FF), D ** -0.5),
        'mlp_w2': nrm((DEPTH, D_FF, D), D_FF ** -0.5),
        'hg_lb': nrm((DEPTH, HG_WIDTH), 0.5),
        'ev_w_in': nrm((N_EVEN, D, EV_PROJ), D ** -0.5),
        'ml_conv': nrm((N_EVEN, ML_CONV, 2 * ML_WIDTH), ML_CONV ** -0.5),
        'ml_gate_b': jnp.stack([ig_b[:, 0], fg_b[:, 0], ig_b[:, 1], fg_b[:, 1]], axis=1),
        'ml_norm': 1.0 + nrm((N_EVEN, ML_WIDTH), 0.02),
        'rw_mu': uni((N_EVEN, RW_FEAT), 0.0, 1.0),
        'rw_w0': uni((N_EVEN, 2, RW_WIDTH), -6.0, -1.0),
        'rw_w2': nrm((N_EVEN, 2, RW_DECAY_RANK, RW_WIDTH), 0.5 * RW_DECAY_RANK ** -0.5),
        'rw_a0': nrm((N_EVEN, 2, RW_WIDTH), 0.1),
        'rw_a2': nrm((N_EVEN, 2, RW_ICL_RANK, RW_WIDTH), 0.5 * RW_ICL_RANK ** -0.5),
        'rw_g2': nrm((N_EVEN, RW_GATE_RANK, RW_WIDTH), RW_GATE_RANK ** -0.5),
        'rw_kk': 0.85 + nrm((N_EVEN, RW_WIDTH), 0.05),
        'rw_ka': 1.0 + nrm((N_EVEN, RW_WIDTH), 0.05),
        'rw_rk': nrm((N_EVEN, RW_WIDTH), 0.1),
        'rw_ln_w': 1.0 + nrm((N_EVEN, RW_WIDTH), 0.02),
        'rw_ln_b': nrm((N_EVEN, RW_WIDTH), 0.02),
        'od_w_in': nrm((N_ODD, D, OD_PROJ), D ** -0.5),
        'hg_norm': 1.0 + nrm((N_ODD, HG_WIDTH), 0.02),
        'mla_q_norm': 1.0 + nrm((N_ODD, MLA_Q_RANK), 0.02),
        'mla_w_qb': nrm((N_ODD, MLA_Q_RANK, MLA_HEADS * (MLA_NOPE + MLA_ROPE)), MLA_Q_RANK ** -0.5),
        'mla_kv_norm': 1.0 + nrm((N_ODD, MLA_KV_RANK), 0.02),
        'mla_w_kvb': nrm((N_ODD, MLA_KV_RANK, MLA_HEADS * (MLA_NOPE + MLA_V)), MLA_KV_RANK ** -0.5),
    }


def reference(x, c, ctx, c_ctx, ada_w, ada_b, norm_g, out_w, mlp_w1, mlp_w2, hg_lb, ev_w_in, ml_conv, ml_gate_b,
              ml_norm, rw_mu, rw_w0, rw_w2, rw_a0, rw_a2, rw_g2, rw_kk, rw_ka, rw_rk, rw_ln_w, rw_ln_b, od_w_in,
              hg_norm, mla_q_norm, mla_w_qb, mla_kv_norm, mla_w_kvb):
    B, N, D = x.shape
    n_ctx = ctx.shape[1]
    rows = N // GRID_W
    row = jnp.broadcast_to(jnp.arange(rows)[:, None], (rows, GRID_W)).reshape(-1)
    col = jnp.broadcast_to(jnp.arange(GRID_W)[None, :], (rows, GRID_W)).reshape(-1)
    lb_all = jax.nn.softmax(hg_lb.astype(jnp.float32), axis=0)
    lb_all = jnp.cumsum(lb_all, axis=0) - lb_all[0]
    silu_c = jax.nn.silu(c)
    silu_cc = jax.nn.silu(c_ctx)
    xc = ctx
    for l in range(DEPTH):
        last = l == DEPTH - 1
        mod = (silu_c @ ada_w[l] + ada_b[l])[:, None, :]
        modc = (silu_cc @ ada_w[l] + ada_b[l])[None, None, :]
        sh_a, sc_a, g_a, sh_m, sc_m, g_m = jnp.split(mod, N_MOD, axis=-1)
        shc_a, scc_a, gc_a, shc_m, scc_m, gc_m = jnp.split(modc, N_MOD, axis=-1)
        hj = jnp.concatenate([modulate(rmsnorm(xc, norm_g[l, 0]), shc_a, scc_a),
                              modulate(rmsnorm(x, norm_g[l, 0]), sh_a, sc_a)], axis=1)
        if l % 2 == 0:
            e = l // 2
            y_ctx, y_lat = even_mixer(hj, n_ctx, not last, ev_w_in[e], ml_conv[e], ml_gate_b[e], ml_norm[e], rw_mu[e],
                                      rw_w0[e], rw_w2[e], rw_a0[e], rw_a2[e], rw_g2[e], rw_kk[e], rw_ka[e], rw_rk[e],
                                      rw_ln_w[e], rw_ln_b[e])
        else:
            o = l // 2
            y_ctx, y_lat = odd_mixer(hj, n_ctx, not last, lb_all[l], row, col, od_w_in[o], hg_norm[o], mla_q_norm[o],
                                     mla_w_qb[o], mla_kv_norm[o], mla_w_kvb[o])
        x = x + g_a * rmsnorm(y_lat @ out_w[l], norm_g[l, 1])
        x = x + g_m * rmsnorm(sqrelu_mlp(modulate(rmsnorm(x, norm_g[l, 2]), sh_m, sc_m), mlp_w1[l], mlp_w2[l]), norm_g[l, 3])
        if not last:
            xc = xc + gc_a * rmsnorm(y_ctx @ out_w[l], norm_g[l, 1])
            xc = xc + gc_m * rmsnorm(sqrelu_mlp(modulate(rmsnorm(xc, norm_g[l, 2]), shc_m, scc_m), mlp_w1[l], mlp_w2[l]), norm_g[l, 3])
    return x
```

```python
import contextlib
from concourse.bass_utils import run_bass_kernel_spmd
import numpy as np
import concourse.bass as bass
import concourse.mybir as mybir

F32 = mybir.dt.float32
BF16 = mybir.dt.bfloat16
AF = mybir.ActivationFunctionType
ALU = mybir.AluOpType
AX = mybir.AxisListType

ENGS = ("tensor", "vector", "scalar", "gpsimd", "sync")
EPOCH = 30000
NDMA = 48
NCC = 4


class Buf:
    __slots__ = ("name", "w", "r")

    def __init__(self, name=""):
        self.name = name
        self.w = None
        self.r = []


class Prog:
    def __init__(self, nc, stack):
        self.nc = nc
        self.stack = stack
        self.ops = {e: [] for e in ENGS}
        self.cnt = {e: 0 for e in ENGS}
        self.known = {e: {} for e in ENGS}
        self.clock = {}
        self.dma_slot = 0
        self.dma_val = [0] * NDMA
        self.cc_slot = 0
        self.cc_val = [0] * NCC
        self.sem_keys = set()
        self.sems = {}
        self.last = {}

    def _need(self, eng, ev, waits):
        key, val = ev
        kn = self.known[eng]
        if kn.get(key, 0) >= val:
            return
        if key[0] == "e":
            for k2, v2 in kn.items():
                if k2[0] == "e" and k2[1] == key[1] and k2[2] > key[2]:
                    return
        waits[key] = max(waits.get(key, 0), val)

    def _merge(self, eng, ev):
        kn = self.known[eng]
        key, val = ev
        if kn.get(key, 0) < val:
            kn[key] = val
        ck = self.clock.get(ev)
        if ck:
            for k, v in ck.items():
                if kn.get(k, 0) < v:
                    kn[k] = v

    def op(self, eng, fn, reads=(), writes=(), dma=False, cc=False):
        deps = []
        for b in reads:
            if b.w is not None:
                deps.append(b.w)
        for b in writes:
            if b.w is not None:
                deps.append(b.w)
            deps.extend(b.r)
        if cc:
            slot = self.cc_slot
            self.cc_slot = (slot + 1) % NCC
            key = ("c", slot)
            if self.cc_val[slot] > 0:
                deps.append((key, self.cc_val[slot]))
            self.cc_val[slot] += 1
            ev = (key, self.cc_val[slot])
            inc = 1
            dma = True
        elif dma:
            slot = self.dma_slot
            self.dma_slot = (slot + 1) % NDMA
            key = ("d", slot)
            if self.dma_val[slot] > 0:
                deps.append((key, 16 * self.dma_val[slot]))
            self.dma_val[slot] += 1
            ev = (key, 16 * self.dma_val[slot])
            inc = 16
        else:
            self.cnt[eng] += 1
            c = self.cnt[eng]
            key = ("e", eng, (c - 1) // EPOCH)
            ev = (key, (c - 1) % EPOCH + 1)
            inc = 1
        waits = {}
        for d in deps:
            self._need(eng, d, waits)
        for k, v in waits.items():
            self._merge(eng, (k, v))
        if not dma:
            ck = {k: v for k, v in self.known[eng].items() if k[0] == "e"}
            self.clock[ev] = ck
        else:
            self.clock[ev] = {k: v for k, v in self.known[eng].items() if k[0] == "e"}
        self.last[eng if not dma else key] = ev
        E = getattr(self.nc, eng)
        for k, v in waits.items():
            E.wait_ge(self._sem(k), v)
        ins = fn(E)
        ins.then_inc(self._sem(key), inc)
        for b in reads:
            b.r.append(ev)
        for b in writes:
            b.w = ev
            b.r = []
        return ev

    def wait_all(self, eng, evs):
        waits = {}
        for ev in evs:
            self._need(eng, ev, waits)
        for k, v in waits.items():
            self._merge(eng, (k, v))
        E = getattr(self.nc, eng)
        for k, v in waits.items():
            E.wait_ge(self._sem(k), v)

    def _sem(self, key):
        if key not in self.sems:
            self.sems[key] = self.stack.enter_context(self.nc.semaphore("s%d" % len(self.sems)))
        return self.sems[key]

    def barrier(self):
        evs = list(self.last.values())
        for e in ENGS:
            self.wait_all(e, evs)


D = 1024
T = 8448
NCTX = 256
NT = T // 128


class Tn:
    __slots__ = ("t", "b")

    def __init__(self, t, b=None):
        self.t = t
        self.b = b if b is not None else Buf()

    def __getitem__(self, k):
        return self.t[k]


class KB:
    def __init__(self, nc, st):
        self.nc = nc
        self.st = st
        self.P = Prog(nc, st)
        self.cur = st
        self.n = 0
        self.rr = 0

    def _nm(self, p):
        self.n += 1
        return "%s%d" % (p, self.n)

    def sb(self, shape, dt=F32):
        return Tn(self.cur.enter_context(self.nc.sbuf_tensor(self._nm("sb"), list(shape), dt)))

    def ps(self, shape, dt=F32):
        return Tn(self.cur.enter_context(self.nc.psum_tensor(self._nm("ps"), list(shape), dt)))

    def dram(self, name, shape, dt=F32, kind="Internal"):
        return self.nc.dram_tensor(name, list(shape), dt, kind=kind).ap()

    @contextlib.contextmanager
    def phase(self):
        prev = self.cur
        with contextlib.ExitStack() as ph:
            self.cur = ph
            yield
            self.P.barrier()
        self.cur = prev

    def op(self, eng, fn, r=(), w=(), dma=False):
        return self.P.op(eng, fn, reads=[x.b if isinstance(x, Tn) else x for x in r],
                         writes=[x.b if isinstance(x, Tn) else x for x in w], dma=dma)

    def V(self, fn, r=(), w=()):
        return self.op("vector", fn, r, w)

    def A(self, fn, r=(), w=()):
        return self.op("scalar", fn, r, w)

    def G(self, fn, r=(), w=()):
        return self.op("gpsimd", fn, r, w)

    def PE(self, fn, r=(), w=()):
        return self.op("tensor", fn, r, w)

    def dma(self, out, in_, r=(), w=(), q=None):
        if q is None:
            q = "gpsimd" if (str(out.space) == "DRAM" and str(in_.space) != "DRAM") else "sync"
            self.rr += 1
        return self.op(q, lambda e: e.dma_start(out=out, in_=in_), r, w, dma=True)

    def allreduce(self, out, in_, groups, r=(), w=()):
        return self.P.op("gpsimd", lambda e: e.collective_compute("AllReduce", ALU.add, replica_groups=groups, ins=[in_], outs=[out]),
                         reads=[x.b if isinstance(x, Tn) else x for x in r], writes=[x.b if isinstance(x, Tn) else x for x in w], cc=True)

    def VA(self, i, fn, r=(), w=()):
        if i % 2 == 0:
            return self.V(lambda e: fn(e, False), r, w)
        return self.A(lambda e: fn(e, True), r, w)

    def identity(self, dt):
        idf = self.sb([128, 128], F32)
        self.G(lambda e: e.memset(idf[:], 0.0), w=[idf])
        self.G(lambda e: e.affine_select(out=idf[:], in_=idf[:], pattern=[[-1, 128]], compare_op=ALU.not_equal,
                                         fill=1.0, base=0, channel_multiplier=1), r=[idf], w=[idf])
        if dt == F32:
            return idf
        idb = self.sb([128, 128], dt)
        self.V(lambda e: e.tensor_copy(out=idb[:], in_=idf[:]), r=[idf], w=[idb])
        return idb

    def load_w_bf16(self, wd, kdim, n, stage, col0=0, grp=512):
        kc = kdim // 128
        wb = self.sb([128, kc, n], BF16)
        src = wd.rearrange("(k p) n -> p k n", p=128)
        for gi, c0 in enumerate(range(0, n, grp)):
            cn = min(grp, n - c0)
            for k0 in range(0, kc, 8):
                kn = min(8, kc - k0)
                stg = stage[self.rr % len(stage)]
                self.dma(stg[:, 0:kn, 0:cn], src[:, k0:k0 + kn, col0 + c0:col0 + c0 + cn], w=[stg])
                self.VA(self.rr, lambda e, a, stg=stg, k0=k0, kn=kn, c0=c0, cn=cn:
                        (e.copy(out=wb[:, k0:k0 + kn, c0:c0 + cn], in_=stg[:, 0:kn, 0:cn]) if a else
                         e.tensor_copy(out=wb[:, k0:k0 + kn, c0:c0 + cn], in_=stg[:, 0:kn, 0:cn])),
                        r=[stg], w=[wb])
        return wb

    def eps_ap(self, eps):
        if not hasattr(self, "_eps"):
            self._eps = {}
        if eps not in self._eps:
            t = Tn(self.st.enter_context(self.nc.sbuf_tensor(self._nm("eps"), [128, 1], F32)))
            self.V(lambda e: e.memset(t[:], eps), w=[t])
            self._eps[eps] = t
        return self._eps[eps][:, 0:1]

    def rstd(self, ss, n, eps):
        self.A(lambda e: e.activation(out=ss[:], in_=ss[:], func=AF.Sqrt, scale=1.0 / n, bias=self.eps_ap(eps)), r=[ss], w=[ss])
        self.V(lambda e: e.reciprocal(out=ss[:], in_=ss[:]), r=[ss], w=[ss])

EPS = 1e-6


def phase_mod(kb, l, cvec, ada_w, ada_b, norm_g, modrow, keep):
    with kb.phase():
        idf = kb.identity(F32)
        cvr = kb.sb([16, 128]); cvT = kb.sb([128, 2, 8])
        kb.dma(cvr[:], cvec.rearrange("r (k p) -> (r k) p", p=128), w=[cvr])
        kb.A(lambda e: e.activation(out=cvr[:], in_=cvr[:], func=AF.Silu), r=[cvr], w=[cvr])
        pt = kb.ps([128, 64])
        kb.PE(lambda e: e.transpose(out=pt[:, 0:16], in_=cvr[:], identity=idf[0:16, 0:16]), r=[cvr, idf], w=[pt])
        kb.V(lambda e: e.tensor_copy(out=cvT[:].rearrange('p r k -> p (r k)'), in_=pt[:, 0:16]), r=[pt], w=[cvT])
        abr = kb.sb([48, 128]); ngr = kb.sb([32, 128]); abT = kb.sb([128, 48]); ngT = kb.sb([128, 32])
        kb.dma(abr[:], ada_b[l].rearrange("(j p) -> j p", p=128), w=[abr])
        kb.dma(ngr[:], norm_g[l].rearrange("i (k p) -> (i k) p", p=128), w=[ngr])
        pt2 = kb.ps([128, 64]); pt3 = kb.ps([128, 64])
        kb.PE(lambda e: e.transpose(out=pt2[:, 0:48], in_=abr[:], identity=idf[0:48, 0:48]), r=[abr, idf], w=[pt2])
        kb.V(lambda e: e.tensor_copy(out=abT[:], in_=pt2[:, 0:48]), r=[pt2], w=[abT])
        kb.PE(lambda e: e.transpose(out=pt3[:, 0:32], in_=ngr[:], identity=idf[0:32, 0:32]), r=[ngr, idf], w=[pt3])
        kb.V(lambda e: e.tensor_copy(out=ngT[:], in_=pt3[:, 0:32]), r=[pt3], w=[ngT])
        pm = kb.ps([128, 48, 2])
        wst = [kb.sb([128, 8, 1024]) for _ in range(2)]
        src = ada_w[l].rearrange("(k p) n -> p k n", p=128)
        for g in range(6):
            ws = wst[g % 2]
            kb.dma(ws[:], src[:, :, g * 1024:(g + 1) * 1024], w=[ws])
            for jj in range(8):
                j = g * 8 + jj
                for k in range(8):
                    kb.PE(lambda e, ws=ws, jj=jj, j=j, k=k: e.matmul(pm[:, j, :], lhsT=ws[:, k, jj * 128:(jj + 1) * 128],
                                                                   rhs=cvT[:, :, k], start=(k == 0), stop=(k == 7)),
                          r=[ws, cvT], w=[pm])
        mod = kb.sb([128, 2, 48])
        for r in range(2):
            kb.V(lambda e, r=r: e.tensor_tensor(out=mod[:, r, :], in0=pm[:, :, r], in1=abT[:], op=ALU.add), r=[pm, abT], w=[mod])
        for nm, gi, sc, sh in (("pre", 0, 1, 0), ("mlp", 2, 4, 3)):
            A_, B_ = keep["A_" + nm], keep["B_" + nm]
            for r in range(2):
                kb.V(lambda e, r=r, A_=A_, sc=sc, gi=gi: e.scalar_tensor_tensor(
                    out=A_[:, r, :], in0=mod[:, r, sc * 8:sc * 8 + 8], scalar=1.0, in1=ngT[:, gi * 8:gi * 8 + 8],
                    op0=ALU.add, op1=ALU.mult), r=[mod, ngT], w=[A_])
                kb.V(lambda e, r=r, B_=B_, sh=sh: e.tensor_copy(out=B_[:, r, :], in_=mod[:, r, sh * 8:sh * 8 + 8]), r=[mod], w=[B_])
        gg = kb.sb([128, 32]); ggr = kb.sb([32, 128])
        for r in range(2):
            for wi, (gi, gs) in enumerate(((1, 2), (3, 5))):
                c0 = (r * 2 + wi) * 8
                kb.V(lambda e, c0=c0, r=r, gi=gi, gs=gs: e.tensor_tensor(out=gg[:, c0:c0 + 8], in0=mod[:, r, gs * 8:gs * 8 + 8],
                                                                      in1=ngT[:, gi * 8:gi * 8 + 8], op=ALU.mult), r=[mod, ngT], w=[gg])
        pt4 = kb.ps([128, 128])
        kb.PE(lambda e: e.transpose(out=pt4[0:32, :], in_=gg[:], identity=idf[:]), r=[gg, idf], w=[pt4])
        kb.V(lambda e: e.tensor_copy(out=ggr[:], in_=pt4[0:32, :]), r=[pt4], w=[ggr])
        mb = Buf()
        kb.dma(modrow[l].rearrange("q (k p) -> (q k) p", p=128), ggr[:], r=[ggr], w=[mb])
        return mb


def norm_T(kb, xt, r, A_, B_, idb, xn, ptr, hT, col0, ss, junk, cnt):
    kb.A(lambda e: e.activation(out=junk[:], in_=xt[:], func=AF.Square, accum_out=ss[:]), r=[xt], w=[junk, ss])
    kb.rstd(ss, D, EPS)
    kb.V(lambda e: e.tensor_scalar(out=xn[:], in0=xt[:], scalar1=ss[:, 0:1], scalar2=None, op0=ALU.mult), r=[xt, ss], w=[xn])
    for k in range(8):
        kb.PE(lambda e, k=k: e.transpose(out=ptr[:, k, :], in_=xn[:, k * 128:(k + 1) * 128], identity=idb[:]), r=[xn, idb], w=[ptr])
    for k in range(8):
        kb.VA(k + cnt, lambda e, a, k=k: (
            e.activation(out=hT[:, k, col0:col0 + 128], in_=ptr[:, k, :], func=AF.Identity, scale=A_[:, r, k:k + 1], bias=B_[:, r, k:k + 1])
            if a else
            e.tensor_scalar(out=hT[:, k, col0:col0 + 128], in0=ptr[:, k, :], scalar1=A_[:, r, k:k + 1], scalar2=B_[:, r, k:k + 1],
                            op0=ALU.mult, op1=ALU.add)), r=[ptr, A_, B_], w=[hT])


def norm_T_gen(kb, xt, r, A_, B_, idb, xn, ptr, hT, col0, ss, junk, cnt):
    kb.A(lambda e: e.activation(out=junk[:], in_=xt[:], func=AF.Square, accum_out=ss[:]), r=[xt], w=[junk, ss])
    yield
    kb.A(lambda e: e.activation(out=ss[:], in_=ss[:], func=AF.Sqrt, scale=1.0 / D, bias=kb.eps_ap(EPS)), r=[ss], w=[ss])
    yield
    kb.V(lambda e: e.reciprocal(out=ss[:], in_=ss[:]), r=[ss], w=[ss])
    kb.V(lambda e: e.tensor_scalar(out=xn[:], in0=xt[:], scalar1=ss[:, 0:1], scalar2=None, op0=ALU.mult), r=[xt, ss], w=[xn])
    yield
    for k in range(8):
        kb.PE(lambda e, k=k: e.transpose(out=ptr[:, k, :], in_=xn[:, k * 128:(k + 1) * 128], identity=idb[:]), r=[xn, idb], w=[ptr])
    yield
    for k in range(8):
        kb.VA(k + cnt, lambda e, a, k=k: (
            e.activation(out=hT[:, k, col0:col0 + 128], in_=ptr[:, k, :], func=AF.Identity, scale=A_[:, r, k:k + 1], bias=B_[:, r, k:k + 1])
            if a else
            e.tensor_scalar(out=hT[:, k, col0:col0 + 128], in0=ptr[:, k, :], scalar1=A_[:, r, k:k + 1], scalar2=B_[:, r, k:k + 1],
                            op0=ALU.mult, op1=ALU.add)), r=[ptr, A_, B_], w=[hT])
    yield


def round_robin(gens):
    live = list(gens)
    while live:
        for g_ in list(live):
            try:
                next(g_)
            except StopIteration:
                live.remove(g_)


def phase_proj(kb, XS, w_in, ncols, blocks, PT, keep, xs_buf, pt_bufs):
    with kb.phase():
        idb = kb.identity(BF16)
        stage = [kb.sb([128, 8, 512]) for _ in range(2)]
        wb = kb.load_w_bf16(w_in, D, ncols, stage)
        xts = [kb.sb([128, D]) for _ in range(4)]
        xns = [kb.sb([128, D], BF16) for _ in range(4)]
        junks = [kb.sb([128, D]) for _ in range(2)]
        sss = [kb.sb([128, 1]) for _ in range(4)]
        ptrs = [kb.ps([128, 8, 128], BF16) for _ in range(4)]
        hTs = [kb.sb([128, 8, 512], BF16) for _ in range(2)]
        pos = [kb.ps([128, 512]) for _ in range(4)]
        obs = [kb.sb([128, 512]) for _ in range(4)]
        ti = 0
        oc = 0
        sts = [(0, 256)] + [(256 + 512 * i, 512) for i in range(16)]
        for si, (t0, tn) in enumerate(sts):
            hT = hTs[si % 2]
            gens = []
            for j in range(tn // 128):
                xt = xts[ti % 4]
                kb.dma(xt[:], XS[t0 + j * 128:t0 + (j + 1) * 128, :], r=[xs_buf[(t0 + j * 128) // 128]], w=[xt])
                gens.append(norm_T_gen(kb, xt, 1 if t0 == 0 else 0, keep["A_pre"], keep["B_pre"], idb, xns[ti % 4], ptrs[ti % 4], hT, j * 128,
                                       sss[ti % 4], junks[ti % 2], ti))
                ti += 1
            round_robin(gens)
            for cb, (wc0, cn, prow) in enumerate(blocks):
                po = pos[oc % 4]; ob = obs[oc % 4]
                for k in range(8):
                    kb.PE(lambda e, po=po, wc0=wc0, cn=cn, k=k, hT=hT, tn=tn: e.matmul(
                        po[0:cn, 0:tn], lhsT=wb[:, k, wc0:wc0 + cn], rhs=hT[:, k, 0:tn], start=(k == 0), stop=(k == 7)),
                        r=[wb, hT], w=[po])
                kb.VA(oc, lambda e, a, po=po, ob=ob, cn=cn, tn=tn: (e.copy(out=ob[0:cn, 0:tn], in_=po[0:cn, 0:tn]) if a else
                                                                     e.tensor_copy(out=ob[0:cn, 0:tn], in_=po[0:cn, 0:tn])), r=[po], w=[ob])
                kb.dma(PT[prow:prow + cn, t0:t0 + tn], ob[0:cn, 0:tn], r=[ob], w=[pt_bufs[cb]])
                oc += 1


def phase_post(kb, l, XS, Y, OUT, out_w, w1, w2, modrow, keep, xs_buf, y_buf, mod_buf, t_start, t_end=T):
    def gam(wi):
        GAM = {}
        for r in range(2):
            g = kb.sb([128, D])
            q = r * 2 + wi
            kb.dma(g[:], modrow[l][q:q + 1, :].partition_broadcast(128), r=[mod_buf], w=[g])
            GAM[r] = g
        return GAM

    with kb.phase():
        idb = kb.identity(BF16)
        stage = [kb.sb([128, 8, 512]) for _ in range(2)]
        owb = kb.load_w_bf16(out_w, D, D, stage)
        GA = gam(0)
        yts = [kb.sb([128, D]) for _ in range(2)]
        ybs = [kb.sb([128, D], BF16) for _ in range(2)]
        yTs = [kb.sb([128, 8, 128], BF16) for _ in range(2)]
        xts = [kb.sb([128, D]) for _ in range(4)]
        x1s = [kb.sb([128, D]) for _ in range(3)]
        tmps = [kb.sb([128, D]) for _ in range(2)]; junk = kb.sb([128, D])
        sss = [kb.sb([128, 1]) for _ in range(8)]
        ptrs = [kb.ps([128, 8, 128], BF16) for _ in range(4)]
        pOs = [[kb.ps([128, 512]) for _ in range(2)] for _ in range(2)]
        cnt = 0
        for tt in range(t_start, t_end, 128):
            r = 1 if tt < NCTX else 0
            yt = yts[cnt % 2]; yb = ybs[cnt % 2]; yT = yTs[cnt % 2]; xt = xts[cnt % 3]; x1 = x1s[cnt % 3]
            ptr = ptrs[cnt % 2]; pO = pOs[cnt % 2]; tmp = tmps[cnt % 2]
            kb.dma(yt[:], Y[tt:tt + 128, :], r=[y_buf[tt // 128]], w=[yt])
            kb.dma(xt[:], XS[tt:tt + 128, :], r=[xs_buf[tt // 128]], w=[xt])
            kb.G(lambda e, yb=yb, yt=yt: e.tensor_copy(out=yb[:], in_=yt[:]), r=[yt], w=[yb])
            for k in range(8):
                kb.PE(lambda e, k=k, yb=yb, ptr=ptr: e.transpose(out=ptr[:, k, :], in_=yb[:, k * 128:(k + 1) * 128], identity=idb[:]),
                      r=[yb, idb], w=[ptr])
            kb.V(lambda e, yT=yT, ptr=ptr: e.tensor_copy(out=yT[:], in_=ptr[:]), r=[ptr], w=[yT])
            for h in range(2):
                for k in range(8):
                    kb.PE(lambda e, h=h, k=k, yT=yT, pO=pO: e.matmul(pO[h][:], lhsT=yT[:, k, :], rhs=owb[:, k, h * 512:(h + 1) * 512],
                                                                   start=(k == 0), stop=(k == 7)), r=[yT, owb], w=[pO[h]])
            ss = sss[(cnt * 2) % 8]; ssb = sss[(cnt * 2 + 1) % 8]
            kb.A(lambda e, ss=ss, pO=pO: e.activation(out=junk[:, 0:512], in_=pO[0][:], func=AF.Square, accum_out=ss[:]), r=[pO[0]], w=[junk, ss])
            kb.A(lambda e, ssb=ssb, pO=pO: e.activation(out=junk[:, 512:1024], in_=pO[1][:], func=AF.Square, accum_out=ssb[:]), r=[pO[1]], w=[junk, ssb])
            kb.V(lambda e, ss=ss, ssb=ssb: e.tensor_tensor(out=ss[:], in0=ss[:], in1=ssb[:], op=ALU.add), r=[ss, ssb], w=[ss])
            kb.rstd(ss, D, EPS)
            ga = GA[r]
            for h in range(2):
                kb.V(lambda e, h=h, ss=ss, ga=ga, pO=pO, tmp=tmp: e.scalar_tensor_tensor(
                    out=tmp[:, h * 512:(h + 1) * 512], in0=pO[h][:], scalar=ss[:, 0:1], in1=ga[:, h * 512:(h + 1) * 512],
                    op0=ALU.mult, op1=ALU.mult), r=[pO[h], ss, ga], w=[tmp])
            kb.G(lambda e, x1=x1, xt=xt, tmp=tmp: e.tensor_tensor(out=x1[:], in0=xt[:], in1=tmp[:], op=ALU.add), r=[xt, tmp], w=[x1])
            kb.dma(XS[tt:tt + 128, :], x1[:], r=[x1], w=[xs_buf[tt // 128]])
            cnt += 1

    with kb.phase():
        idb = kb.identity(BF16)
        stage = [kb.sb([128, 8, 128]) for _ in range(2)]
        w1b = kb.load_w_bf16(w1, D, 4 * D, stage, grp=128)
        w2b = kb.load_w_bf16(w2, 4 * D, D, stage, grp=128)
        GM = gam(1)
        x1s = [kb.sb([128, D]) for _ in range(4)]
        xns = [kb.sb([128, D], BF16) for _ in range(4)]
        tmps = [kb.sb([128, D]) for _ in range(2)]; junk = kb.sb([128, D])
        sss = [kb.sb([128, 1]) for _ in range(8)]
        ptrs = [kb.ps([128, 8, 128], BF16) for _ in range(4)]
        pU = [kb.ps([128, 256]) for _ in range(2)]
        p2s = [[kb.ps([128, 512]) for _ in range(2)] for _ in range(2)]
        hTs = [kb.sb([128, 8, 256], BF16) for _ in range(2)]
        UT = kb.sb([128, 32, 256], BF16)
        ur = [kb.sb([128, 256]) for _ in range(2)]
        sc = 0
        cnt = 0
        for t0 in range(t_start, t_end, 256):
            r = 1 if t0 < NCTX else 0
            hT = hTs[sc % 2]
            x1l = []
            for j in range(2):
                tt = t0 + j * 128
                x1 = x1s[cnt % 4]
                kb.dma(x1[:], XS[tt:tt + 128, :], r=[xs_buf[tt // 128]], w=[x1])
                norm_T(kb, x1, r, keep["A_mlp"], keep["B_mlp"], idb, xns[cnt % 2], ptrs[cnt % 2], hT, j * 128, sss[cnt % 8], junk, cnt)
                x1l.append(x1)
                cnt += 1
            for fb in range(32):
                pu = pU[fb % 2]; u = ur[fb % 2]
                for k in range(8):
                    kb.PE(lambda e, pu=pu, fb=fb, k=k, hT=hT: e.matmul(pu[:], lhsT=w1b[:, k, fb * 128:(fb + 1) * 128], rhs=hT[:, k, :],
                                                                     start=(k == 0), stop=(k == 7)), r=[w1b, hT], w=[pu])
                kb.A(lambda e, pu=pu, u=u: e.activation(out=u[:], in_=pu[:], func=AF.Relu), r=[pu], w=[u])
                kb.G(lambda e, u=u, fb=fb: e.tensor_tensor(out=UT[:, fb, :], in0=u[:], in1=u[:], op=ALU.mult), r=[u], w=[UT])
            for j in range(2):
                tt = t0 + j * 128
                x1 = x1l[j]; p2 = p2s[j]; tmp = tmps[j]
                for h in range(2):
                    for fb in range(32):
                        kb.PE(lambda e, h=h, fb=fb, j=j, p2=p2: e.matmul(p2[h][:], lhsT=UT[:, fb, j * 128:(j + 1) * 128],
                                                                       rhs=w2b[:, fb, h * 512:(h + 1) * 512],
                                                                       start=(fb == 0), stop=(fb == 31)), r=[UT, w2b], w=[p2[h]])
                ss = sss[(sc * 4 + j * 2) % 8]; ssb = sss[(sc * 4 + j * 2 + 1) % 8]
                kb.A(lambda e, ss=ss, p2=p2: e.activation(out=junk[:, 0:512], in_=p2[0][:], func=AF.Square, accum_out=ss[:]), r=[p2[0]], w=[junk, ss])
                kb.A(lambda e, ssb=ssb, p2=p2: e.activation(out=junk[:, 512:1024], in_=p2[1][:], func=AF.Square, accum_out=ssb[:]), r=[p2[1]], w=[junk, ssb])
                kb.V(lambda e, ss=ss, ssb=ssb: e.tensor_tensor(out=ss[:], in0=ss[:], in1=ssb[:], op=ALU.add), r=[ss, ssb], w=[ss])
                kb.rstd(ss, D, EPS)
                gm = GM[r]
                for h in range(2):
                    kb.V(lambda e, h=h, ss=ss, gm=gm, p2=p2, tmp=tmp: e.scalar_tensor_tensor(
                        out=tmp[:, h * 512:(h + 1) * 512], in0=p2[h][:], scalar=ss[:, 0:1], in1=gm[:, h * 512:(h + 1) * 512],
                        op0=ALU.mult, op1=ALU.mult), r=[p2[h], ss, gm], w=[tmp])
                kb.G(lambda e, x1=x1, tmp=tmp: e.tensor_tensor(out=x1[:], in0=x1[:], in1=tmp[:], op=ALU.add), r=[x1, tmp], w=[x1])
                if OUT is not None:
                    kb.dma(OUT[tt - NCTX:tt - NCTX + 128, :], x1[:], r=[x1], w=[xs_buf[tt // 128]])
                else:
                    kb.dma(XS[tt:tt + 128, :], x1[:], r=[x1], w=[xs_buf[tt // 128]])
            sc += 1


ARCH = 1024


def allreduce_rows(kb, ARi, ARo, ari_b, aro_b, t_start, groups):
    for r0 in range(t_start, T, ARCH):
        r1 = min(T, r0 + ARCH)
        kb.allreduce(ARo[r0:r1, :], ARi[r0:r1, :], groups, r=ari_b[r0 // 128:r1 // 128], w=aro_b[r0 // 128:r1 // 128])


def phase_post_tp(kb, l, XS, Y, OUT, ow, w1c, w2c, modrow, keep, xs_buf, y_buf, mod_buf, t_start, ARi, ARo, ari_b, aro_b, groups):
    def gam(wi):
        GAM = {}
        for r in range(2):
            g = kb.sb([128, D])
            q = r * 2 + wi
            kb.dma(g[:], modrow[l][q:q + 1, :].partition_broadcast(128), r=[mod_buf], w=[g])
            GAM[r] = g
        return GAM

    with kb.phase():
        idb = kb.identity(BF16)
        stage = [kb.sb([128, 8, 128]) for _ in range(2)]
        owb = kb.load_w_bf16(ow, 256, D, stage, grp=128)
        w1b = kb.load_w_bf16(w1c, D, D, stage, grp=128)
        w2b = kb.load_w_bf16(w2c, D, D, stage, grp=128)
        GA = gam(0); GM = gam(1)
        yts = [kb.sb([128, 256]) for _ in range(2)]; ybs = [kb.sb([128, 256], BF16) for _ in range(2)]
        yTs = [kb.sb([128, 2, 128], BF16) for _ in range(2)]
        ots = [kb.sb([128, D]) for _ in range(2)]
        ptrA = kb.ps([128, 2, 128], BF16)
        pO = [kb.ps([128, 512])] * 2
        Os = [kb.sb([128, D]) for _ in range(2)]; xts = [kb.sb([128, D]) for _ in range(2)]
        x1s = [kb.sb([128, D]) for _ in range(3)]
        xns = [kb.sb([128, D], BF16) for _ in range(2)]
        tmps = [kb.sb([128, D]) for _ in range(2)]; junks = [kb.sb([128, D]) for _ in range(2)]
        sss = [kb.sb([128, 1]) for _ in range(8)]
        ptrB = [kb.ps([128, 8, 128], BF16) for _ in range(2)]
        pU = [kb.ps([128, 512]) for _ in range(2)]
        p2 = [kb.ps([128, 512]) for _ in range(2)]
        hTs = [kb.sb([128, 8, 512], BF16) for _ in range(2)]
        UT = [kb.sb([128, 8, 512], BF16) for _ in range(2)]
        ur = [kb.sb([128, 512]) for _ in range(2)]
        mts = [kb.sb([128, D]) for _ in range(2)]
        Ms = [kb.sb([128, D]) for _ in range(2)]; xcs = [kb.sb([128, D]) for _ in range(2)]
        tmpc = [kb.sb([128, D]) for _ in range(2)]
        ssc = [kb.sb([128, 1]) for _ in range(4)]
        ca = [0]; cb = [0]; cc_ = [0]

        def A_tile(tt):
            cnt = ca[0]; ca[0] += 1
            yt = yts[cnt % 2]; yb = ybs[cnt % 2]; yT = yTs[cnt % 2]; ot = ots[cnt % 2]
            kb.dma(yt[:, 0:128], Y[tt:tt + 128, 0:128], r=[y_buf[tt // 128]], w=[yt])
            kb.dma(yt[:, 128:256], Y[tt:tt + 128, 512:640], r=[y_buf[tt // 128]], w=[yt])
            yield
            kb.G(lambda e: e.tensor_copy(out=yb[:], in_=yt[:]), r=[yt], w=[yb])
            yield
            for k in range(2):
                kb.PE(lambda e, k=k: e.transpose(out=ptrA[:, k, :], in_=yb[:, k * 128:(k + 1) * 128], identity=idb[:]), r=[yb, idb], w=[ptrA])
            kb.V(lambda e: e.tensor_copy(out=yT[:], in_=ptrA[:]), r=[ptrA], w=[yT])
            yield
            for h in range(2):
                for k in range(2):
                    kb.PE(lambda e, h=h, k=k: e.matmul(pO[h][:], lhsT=yT[:, k, :], rhs=owb[:, k, h * 512:(h + 1) * 512],
                                                       start=(k == 0), stop=(k == 1)), r=[yT, owb], w=[pO[h]])
                if h == 0:
                    kb.A(lambda e: e.copy(out=ot[:, 0:512], in_=pO[0][:]), r=[pO[0]], w=[ot])
                else:
                    kb.V(lambda e: e.tensor_copy(out=ot[:, 512:1024], in_=pO[1][:]), r=[pO[1]], w=[ot])
            kb.dma(ARi[tt:tt + 128, :], ot[:], r=[ot], w=[ari_b[tt // 128]])
            yield

        def B_tile(sc, t0, j, hT, cnt):
            tt = t0 + j * 128
            r = 1 if tt < NCTX else 0
            O = Os[cnt % 2]; xt = xts[cnt % 2]; x1 = x1s[cnt % 3]; tmp = tmps[cnt % 2]
            ss = sss[(cnt * 2) % 8]; ss2 = sss[(cnt * 2 + 1) % 8]; xn = xns[cnt % 2]; ptr = ptrB[cnt % 2]
            jk = junks[cnt % 2]
            kb.dma(O[:], ARo[tt:tt + 128, :], r=[aro_b[tt // 128]], w=[O])
            kb.dma(xt[:], XS[tt:tt + 128, :], r=[xs_buf[tt // 128]], w=[xt])
            yield
            kb.A(lambda e: e.activation(out=jk[:], in_=O[:], func=AF.Square, accum_out=ss[:]), r=[O], w=[jk, ss])
            yield
            kb.A(lambda e: e.activation(out=ss[:], in_=ss[:], func=AF.Sqrt, scale=1.0 / D, bias=kb.eps_ap(EPS)), r=[ss], w=[ss])
            yield
            kb.V(lambda e: e.reciprocal(out=ss[:], in_=ss[:]), r=[ss], w=[ss])
            ga = GA[r]
            kb.V(lambda e: e.scalar_tensor_tensor(out=tmp[:], in0=O[:], scalar=ss[:, 0:1], in1=ga[:], op0=ALU.mult, op1=ALU.mult),
                 r=[O, ss, ga], w=[tmp])
            yield
            kb.V(lambda e: e.tensor_tensor(out=x1[:], in0=xt[:], in1=tmp[:], op=ALU.add), r=[xt, tmp], w=[x1])
            kb.dma(XS[tt:tt + 128, :], x1[:], r=[x1], w=[xs_buf[tt // 128]])
            yield
            kb.A(lambda e: e.activation(out=jk[:], in_=x1[:], func=AF.Square, accum_out=ss2[:]), r=[x1], w=[jk, ss2])
            yield
            kb.A(lambda e: e.activation(out=ss2[:], in_=ss2[:], func=AF.Sqrt, scale=1.0 / D, bias=kb.eps_ap(EPS)), r=[ss2], w=[ss2])
            yield
            kb.V(lambda e: e.reciprocal(out=ss2[:], in_=ss2[:]), r=[ss2], w=[ss2])
            kb.V(lambda e: e.tensor_scalar(out=xn[:], in0=x1[:], scalar1=ss2[:, 0:1], scalar2=None, op0=ALU.mult), r=[x1, ss2], w=[xn])
            yield
            for k in range(8):
                kb.PE(lambda e, k=k: e.transpose(out=ptr[:, k, :], in_=xn[:, k * 128:(k + 1) * 128], identity=idb[:]), r=[xn, idb], w=[ptr])
            yield
            A_, B_ = keep["A_mlp"], keep["B_mlp"]
            col0 = j * 128
            for k in range(8):
                kb.VA(k + cnt, lambda e, a, k=k: (
                    e.activation(out=hT[:, k, col0:col0 + 128], in_=ptr[:, k, :], func=AF.Identity, scale=A_[:, r, k:k + 1], bias=B_[:, r, k:k + 1])
                    if a else
                    e.tensor_scalar(out=hT[:, k, col0:col0 + 128], in0=ptr[:, k, :], scalar1=A_[:, r, k:k + 1], scalar2=B_[:, r, k:k + 1],
                                    op0=ALU.mult, op1=ALU.add)), r=[ptr, A_, B_], w=[hT])
            yield

        def B_super(t0, tn):
            sc = cb[0]; cb[0] += 1
            hT = hTs[sc % 2]; ut = UT[sc % 2]
            nt = tn // 128
            for j0 in range(0, nt, 2):
                gens = [B_tile(sc, t0, j, hT, sc * 4 + j) for j in range(j0, min(nt, j0 + 2))]
                live = list(gens)
                while live:
                    for g_ in list(live):
                        try:
                            next(g_)
                        except StopIteration:
                            live.remove(g_)
            for fb in range(8):
                pu = pU[fb % 2]; u = ur[fb % 2]
                for k in range(8):
                    kb.PE(lambda e, pu=pu, fb=fb, k=k: e.matmul(pu[:, 0:tn], lhsT=w1b[:, k, fb * 128:(fb + 1) * 128], rhs=hT[:, k, 0:tn],
                                                              start=(k == 0), stop=(k == 7)), r=[w1b, hT], w=[pu])
                kb.A(lambda e, pu=pu, u=u: e.activation(out=u[:, 0:tn], in_=pu[:, 0:tn], func=AF.Relu), r=[pu], w=[u])
                kb.G(lambda e, u=u, fb=fb: e.tensor_tensor(out=ut[:, fb, 0:tn], in0=u[:, 0:tn], in1=u[:, 0:tn], op=ALU.mult), r=[u], w=[ut])
            for j in range(nt):
                tt = t0 + j * 128
                mt = mts[j % 2]
                for h in range(2):
                    for fb in range(8):
                        kb.PE(lambda e, h=h, fb=fb, j=j: e.matmul(p2[h][:], lhsT=ut[:, fb, j * 128:(j + 1) * 128], rhs=w2b[:, fb, h * 512:(h + 1) * 512],
                                                                start=(fb == 0), stop=(fb == 7)), r=[ut, w2b], w=[p2[h]])
                kb.A(lambda e, mt=mt: e.copy(out=mt[:, 0:512], in_=p2[0][:]), r=[p2[0]], w=[mt])
                kb.V(lambda e, mt=mt: e.tensor_copy(out=mt[:, 512:1024], in_=p2[1][:]), r=[p2[1]], w=[mt])
                kb.dma(ARi[tt:tt + 128, :], mt[:], r=[mt], w=[ari_b[tt // 128]])

        def C_tile(tt):
            cnt = cc_[0]; cc_[0] += 1
            r = 1 if tt < NCTX else 0
            M = Ms[cnt % 2]; x1 = xcs[cnt % 2]; tmp = tmpc[cnt % 2]; ss = ssc[cnt % 4]
            kb.dma(M[:], ARo[tt:tt + 128, :], r=[aro_b[tt // 128]], w=[M])
            kb.dma(x1[:], XS[tt:tt + 128, :], r=[xs_buf[tt // 128]], w=[x1])
            yield
            kb.A(lambda e: e.activation(out=tmp[:], in_=M[:], func=AF.Square, accum_out=ss[:]), r=[M], w=[tmp, ss])
            yield
            kb.A(lambda e: e.activation(out=ss[:], in_=ss[:], func=AF.Sqrt, scale=1.0 / D, bias=kb.eps_ap(EPS)), r=[ss], w=[ss])
            yield
            kb.V(lambda e: e.reciprocal(out=ss[:], in_=ss[:]), r=[ss], w=[ss])
            gm = GM[r]
            kb.V(lambda e: e.scalar_tensor_tensor(out=tmp[:], in0=M[:], scalar=ss[:, 0:1], in1=gm[:], op0=ALU.mult, op1=ALU.mult),
                 r=[M, ss, gm], w=[tmp])
            yield
            kb.V(lambda e: e.tensor_tensor(out=x1[:], in0=x1[:], in1=tmp[:], op=ALU.add), r=[x1, tmp], w=[x1])
            if OUT is not None:
                kb.dma(OUT[tt - NCTX:tt - NCTX + 128, :], x1[:], r=[x1], w=[xs_buf[tt // 128]])
            else:
                kb.dma(XS[tt:tt + 128, :], x1[:], r=[x1], w=[xs_buf[tt // 128]])
            yield

        chunks = [(r0, min(T, r0 + ARCH)) for r0 in range(t_start, T, ARCH)]

        def AR(ci):
            r0, r1 = chunks[ci]
            kb.allreduce(ARo[r0:r1, :], ARi[r0:r1, :], groups, r=ari_b[r0 // 128:r1 // 128], w=aro_b[r0 // 128:r1 // 128])

        def stA(ci):
            tts = list(range(chunks[ci][0], chunks[ci][1], 128))
            for i in range(0, len(tts), 2):
                round_robin([A_tile(tt) for tt in tts[i:i + 2]])
            AR(ci)

        def stB(ci):
            for t0 in range(chunks[ci][0], chunks[ci][1], 512):
                B_super(t0, min(512, chunks[ci][1] - t0))
            AR(ci)

        def stC(ci):
            tts = list(range(chunks[ci][0], chunks[ci][1], 128))
            for i in range(0, len(tts), 2):
                round_robin([C_tile(tt) for tt in tts[i:i + 2]])

        n = len(chunks)
        for step in range(n + 3):
            if step < n:
                stA(step)
            if 0 <= step - 2 < n:
                stB(step - 2)
            if 0 <= step - 3 < n:
                stC(step - 3)

NCH = T // 64
ORDER_FW = list(range(NCH))
ORDER_BW = [3, 2, 1, 0] + list(range(NCH - 1, 3, -1))


def flat(t):
    return t[:].rearrange("p c l -> p (c l)")


def make_masks(kb):
    ones = kb.sb([64, 64]); mf = kb.sb([64, 64]); mb = kb.sb([64, 64])
    kb.G(lambda e: e.memset(ones[:], 1.0), w=[ones])
    kb.G(lambda e: e.affine_select(out=mf[:], in_=ones[:], pattern=[[1, 64]], compare_op=ALU.is_ge, fill=0.0, base=0,
                                   channel_multiplier=-1), r=[ones], w=[mf])
    kb.G(lambda e: e.affine_select(out=mb[:], in_=ones[:], pattern=[[-1, 64]], compare_op=ALU.is_ge, fill=0.0, base=0,
                                   channel_multiplier=1), r=[ones], w=[mb])
    return mf, mb


def chunk_cumsum(kb, A, B, rev):
    src, dst = A, B
    for s in (1, 2, 4, 8, 16, 32):
        if not rev:
            kb.V(lambda e, s=s, src=src, dst=dst: e.tensor_tensor(out=dst[:, :, s:], in0=src[:, :, s:], in1=src[:, :, :64 - s], op=ALU.add),
                 r=[src], w=[dst])
            kb.G(lambda e, s=s, src=src, dst=dst: e.tensor_copy(out=dst[:, :, :s], in_=src[:, :, :s]), r=[src], w=[dst])
        else:
            kb.V(lambda e, s=s, src=src, dst=dst: e.tensor_tensor(out=dst[:, :, :64 - s], in0=src[:, :, :64 - s], in1=src[:, :, s:], op=ALU.add),
                 r=[src], w=[dst])
            kb.G(lambda e, s=s, src=src, dst=dst: e.tensor_copy(out=dst[:, :, 64 - s:], in_=src[:, :, 64 - s:]), r=[src], w=[dst])
        src, dst = dst, src
    assert src is A


HC = NCH // 3
HT = HC * 64


def gla_engine(kb, tl, load, Vp, NV, rev, mask, idb, OD, od_bufs, col0, norm_den):
    A, B, C, Cx, QT, KT, ed, Cf, Cb = (tl[k] for k in ("A", "B", "C", "Cx", "QT", "KT", "ed", "Cf", "Cb"))
    pT = tl["pT"]; pG = tl["pG"]; pN = tl["pN"]; pC = tl["pC"]
    kb.V(lambda e: e.memset(Cf[:], 0.0), w=[Cf])
    kb.G(lambda e: e.memset(Cb[:], 0.0), w=[Cb])
    order = ORDER_BW if rev else ORDER_FW
    runs = []
    for c in order:
        hh = (c // HC) * HC
        if not runs or runs[-1][0] != hh:
            runs.append((hh, []))
        runs[-1][1].append(c)
    i = 0
    for c0, clist in runs:
        load("lf", A, Cx, c0)
        chunk_cumsum(kb, A, B, rev)
        pos = 0 if rev else 63
        kb.A(lambda e: e.activation(out=ed[:], in_=A[:, :, pos], func=AF.Exp), r=[A], w=[ed])
        yield
        load("q", B, Cx, c0)
        kb.A(lambda e: e.activation(out=flat(C), in_=flat(A), func=AF.Exp), r=[A], w=[C])
        kb.V(lambda e: e.tensor_tensor(out=flat(QT), in0=flat(B), in1=flat(C), op=ALU.mult), r=[B, C], w=[QT])
        yield
        load("k", B, Cx, c0)
        if load("ig", C, None, c0):
            kb.V(lambda e: e.tensor_tensor(out=flat(C), in0=flat(C), in1=flat(A), op=ALU.subtract), r=[C, A], w=[C])
            kb.A(lambda e: e.activation(out=flat(C), in_=flat(C), func=AF.Exp), r=[C], w=[C])
        else:
            kb.A(lambda e: e.activation(out=flat(C), in_=flat(A), func=AF.Exp, scale=-1.0), r=[A], w=[C])
        kb.V(lambda e: e.tensor_tensor(out=flat(KT), in0=flat(B), in1=flat(C), op=ALU.mult), r=[B, C], w=[KT])
        yield
        for c in clist:
            lc = c - c0
            ST = tl["ST"][i % 2]; ob = tl["ob"][i % 3]; Kt = tl["Kt"][i % 2]; dn = tl["dn"][i % 4]
            i += 1
            kb.PE(lambda e: e.matmul(pG[:], lhsT=KT[:, lc, :], rhs=QT[:, lc, :], start=True, stop=True), r=[KT, QT], w=[pG])
            kb.PE(lambda e: e.transpose(out=pT[:], in_=KT[:, lc, :], identity=idb[:]), r=[KT, idb], w=[pT])
            kb.A(lambda e: e.copy(out=Kt[:], in_=pT[:]), r=[pT], w=[Kt])
            yield
            kb.V(lambda e: e.tensor_tensor(out=ST[:], in0=pG[:], in1=mask[:], op=ALU.mult), r=[pG, mask], w=[ST])
            yield
            kb.PE(lambda e: e.matmul(pN[:, 0:NV], lhsT=ST[:], rhs=Vp[:, c, :], start=True, stop=False), r=[ST, Vp], w=[pN])
            kb.PE(lambda e: e.matmul(pN[:, 0:NV], lhsT=QT[:, lc, :], rhs=Cb[:, 0:NV], start=False, stop=True), r=[QT, Cb, pN], w=[pN])
            kb.PE(lambda e: e.matmul(pC[:, 0:NV], lhsT=Kt[:], rhs=Vp[:, c, :], start=True, stop=True), r=[Kt, Vp], w=[pC])
            kb.G(lambda e: e.tensor_scalar(out=Cf[:], in0=Cf[:], scalar1=ed[:, lc:lc + 1], scalar2=None, op0=ALU.mult), r=[Cf, ed], w=[Cf])
            yield
            if norm_den:
                kb.A(lambda e: e.activation(out=dn[:], in_=pN[:, 128:129], func=AF.Abs), r=[pN], w=[dn])
            else:
                kb.A(lambda e: e.copy(out=ob[:], in_=pN[:, 0:128]), r=[pN], w=[ob])
            kb.V(lambda e: e.scalar_tensor_tensor(out=Cf[:, 0:NV], in0=pC[:, 0:NV], scalar=ed[:, lc:lc + 1], in1=Cf[:, 0:NV],
                                                  op0=ALU.mult, op1=ALU.add), r=[pC, ed, Cf], w=[Cf])
            yield
            if norm_den:
                kb.V(lambda e: e.tensor_scalar(out=dn[:], in0=dn[:], scalar1=1.0, scalar2=None, op0=ALU.max), r=[dn], w=[dn])
                kb.V(lambda e: e.reciprocal(out=dn[:], in_=dn[:]), r=[dn], w=[dn])
            kb.A(lambda e: e.copy(out=Cb[:], in_=Cf[:]), r=[Cf], w=[Cb])
            yield
            if norm_den:
                kb.A(lambda e: e.activation(out=ob[:], in_=pN[:, 0:128], func=AF.Identity, scale=dn[:, 0:1]), r=[pN, dn], w=[ob])
            kb.dma(OD[c * 64:(c + 1) * 64, col0:col0 + 128], ob[:], r=[ob], w=[od_bufs[c]])
            yield


def gla_tiles(kb, pT):
    tl = {}
    for k in ("A", "B"):
        tl[k] = kb.sb([128, HC, 64])
    tl["Cx"] = kb.sb([128, HT + 2])
    tl["C"] = Tn(tl["Cx"][:, 0:HT].rearrange("p (c l) -> p c l", l=64), tl["Cx"].b)
    tl["QT"] = kb.sb([128, HC, 64], BF16); tl["KT"] = kb.sb([128, HC, 64], BF16)
    tl["ed"] = kb.sb([128, HC]); tl["Cf"] = kb.sb([128, 132]); tl["Cb"] = kb.sb([128, 132], BF16)
    tl["pG"] = kb.ps([64, 64])
    tl["ST"] = [kb.sb([64, 64], BF16) for _ in range(2)]
    tl["pN"] = kb.ps([64, 132])
    tl["ob"] = [kb.sb([64, 128]) for _ in range(3)]
    tl["pT"] = pT
    tl["Kt"] = [kb.sb([64, 128], BF16) for _ in range(2)]
    tl["pC"] = kb.ps([128, 132])
    tl["dn"] = [kb.sb([64, 1]) for _ in range(4)]
    return tl


def run_interleaved(gens):
    import os
    if os.environ.get("SEQ"):
        for g_ in gens:
            for _ in g_:
                pass
        return
    live = list(gens)
    while live:
        for g_ in list(live):
            try:
                next(g_)
            except StopIteration:
                live.remove(g_)


def build_Vp(kb, PT, row0, Vp, NV, idf, pt_bufs, stg, pV):
    if NV == 129:
        kb.G(lambda e: e.memset(Vp[:, :, 128:129], 1.0), w=[Vp])
    for g in range(T // 512 + 1):
        t0 = g * 512
        tn = min(512, T - t0)
        s = stg[g % 2]
        kb.dma(s[:, 0:tn], PT[row0:row0 + 128, t0:t0 + tn], r=pt_bufs, w=[s])
        for j in range(tn // 64):
            c = (t0 + j * 64) // 64
            p = pV[c % 2]
            kb.PE(lambda e, s=s, j=j, p=p: e.transpose(out=p[:], in_=s[:, j * 64:(j + 1) * 64], identity=idf[:]), r=[s, idf], w=[p])
            kb.VA(c, lambda e, a, c=c, p=p: (e.copy(out=Vp[:, c, 0:128], in_=p[:]) if a else e.tensor_copy(out=Vp[:, c, 0:128], in_=p[:])),
                  r=[p], w=[Vp])


def load_rows_T(kb, src2d, nrows, idf, pt, out, wt):
    r = kb.sb([nrows, 128])
    kb.dma(r[:], src2d, w=[r])
    kb.PE(lambda e: e.transpose(out=pt[:, 0:nrows], in_=r[:], identity=idf[0:nrows, 0:nrows]), r=[r, idf], w=[pt])
    kb.V(lambda e: e.tensor_copy(out=out, in_=pt[:, 0:nrows]), r=[pt], w=[wt])


def conv3(kb, dst, Cx, PT, pt_bufs, row0, w, j0, c0):
    t0 = c0 * 64; t1 = t0 + HT
    la = max(t0 - 1, 0); lb = min(t1 + 1, T)
    kb.dma(Cx[:, la - t0 + 1:lb - t0 + 1], PT[row0:row0 + 128, la:lb], r=pt_bufs, w=[Cx])
    d = flat(dst)
    kb.V(lambda e: e.tensor_scalar(out=d, in0=Cx[:, 1:HT + 1], scalar1=w[:, 1, j0:j0 + 1], scalar2=None, op0=ALU.mult), r=[Cx, w], w=[dst])
    for sa, sb in ((0, NCTX), (NCTX, T)):
        a = max(sa, t0); b = min(sb, t1)
        if a >= b:
            continue
        lo = a + 1 if a == sa else a
        hi = b - 1 if b == sb else b
        kb.V(lambda e, lo=lo, b=b: e.scalar_tensor_tensor(out=d[:, lo - t0:b - t0], in0=Cx[:, lo - t0:b - t0], scalar=w[:, 0, j0:j0 + 1],
                                                         in1=d[:, lo - t0:b - t0], op0=ALU.mult, op1=ALU.add), r=[Cx, w, dst], w=[dst])
        kb.V(lambda e, a=a, hi=hi: e.scalar_tensor_tensor(out=d[:, a - t0:hi - t0], in0=Cx[:, a - t0 + 2:hi - t0 + 2], scalar=w[:, 2, j0:j0 + 1],
                                                         in1=d[:, a - t0:hi - t0], op0=ALU.mult, op1=ALU.add), r=[Cx, w, dst], w=[dst])


def phase_mlstm(kb, PT, pt_bufs, ml_conv, ml_gate_b, OD, od_bufs, nh=4):
    with kb.phase():
        idf = kb.identity(F32)
        idb = kb.sb([128, 128], BF16)
        kb.V(lambda e: e.tensor_copy(out=idb[:], in_=idf[:]), r=[idf], w=[idb])
        mf, mb = make_masks(kb)
        pT_ = kb.ps([64, 128], BF16)
        tls = [gla_tiles(kb, pT_), gla_tiles(kb, pT_)]
        pB = [kb.ps([128, 512])] * 2
        ptm = pB[0]
        cw = kb.sb([128, 3, 8])
        load_rows_T(kb, ml_conv.rearrange("j (c p) -> (j c) p", p=128), 24, idf, ptm, cw[:].rearrange("p j c -> p (j c)"), cw)
        kb.V(lambda e: e.tensor_scalar(out=cw[:, :, 4:8], in0=cw[:, :, 4:8], scalar1=128.0 ** -0.5, scalar2=None, op0=ALU.mult), r=[cw], w=[cw])
        GR = kb.sb([16, T]); gb = kb.sb([16, 1]); msk = kb.sb([16, 1])
        for q, (prow, brow) in enumerate(((2048, 0), (2056, 2), (2052, 1), (2060, 3))):
            kb.dma(GR[q * 4:q * 4 + 4, :], PT[prow:prow + 4, :], r=pt_bufs, w=[GR])
            kb.dma(gb[q * 4:q * 4 + 4, :], ml_gate_b[brow:brow + 1, :].rearrange("o h -> h o"), w=[gb])
        kb.V(lambda e: e.memset(msk[:], 1.0), w=[msk])
        kb.V(lambda e: e.memset(msk[0:8, :], 0.0), r=[msk], w=[msk])
        tmpA = tls[0]["A"]
        GW = HC * 64
        kb.V(lambda e: e.tensor_scalar(out=GR[:], in0=GR[:], scalar1=gb[:, 0:1], scalar2=None, op0=ALU.add), r=[GR, gb], w=[GR])
        for g0 in range(0, T, GW):
            gn = min(GW, T - g0)
            tg = flat(tmpA)[0:16, 0:gn]
            grs = GR[:, g0:g0 + gn]
            kb.A(lambda e, tg=tg, grs=grs: e.activation(out=tg, in_=grs, func=AF.Sigmoid), r=[GR], w=[tmpA])
            kb.A(lambda e, tg=tg: e.activation(out=tg, in_=tg, func=AF.Ln), r=[tmpA], w=[tmpA])
            kb.V(lambda e, tg=tg, grs=grs: e.tensor_tensor(out=tg, in0=tg, in1=grs, op=ALU.subtract), r=[tmpA, GR], w=[tmpA])
            kb.V(lambda e, tg=tg, grs=grs: e.scalar_tensor_tensor(out=grs, in0=tg, scalar=msk[:, 0:1], in1=grs, op0=ALU.mult, op1=ALU.add),
                 r=[tmpA, msk, GR], w=[GR])
        sel = kb.sb([16, 16, 128])
        kb.V(lambda e: e.tensor_copy(out=sel[:], in_=idf[0:16, 0:16].unsqueeze(2).to_broadcast([16, 16, 128])), r=[idf], w=[sel])
        Vp = kb.sb([64, NCH, 129], BF16)
        stg = [kb.sb([128, 512]) for _ in range(2)]
        pV = [Tn(pB[0][0:64, 0:128], pB[0].b)] * 2
        for h in range(nh):
            build_Vp(kb, PT, 1024 + h * 128, Vp, 129, idf, pt_bufs, stg, pV)
            gens = []
            for d in range(2):
                r = d * 4 + h

                def load(kind, dst, tmp, c0, h=h, r=r):
                    if kind in ("lf", "ig"):
                        rr = r + (8 if kind == "lf" else 0)
                        for g in range((HT + 511) // 512):
                            t0 = g * 512; tn = min(512, HT - t0); p = pB[g % 2]
                            kb.PE(lambda e, p=p, t0=t0, tn=tn, rr=rr: e.matmul(p[:, 0:tn], lhsT=sel[:, rr, :], rhs=GR[:, c0 * 64 + t0:c0 * 64 + t0 + tn],
                                                                          start=True, stop=True), r=[sel, GR], w=[p])
                            kb.VA(g, lambda e, a, p=p, t0=t0, tn=tn: (e.copy(out=flat(dst)[:, t0:t0 + tn], in_=p[:, 0:tn]) if a else
                                                                     e.tensor_copy(out=flat(dst)[:, t0:t0 + tn], in_=p[:, 0:tn])), r=[p], w=[dst])
                        return True
                    conv3(kb, dst, tmp, PT, pt_bufs, h * 128 if kind == "q" else 512 + h * 128, cw, h if kind == "q" else 4 + h, c0)
                    return True

                gens.append(gla_engine(kb, tls[d], load, Vp, 129, d == 1, mb if d == 1 else mf, idb, OD[d], od_bufs[d], h * 128, True))
            run_interleaved(gens)


def phase_gla_merge(kb, OD, od_bufs, PT, pt_bufs, gate_row0, gate_func, gain, Y, y_buf, nh=4):
    with kb.phase():
        idf = kb.identity(F32)
        NB_ = 4
        W_ = nh * 128
        gr = kb.sb([128, W_])
        kb.dma(gr[:], gain.rearrange("(o n) -> o n", o=1)[:, 0:W_].partition_broadcast(128), w=[gr])
        o0s = [kb.sb([128, W_]) for _ in range(NB_)]; o1s = [kb.sb([128, W_]) for _ in range(NB_)]
        gts = [kb.sb([128, nh, 128]) for _ in range(NB_)]
        pg = [kb.ps([128, 512]) for _ in range(NB_)]
        gs = [kb.sb([128, W_]) for _ in range(NB_)]
        junk = kb.sb([128, 128]); sss = [kb.sb([128, 4]) for _ in range(NB_)]
        ys = [kb.sb([128, W_]) for _ in range(NB_)]
        def tile(i):
            o0 = o0s[i % NB_]; o1 = o1s[i % NB_]; gt = gts[i % NB_]; p = pg[i % NB_]; g = gs[i % NB_]; ss = sss[i % NB_]; y = ys[i % NB_]
            kb.dma(o0[:], OD[0][i * 128:(i + 1) * 128, 0:W_], r=od_bufs[0][2 * i:2 * i + 2], w=[o0])
            kb.dma(o1[:], OD[1][i * 128:(i + 1) * 128, 0:W_], r=od_bufs[1][2 * i:2 * i + 2], w=[o1])
            kb.dma(gt[:], PT[gate_row0:gate_row0 + W_, i * 128:(i + 1) * 128].rearrange("(h p) t -> p h t", p=128), r=pt_bufs, w=[gt])
            yield
            for h in range(nh):
                kb.PE(lambda e, h=h, gt=gt, p=p: e.transpose(out=p[:, h * 128:(h + 1) * 128], in_=gt[:, h, :], identity=idf[:]), r=[gt, idf], w=[p])
            yield
            kb.A(lambda e, g=g, p=p: e.activation(out=g[:], in_=p[:, 0:W_], func=AF.Exp, scale=-1.0), r=[p], w=[g])
            yield
            kb.V(lambda e, g=g: e.tensor_scalar(out=g[:], in0=g[:], scalar1=1.0, scalar2=None, op0=ALU.add), r=[g], w=[g])
            kb.V(lambda e, g=g: e.reciprocal(out=g[:], in_=g[:]), r=[g], w=[g])
            if gate_func == AF.Silu:
                kb.V(lambda e, g=g, p=p: e.tensor_tensor(out=g[:], in0=g[:], in1=p[:, 0:W_], op=ALU.mult), r=[g, p], w=[g])
            kb.G(lambda e, o0=o0, o1=o1: e.tensor_tensor(out=o0[:], in0=o0[:], in1=o1[:], op=ALU.add), r=[o0, o1], w=[o0])
            yield
            for h in range(nh):
                kb.A(lambda e, h=h, o0=o0, ss=ss: e.activation(out=junk[:], in_=o0[:, h * 128:(h + 1) * 128], func=AF.Square,
                                                             accum_out=ss[:, h:h + 1]), r=[o0], w=[junk, ss])
            yield
            kb.V(lambda e, ss=ss: e.tensor_scalar(out=ss[:], in0=ss[:], scalar1=1.0 / 128, scalar2=EPS, op0=ALU.mult, op1=ALU.add), r=[ss], w=[ss])
            yield
            kb.A(lambda e, ss=ss: e.activation(out=ss[:], in_=ss[:], func=AF.Ln), r=[ss], w=[ss])
            kb.A(lambda e, ss=ss: e.activation(out=ss[:], in_=ss[:], func=AF.Exp, scale=-0.5), r=[ss], w=[ss])
            yield
            kb.V(lambda e, g=g: e.tensor_tensor(out=g[:], in0=g[:], in1=gr[:], op=ALU.mult), r=[g, gr], w=[g])
            for h in range(nh):
                kb.V(lambda e, h=h, o0=o0, ss=ss, g=g, y=y: e.scalar_tensor_tensor(
                    out=y[:, h * 128:(h + 1) * 128], in0=o0[:, h * 128:(h + 1) * 128], scalar=ss[:, h:h + 1], in1=g[:, h * 128:(h + 1) * 128],
                    op0=ALU.mult, op1=ALU.mult), r=[o0, ss, g], w=[y])
            kb.dma(Y[i * 128:(i + 1) * 128, 0:W_], y[:], r=[y], w=[y_buf[i]])


            yield

        for i0_ in range(0, NT, NB_):
            run_interleaved([tile(i) for i in range(i0_, min(NT, i0_ + NB_))])


def phase_hgrn(kb, PT, pt_bufs, hg_lb, OD, od_bufs, nh=4):
    with kb.phase():
        idf = kb.identity(F32)
        idb = kb.sb([128, 128], BF16)
        kb.V(lambda e: e.tensor_copy(out=idb[:], in_=idf[:]), r=[idf], w=[idb])
        mf, mb = make_masks(kb)
        pT_ = kb.ps([64, 128], BF16)
        tls = [gla_tiles(kb, pT_), gla_tiles(kb, pT_)]
        ptm = kb.ps([128, 128])
        pV = [Tn(ptm[0:64, 0:128], ptm.b)] * 2
        lbr = kb.sb([128, 8]); lb = kb.sb([128, 4]); oml = kb.sb([128, 4])
        load_rows_T(kb, hg_lb.rearrange("l (c p) -> (l c) p", p=128), 8, idf, ptm, lbr[:], lbr)
        kb.V(lambda e: e.tensor_tensor(out=lb[:], in0=lbr[:, 4:8], in1=lbr[:, 0:4], op=ALU.subtract), r=[lbr], w=[lb])
        kb.A(lambda e: e.activation(out=lb[:], in_=lb[:], func=AF.Sigmoid), r=[lb], w=[lb])
        kb.V(lambda e: e.tensor_scalar(out=oml[:], in0=lb[:], scalar1=-1.0, scalar2=1.0, op0=ALU.mult, op1=ALU.add), r=[lb], w=[oml])
        Vp = kb.sb([64, NCH, 128], BF16)
        stg = [kb.sb([128, 512]) for _ in range(2)]
        for h in range(nh):
            build_Vp(kb, PT, 1536 + h * 128, Vp, 128, idf, pt_bufs, stg, pV)
            gens = []
            for d in range(2):
                def load(kind, dst, tmp, c0, h=h, d=d):
                    if kind == "ig":
                        return False
                    row0 = h * 128 if kind == "q" else 512 + d * 512 + h * 128
                    fd = flat(dst)
                    kb.dma(fd, PT[row0:row0 + 128, c0 * 64:c0 * 64 + HT], r=pt_bufs, w=[dst])
                    if kind == "q":
                        kb.A(lambda e: e.activation(out=fd, in_=fd, func=AF.Silu), r=[dst], w=[dst])
                        kb.V(lambda e: e.tensor_scalar(out=fd, in0=fd, scalar1=128.0 ** -0.5, scalar2=None, op0=ALU.mult), r=[dst], w=[dst])
                    elif kind == "lf":
                        kb.A(lambda e: e.activation(out=fd, in_=fd, func=AF.Sigmoid), r=[dst], w=[dst])
                        kb.V(lambda e: e.tensor_scalar(out=fd, in0=fd, scalar1=oml[:, h:h + 1], scalar2=lb[:, h:h + 1], op0=ALU.mult, op1=ALU.add),
                             r=[dst, oml, lb], w=[dst])
                        kb.A(lambda e: e.activation(out=fd, in_=fd, func=AF.Ln), r=[dst], w=[dst])
                    else:
                        kb.A(lambda e: e.activation(out=fd, in_=fd, func=AF.Sigmoid, scale=-1.0), r=[dst], w=[dst])
                        kb.V(lambda e: e.tensor_scalar(out=fd, in0=fd, scalar1=oml[:, h:h + 1], scalar2=None, op0=ALU.mult), r=[dst, oml], w=[dst])
                    return True

                gens.append(gla_engine(kb, tls[d], load, Vp, 128, d == 1, mb if d == 1 else mf, idb, OD[d], od_bufs[d], h * 128, False))
            run_interleaved(gens)

TL = T - NCTX
SCALE = 192.0 ** -0.5


def fm_rmsnorm(kb, PT, pt_bufs, row0, nch, t_lo, t_hi, gainT, ones, out, out_off, srcs, sq, pss, rst):
    n = nch * 128
    for bi, t0 in enumerate(range(t_lo, t_hi, 512)):
        tn = min(512, t_hi - t0)
        s = srcs[bi % 2]; ps = pss[bi % 2]; rs = rst[bi % 2]
        kb.dma(s[:, 0:nch, 0:tn], PT[row0:row0 + n, t0:t0 + tn].rearrange("(k p) t -> p k t", p=128), r=pt_bufs, w=[s])
        kb.A(lambda e, s=s, tn=tn: e.activation(out=sq[:, 0:nch, 0:tn], in_=s[:, 0:nch, 0:tn], func=AF.Square), r=[s], w=[sq])
        for k in range(nch):
            kb.PE(lambda e, k=k, ps=ps, tn=tn: e.matmul(ps[:, 0:tn], lhsT=ones[:], rhs=sq[:, k, 0:tn], start=(k == 0), stop=(k == nch - 1)),
                  r=[ones, sq], w=[ps])
        kb.V(lambda e, ps=ps, rs=rs, tn=tn: e.tensor_scalar(out=rs[:, 0:tn], in0=ps[:, 0:tn], scalar1=1.0 / n, scalar2=EPS, op0=ALU.mult, op1=ALU.add),
             r=[ps], w=[rs])
        kb.A(lambda e, rs=rs, tn=tn: e.activation(out=rs[:, 0:tn], in_=rs[:, 0:tn], func=AF.Sqrt), r=[rs], w=[rs])
        kb.V(lambda e, rs=rs, tn=tn: e.reciprocal(out=rs[:, 0:tn], in_=rs[:, 0:tn]), r=[rs], w=[rs])
        for k in range(nch):
            kb.V(lambda e, k=k, s=s, rs=rs, tn=tn, t0=t0: e.scalar_tensor_tensor(
                out=out[:, k, t0 - out_off:t0 - out_off + tn], in0=s[:, k, 0:tn], scalar=gainT[:, k:k + 1], in1=rs[:, 0:tn],
                op0=ALU.mult, op1=ALU.mult), r=[s, rs, gainT], w=[out])


def rope_mul(kb, eng, out3, in3, tab, cs, r0, nr):
    eng(lambda e: e.tensor_tensor(out=out3[0:32], in0=in3[0:32], in1=tab[0:32, cs, r0:r0 + nr].unsqueeze(2).to_broadcast([32, nr, 64]),
                                  op=ALU.mult))
    eng(lambda e: e.tensor_tensor(out=out3[32:64], in0=in3[32:64], in1=tab[32:64, cs, 0:64].unsqueeze(1).to_broadcast([32, nr, 64]),
                                  op=ALU.mult))


def phase_mla(kb, PT, pt_bufs, q_norm, w_qb, kv_norm, w_kvb, rope_tab, Y, y_buf, nh=4):
    STOP = 99
    with kb.phase():
        idf = kb.identity(F32)
        ones = kb.sb([128, 128])
        kb.V(lambda e: e.memset(ones[:], 1.0), w=[ones])
        pA = kb.ps([128, 512]); pBk = kb.ps([128, 512])
        gq = kb.sb([128, 2]); gkv = kb.sb([128, 1])
        load_rows_T(kb, q_norm.rearrange("(k p) -> k p", p=128), 2, idf, pA, gq[:], gq)
        load_rows_T(kb, kv_norm.rearrange("(k p) -> k p", p=128), 1, idf, pA, gkv[:], gkv)
        stage = [kb.sb([128, 2, 512]) for _ in range(2)]
        wq = kb.load_w_bf16(w_qb, 256, 768, stage)
        wkv = kb.load_w_bf16(w_kvb, 128, 1024, stage)
        wrot = kb.sb([128, 2, 4, 64], BF16)
        wq4 = wq[:].rearrange("p k (h c) -> p k h c", h=4)
        for (o0, i0, sg) in ((0, 16, -1.0), (16, 0, 1.0), (32, 48, -1.0), (48, 32, 1.0)):
            kb.V(lambda e, o0=o0, i0=i0, sg=sg: e.tensor_scalar(out=wrot[:, :, :, o0:o0 + 16], in0=wq4[:, :, :, 128 + i0:128 + i0 + 16], scalar1=sg,
                                                               scalar2=None, op0=ALU.mult), r=[wq], w=[wrot])
        rotm = kb.sb([64, 64], BF16)
        for (o0, i0, sg) in ((0, 16, -1.0), (16, 0, 1.0), (32, 48, -1.0), (48, 32, 1.0)):
            kb.V(lambda e, o0=o0, i0=i0, sg=sg: e.tensor_scalar(out=rotm[:, o0:o0 + 16], in0=idf[0:64, i0:i0 + 16], scalar1=sg, scalar2=None,
                                                               op0=ALU.mult), r=[idf], w=[rotm])
        tab = kb.sb([64, 2, 128])
        kb.dma(tab[:], rope_tab[:, :, :], w=[tab])
        if STOP <= 1:
            return
        srcs = [kb.sb([128, 2, 512]) for _ in range(2)]; sq = kb.sb([128, 2, 512]); rst = [kb.sb([128, 512]) for _ in range(2)]
        cqn = kb.sb([128, 2, TL], BF16); ckvn = kb.sb([128, 1, T], BF16)
        fm_rmsnorm(kb, PT, pt_bufs, 2560, 2, NCTX, T, gq, ones, cqn, NCTX, srcs, sq, [pA, pBk], rst)
        fm_rmsnorm(kb, PT, pt_bufs, 2816, 1, 0, T, gkv, ones, ckvn, 0, srcs, sq, [pA, pBk], rst)
        if STOP <= 2:
            return
        krT = kb.sb([64, T], BF16)
        krf = [kb.sb([64, 512]) for _ in range(2)]; krb = [kb.sb([64, 512], BF16) for _ in range(2)]
        t1s = [kb.sb([64, 512]) for _ in range(2)]; t2s = [kb.sb([64, 512]) for _ in range(2)]
        for bi, t0 in enumerate(range(0, T, 512)):
            tn = min(512, T - t0)
            f = krf[bi % 2]; b_ = krb[bi % 2]; t1 = t1s[bi % 2]; t2 = t2s[bi % 2]
            kb.dma(f[:, 0:tn], PT[2944:3008, t0:t0 + tn], r=pt_bufs, w=[f])
            segs = []
            if t0 < NCTX:
                kb.V(lambda e, f=f: e.tensor_copy(out=krT[:, 0:NCTX], in_=f[:, 0:NCTX]), r=[f], w=[krT])
                segs.append((NCTX, tn))
            else:
                segs.append((0, tn))
            for (a, b2) in segs:
                if a >= b2:
                    continue
                n_ = b2 - a
                r0 = (t0 + a - NCTX) // 64; nr = n_ // 64
                kb.A(lambda e, f=f, b_=b_, a=a, b2=b2: e.copy(out=b_[:, a:b2], in_=f[:, a:b2]), r=[f], w=[b_])
                kb.PE(lambda e, b_=b_, a=a, n_=n_: e.matmul(pA[0:64, 0:n_], lhsT=rotm[:], rhs=b_[:, a:a + n_], start=True, stop=True),
                      r=[rotm, b_], w=[pA])
                v3 = lambda x, a=a, n_=n_: x[:, a:a + n_].rearrange("p (r c) -> p r c", c=64)
                rope_mul(kb, lambda fn: kb.V(fn, r=[f, tab], w=[t1]), v3(t1), v3(f), tab, 0, r0, nr)
                rope_mul(kb, lambda fn: kb.V(fn, r=[pA, tab], w=[t2]), v3(t2), pA[0:64, 0:n_].rearrange("p (r c) -> p r c", c=64), tab, 1, r0, nr)
                kb.G(lambda e, t1=t1, t2=t2, a=a, n_=n_, t0=t0: e.tensor_tensor(out=krT[:, t0 + a:t0 + a + n_], in0=t1[:, a:a + n_], in1=t2[:, a:a + n_],
                                                                              op=ALU.add), r=[t1, t2], w=[krT])
        if STOP <= 3:
            return
        knT = kb.sb([128, T], BF16)
        Vp = kb.sb([128, NT, 129], BF16)
        kb.G(lambda e: e.memset(Vp[:, :, 128:129], 1.0), w=[Vp])
        qn = [kb.sb([128, 512], BF16) for _ in range(2)]; qr = [kb.sb([64, 512], BF16) for _ in range(2)]
        PTs = [kb.sb([128, 512], BF16) for _ in range(3)]
        pS = [kb.ps([128, 512]) for _ in range(2)]
        pO = [kb.ps([128, 132]) for _ in range(4)]
        rd = [kb.sb([128, 1]) for _ in range(4)]; ob = [kb.sb([128, 128]) for _ in range(4)]
        cnt = 0
        for h in range(nh):
            for bi, t0 in enumerate(range(0, T, 512)):
                tn = min(512, T - t0)
                p = (pA, pBk)[bi % 2]
                kb.PE(lambda e, p=p, t0=t0, tn=tn: e.matmul(p[:, 0:tn], lhsT=wkv[:, 0, h * 256:h * 256 + 128], rhs=ckvn[:, 0, t0:t0 + tn],
                                                           start=True, stop=True), r=[wkv, ckvn], w=[p])
                kb.VA(bi, lambda e, a, p=p, t0=t0, tn=tn: (e.copy(out=knT[:, t0:t0 + tn], in_=p[:, 0:tn]) if a else
                                                          e.tensor_copy(out=knT[:, t0:t0 + tn], in_=p[:, 0:tn])), r=[p], w=[knT])
            for kt in range(NT):
                p = (pA, pBk)[kt % 2]
                kb.PE(lambda e, p=p, kt=kt: e.matmul(p[:, 0:128], lhsT=ckvn[:, 0, kt * 128:(kt + 1) * 128], rhs=wkv[:, 0, h * 256 + 128:h * 256 + 256],
                                                    start=True, stop=True), r=[wkv, ckvn], w=[p])
                kb.VA(kt, lambda e, a, p=p, kt=kt: (e.copy(out=Vp[:, kt, 0:128], in_=p[:, 0:128]) if a else
                                                    e.tensor_copy(out=Vp[:, kt, 0:128], in_=p[:, 0:128])), r=[p], w=[Vp])
            if STOP <= 4:
                return
            for qb in range(TL // 512):
                q0 = qb * 512
                qn_ = qn[qb % 2]; qr_ = qr[qb % 2]; t1 = t1s[qb % 2]; t2 = t2s[qb % 2]
                for k in range(2):
                    kb.PE(lambda e, k=k: e.matmul(pA[:, :], lhsT=wq[:, k, h * 192:h * 192 + 128], rhs=cqn[:, k, q0:q0 + 512], start=(k == 0), stop=(k == 1)),
                          r=[wq, cqn], w=[pA])
                kb.A(lambda e, qn_=qn_: e.copy(out=qn_[:], in_=pA[:, :]), r=[pA], w=[qn_])
                for k in range(2):
                    kb.PE(lambda e, k=k: e.matmul(pBk[0:64, :], lhsT=wq[:, k, h * 192 + 128:h * 192 + 192], rhs=cqn[:, k, q0:q0 + 512],
                                                  start=(k == 0), stop=(k == 1)), r=[wq, cqn], w=[pBk])
                v3 = lambda x: x[:, 0:512].rearrange("p (r c) -> p r c", c=64)
                rope_mul(kb, lambda fn: kb.V(fn, r=[pBk, tab], w=[t1]), v3(t1), pBk[0:64, :].rearrange("p (r c) -> p r c", c=64), tab, 0, q0 // 64, 8)
                for k in range(2):
                    kb.PE(lambda e, k=k: e.matmul(pBk[0:64, :], lhsT=wrot[:, k, h, :], rhs=cqn[:, k, q0:q0 + 512], start=(k == 0), stop=(k == 1)),
                          r=[wrot, cqn], w=[pBk])
                rope_mul(kb, lambda fn: kb.V(fn, r=[pBk, tab], w=[t2]), v3(t2), pBk[0:64, :].rearrange("p (r c) -> p r c", c=64), tab, 1, q0 // 64, 8)
                kb.G(lambda e, qr_=qr_, t1=t1, t2=t2: e.tensor_tensor(out=qr_[:], in0=t1[:], in1=t2[:], op=ALU.add), r=[t1, t2], w=[qr_])
                if STOP <= 5:
                    return
                def s_mm(kt, ps):
                    kb.PE(lambda e, ps=ps, kt=kt, qn_=qn_: e.matmul(ps[:], lhsT=knT[:, kt * 128:(kt + 1) * 128], rhs=qn_[:], start=True, stop=False),
                          r=[knT, qn_], w=[ps])
                    kb.PE(lambda e, ps=ps, kt=kt, qr_=qr_: e.matmul(ps[:], lhsT=krT[:, kt * 128:(kt + 1) * 128], rhs=qr_[:], start=False, stop=True),
                          r=[krT, qr_, ps], w=[ps])
                s_mm(0, pS[cnt % 2])
                for kt in range(NT):
                    ps = pS[cnt % 2]; pt_ = PTs[cnt % 3]
                    cnt += 1
                    kb.A(lambda e, ps=ps, pt_=pt_: e.activation(out=pt_[:], in_=ps[:], func=AF.Exp, scale=SCALE), r=[ps], w=[pt_])
                    if kt + 1 < NT:
                        s_mm(kt + 1, pS[cnt % 2])
                    for j in range(4):
                        kb.PE(lambda e, j=j, pt_=pt_, kt=kt: e.matmul(pO[j][:, 0:129], lhsT=pt_[:, j * 128:(j + 1) * 128], rhs=Vp[:, kt, :],
                                                                   start=(kt == 0), stop=(kt == NT - 1)), r=[pt_, Vp] + ([pO[j]] if kt else []), w=[pO[j]])
                for j in range(4):
                    kb.V(lambda e, j=j: e.reciprocal(out=rd[j][:], in_=pO[j][:, 128:129]), r=[pO[j]], w=[rd[j]])
                    kb.A(lambda e, j=j: e.activation(out=ob[j][:], in_=pO[j][:, 0:128], func=AF.Identity, scale=rd[j][:, 0:1]), r=[pO[j], rd[j]], w=[ob[j]])
                    tt = NCTX + q0 + j * 128
                    kb.dma(Y[tt:tt + 128, 512 + h * 128:512 + (h + 1) * 128], ob[j][:], r=[ob[j]], w=[y_buf[tt // 128]])
                if STOP <= 6:
                    return
            if STOP <= 7 + h:
                return

RC = 128
NRC = T // RC
SEG = 4
RW0 = 2064
DEC = -(2.718281828459045 ** -0.5)


def rw_runs(rev):
    order = ([1, 0] + list(range(NRC - 1, 1, -1))) if rev else list(range(NRC))
    groups = [[1, 0]] if rev else [[0, 1]]
    lat = order[2:]
    for i in range(0, len(lat), SEG):
        groups.append(lat[i:i + SEG])
    return groups


def shiftload(kb, dst, X, PT, pt_bufs, row0, t0, n, w3, j, wt):
    t1 = t0 + n
    la = max(t0 - 1, 0); lb = min(t1 + 1, T)
    kb.dma(X[:, la - t0 + 1:lb - t0 + 1], PT[row0:row0 + 128, la:lb], r=pt_bufs, w=[X])
    kb.V(lambda e: e.tensor_scalar(out=dst, in0=X[:, 1:n + 1], scalar1=w3[:, 1, j:j + 1], scalar2=None, op0=ALU.mult), r=[X, w3], w=[wt])
    for sa, sb in ((0, NCTX), (NCTX, T)):
        a = max(sa, t0); b = min(sb, t1)
        if a >= b:
            continue
        lo = a + 1 if a == sa else a
        hi = b - 1 if b == sb else b
        kb.V(lambda e, lo=lo, b=b: e.scalar_tensor_tensor(out=dst[:, lo - t0:b - t0], in0=X[:, lo - t0:b - t0], scalar=w3[:, 0, j:j + 1],
                                                         in1=dst[:, lo - t0:b - t0], op0=ALU.mult, op1=ALU.add), r=[X, w3, wt], w=[wt])
        kb.V(lambda e, a=a, hi=hi: e.scalar_tensor_tensor(out=dst[:, a - t0:hi - t0], in0=X[:, a - t0 + 2:hi - t0 + 2], scalar=w3[:, 2, j:j + 1],
                                                         in1=dst[:, a - t0:hi - t0], op0=ALU.mult, op1=ALU.add), r=[X, w3, wt], w=[wt])


def rw_consts(kb, idf, ptm, rw_mu):
    mu = kb.sb([128, 15]); w3 = kb.sb([128, 3, 15])
    load_rows_T(kb, rw_mu.rearrange("(k p) -> k p", p=128), 15, idf, ptm, mu[:], mu)
    kb.V(lambda e: e.tensor_scalar(out=w3[:, 0, :], in0=mu[:], scalar1=0.5, scalar2=None, op0=ALU.mult), r=[mu], w=[w3])
    kb.V(lambda e: e.tensor_scalar(out=w3[:, 2, :], in0=mu[:], scalar1=0.5, scalar2=None, op0=ALU.mult), r=[mu, w3], w=[w3])
    kb.V(lambda e: e.tensor_scalar(out=w3[:, 1, :], in0=mu[:], scalar1=-1.0, scalar2=1.0, op0=ALU.mult, op1=ALU.add), r=[mu, w3], w=[w3])
    return w3


def phase_rwkv(kb, PT, pt_bufs, prm, OD, od_bufs, BN, bn_bufs, npair=4):
    with kb.phase():
        idf = kb.identity(F32)
        pbanks = [kb.ps([128, 512]) for _ in range(8)]
        pP = [pbanks[0], pbanks[0]]
        w3 = rw_consts(kb, idf, pP[0], prm["rw_mu"])
        ones = kb.sb([128, 128]); kb.G(lambda e: e.memset(ones[:], 1.0), w=[ones])
        msk = {}
        for nm, pat, cm, op in (("gt_up", 1, -1, ALU.is_gt), ("ge_up", 1, -1, ALU.is_ge), ("gt_lo", -1, 1, ALU.is_gt), ("ge_lo", -1, 1, ALU.is_ge)):
            m = kb.sb([128, 128])
            kb.G(lambda e, m=m, pat=pat, cm=cm, op=op: e.affine_select(out=m[:], in_=ones[:], pattern=[[pat, 128]], compare_op=op, fill=0.0, base=0,
                                                                       channel_multiplier=cm), r=[ones], w=[m])
            msk[nm] = m
        bones = kb.sb([128, 128])
        kb.G(lambda e: e.memset(bones[:], 0.0), w=[bones])
        kb.G(lambda e: e.memset(bones[0:64, 0:64], 1.0), r=[bones], w=[bones])
        kb.G(lambda e: e.memset(bones[64:128, 64:128], 1.0), r=[bones], w=[bones])
        pv = {}
        for nm in ("rw_kk", "rw_ka", "rw_rk"):
            t_ = kb.sb([128, 4])
            load_rows_T(kb, prm[nm].rearrange("(k p) -> k p", p=128), 4, idf, pP[0], t_[:], t_)
            pv[nm] = t_
        for nm in ("rw_w0", "rw_a0"):
            t_ = kb.sb([128, 2, 4])
            load_rows_T(kb, prm[nm].rearrange("d (k p) -> (d k) p", p=128), 8, idf, pP[0], t_[:].rearrange("p d k -> p (d k)"), t_)
            pv[nm] = t_
        w2 = kb.sb([128, 512]); a2 = kb.sb([128, 512])
        kb.dma(w2[:], prm["rw_w2"].rearrange("d r c -> (d r) c"), w=[w2])
        kb.dma(a2[:], prm["rw_a2"].rearrange("d r c -> (d r) c"), w=[a2])
        def make_stream(d, bank0):
            pP = [pbanks[bank0], pbanks[bank0]]
            pTr = pbanks[bank0]
            pI = [Tn(pbanks[bank0 + 1][:, 0:128], pbanks[bank0 + 1].b), Tn(pbanks[bank0 + 2][:, 0:128], pbanks[bank0 + 2].b)]
            pXU = Tn(pbanks[bank0 + 3][:, 0:128], pbanks[bank0 + 3].b)
            pY = Tn(pbanks[bank0 + 3][:, 128:256], pbanks[bank0 + 3].b); pH = Tn(pbanks[bank0 + 3][:, 256:384], pbanks[bank0 + 3].b)
            ST = SEG * RC
            X = kb.sb([128, ST + 2])
            nm_ = ("R", "K", "V", "WL", "AL", "KK", "LW", "AA", "KD", "BV", "CWa", "CWb", "T1", "T2", "BT", "KT")
            S = {n: kb.sb([128, ST]) for n in nm_}
            AR = kb.sb([128, SEG, 2, RC])
            GL = kb.sb([128, SEG])
            Hbd = kb.sb([128, 128])
            mats = [{n: kb.sb([128, 128]) for n in ("AabT", "PrbT", "AakT", "PrkT", "A", "Bp0", "Bp1", "Ap0", "Ap1", "TT0", "TT1")} for _ in range(4)]
            toks = [{n: kb.sb([128, 128]) for n in ("Btok", "Ktok", "Vtok")} for _ in range(2)]
            Xs = [kb.sb([128, 128]) for _ in range(2)]; Us = [kb.sb([128, 128]) for _ in range(2)]; obs = [kb.sb([128, 128]) for _ in range(2)]
            cnt = [0]

            def prep(hp, d, c_lo, nch):
                t0 = c_lo * RC; n = nch * RC
                s = {k: v[:, 0:n] for k, v in S.items()}
                rev = d == 1
                for nm, ch in (("R", hp), ("K", 4 + hp), ("V", 8 + hp), ("WL", 12), ("AL", 13)):
                    shiftload(kb, s[nm], X, PT, pt_bufs, RW0 + ch * 128, t0, n, w3, ch, S[nm])
                ds_ = slice(d * 64, d * 64 + 64)
                kb.A(lambda e: e.activation(out=s["WL"], in_=s["WL"], func=AF.Tanh), r=[S["WL"]], w=[S["WL"]])
                for b0 in range(0, n, 512):
                    bn = min(512, n - b0)
                    p = pP[(b0 // 512) % 2]
                    kb.PE(lambda e, p=p, b0=b0, bn=bn: e.matmul(p[:, 0:bn], lhsT=w2[ds_, hp * 128:(hp + 1) * 128], rhs=S["WL"][ds_, b0:b0 + bn],
                                                               start=True, stop=True), r=[w2, S["WL"]], w=[p])
                    kb.A(lambda e, p=p, b0=b0, bn=bn: e.activation(out=S["LW"][:, b0:b0 + bn], in_=p[:, 0:bn], func=AF.Sigmoid,
                                                                  bias=pv["rw_w0"][:, d, hp:hp + 1]), r=[p, pv["rw_w0"]], w=[S["LW"]])
                kb.V(lambda e: e.tensor_scalar(out=s["LW"], in0=s["LW"], scalar1=DEC, scalar2=None, op0=ALU.mult), r=[S["LW"]], w=[S["LW"]])
                for b0 in range(0, n, 512):
                    bn = min(512, n - b0)
                    p = pP[(b0 // 512) % 2]
                    kb.PE(lambda e, p=p, b0=b0, bn=bn: e.matmul(p[:, 0:bn], lhsT=a2[ds_, hp * 128:(hp + 1) * 128], rhs=S["AL"][ds_, b0:b0 + bn],
                                                               start=True, stop=True), r=[a2, S["AL"]], w=[p])
                    kb.A(lambda e, p=p, b0=b0, bn=bn: e.activation(out=S["AA"][:, b0:b0 + bn], in_=p[:, 0:bn], func=AF.Sigmoid,
                                                                  bias=pv["rw_a0"][:, d, hp:hp + 1]), r=[p, pv["rw_a0"]], w=[S["AA"]])
                kb.V(lambda e: e.tensor_scalar(out=s["KK"], in0=s["K"], scalar1=pv["rw_kk"][:, hp:hp + 1], scalar2=None, op0=ALU.mult),
                     r=[S["K"], pv["rw_kk"]], w=[S["KK"]])
                kb.G(lambda e: e.tensor_tensor(out=s["T1"], in0=s["KK"], in1=s["KK"], op=ALU.mult), r=[S["KK"]], w=[S["T1"]])
                for b0 in range(0, n, 512):
                    bn = min(512, n - b0)
                    p = pP[(b0 // 512) % 2]
                    kb.PE(lambda e, p=p, b0=b0, bn=bn: e.matmul(p[:, 0:bn], lhsT=bones[:], rhs=S["T1"][:, b0:b0 + bn], start=True, stop=True),
                          r=[bones, S["T1"]], w=[p])
                    kb.V(lambda e, p=p, b0=b0, bn=bn: e.tensor_scalar(out=S["T2"][:, b0:b0 + bn], in0=p[:, 0:bn], scalar1=1e-12, scalar2=None, op0=ALU.add),
                         r=[p], w=[S["T2"]])
                kb.A(lambda e: e.activation(out=s["T2"], in_=s["T2"], func=AF.Sqrt), r=[S["T2"]], w=[S["T2"]])
                kb.V(lambda e: e.reciprocal(out=s["T2"], in_=s["T2"]), r=[S["T2"]], w=[S["T2"]])
                kb.V(lambda e: e.tensor_tensor(out=s["KK"], in0=s["KK"], in1=s["T2"], op=ALU.mult), r=[S["KK"], S["T2"]], w=[S["KK"]])
                kb.V(lambda e: e.tensor_scalar(out=s["T1"], in0=s["AA"], scalar1=-1.0, scalar2=pv["rw_ka"][:, hp:hp + 1], op0=ALU.add, op1=ALU.mult),
                     r=[S["AA"], pv["rw_ka"]], w=[S["T1"]])
                kb.V(lambda e: e.scalar_tensor_tensor(out=s["KD"], in0=s["T1"], scalar=1.0, in1=s["K"], op0=ALU.add, op1=ALU.mult),
                     r=[S["T1"], S["K"]], w=[S["KD"]])
                kb.G(lambda e: e.tensor_tensor(out=s["BV"], in0=s["KK"], in1=s["AA"], op=ALU.mult), r=[S["KK"], S["AA"]], w=[S["BV"]])
                kb.V(lambda e: e.scalar_tensor_tensor(out=s["T1"], in0=s["R"], scalar=pv["rw_rk"][:, hp:hp + 1], in1=s["KD"], op0=ALU.mult, op1=ALU.mult),
                     r=[S["R"], S["KD"], pv["rw_rk"]], w=[S["T1"]])
                for b0 in range(0, n, 512):
                    bn = min(512, n - b0)
                    p = pP[(b0 // 512) % 2]
                    kb.PE(lambda e, p=p, b0=b0, bn=bn: e.matmul(p[:, 0:bn], lhsT=bones[:], rhs=S["T1"][:, b0:b0 + bn], start=True, stop=True),
                          r=[bones, S["T1"]], w=[p])
                    kb.V(lambda e, p=p, b0=b0, bn=bn: e.tensor_tensor(out=S["T2"][:, b0:b0 + bn], in0=p[:, 0:bn], in1=S["V"][:, b0:b0 + bn], op=ALU.mult),
                         r=[p, S["V"]], w=[S["T2"]])
                kb.dma(BN[d][hp * 128:(hp + 1) * 128, t0:t0 + n], s["T2"], r=[S["T2"]], w=[bn_bufs[d][hp]])
                v3 = lambda x: x[:, 0:n].rearrange("p (c l) -> p c l", l=RC)
                src, dst = S["LW"], S["CWa"]
                first = True
                for sft in (1, 2, 4, 8, 16, 32, 64):
                    a_, b_ = v3(src), v3(dst)
                    if not rev:
                        kb.V(lambda e, a_=a_, b_=b_, sft=sft: e.tensor_tensor(out=b_[:, :, sft:], in0=a_[:, :, sft:], in1=a_[:, :, :RC - sft], op=ALU.add),
                             r=[src], w=[dst])
                        kb.G(lambda e, a_=a_, b_=b_, sft=sft: e.tensor_copy(out=b_[:, :, :sft], in_=a_[:, :, :sft]), r=[src], w=[dst])
                    else:
                        kb.V(lambda e, a_=a_, b_=b_, sft=sft: e.tensor_tensor(out=b_[:, :, :RC - sft], in0=a_[:, :, :RC - sft], in1=a_[:, :, sft:], op=ALU.add),
                             r=[src], w=[dst])
                        kb.G(lambda e, a_=a_, b_=b_, sft=sft: e.tensor_copy(out=b_[:, :, RC - sft:], in_=a_[:, :, RC - sft:]), r=[src], w=[dst])
                    if first:
                        src, dst = S["CWa"], S["CWb"]
                        first = False
                    else:
                        src, dst = dst, src
                CW = src
                pos = 0 if rev else RC - 1
                kb.A(lambda e: e.activation(out=GL[:, 0:nch], in_=v3(CW)[:, :, pos], func=AF.Exp), r=[CW], w=[GL])
                kb.V(lambda e: e.tensor_tensor(out=s["T1"], in0=CW[:, 0:n], in1=s["LW"], op=ALU.subtract), r=[CW, S["LW"]], w=[S["T1"]])
                kb.A(lambda e: e.activation(out=s["T1"], in_=s["T1"], func=AF.Exp), r=[S["T1"]], w=[S["T1"]])
                kb.V(lambda e: e.scalar_tensor_tensor(out=AR[:, 0:nch, 0, :], in0=v3(S["KK"]), scalar=-1.0, in1=v3(S["T1"]), op0=ALU.mult, op1=ALU.mult),
                     r=[S["KK"], S["T1"]], w=[AR])
                kb.A(lambda e: e.activation(out=s["T2"], in_=CW[:, 0:n], func=AF.Exp), r=[CW], w=[S["T2"]])
                kb.G(lambda e: e.tensor_tensor(out=AR[:, 0:nch, 1, :], in0=v3(S["R"]), in1=v3(S["T2"]), op=ALU.mult), r=[S["R"], S["T2"]], w=[AR])
                kb.A(lambda e: e.activation(out=s["T1"], in_=CW[:, 0:n], func=AF.Exp, scale=-1.0), r=[CW], w=[S["T1"]])
                kb.V(lambda e: e.tensor_tensor(out=s["BT"], in0=s["BV"], in1=s["T1"], op=ALU.mult), r=[S["BV"], S["T1"]], w=[S["BT"]])
                kb.G(lambda e: e.tensor_tensor(out=s["KT"], in0=s["KD"], in1=s["T1"], op=ALU.mult), r=[S["KD"], S["T1"]], w=[S["KT"]])

            def pre(lc, rev, slot):
                cs = slice(lc * RC, (lc + 1) * RC)
                tk = toks[slot % 2]
                for nm, srcn in (("Btok", "BT"), ("Ktok", "KT"), ("Vtok", "V")):
                    kb.PE(lambda e, srcn=srcn: e.transpose(out=pTr[:, 0:128], in_=S[srcn][:, cs], identity=idf[:]), r=[S[srcn], idf], w=[pTr])
                    kb.VA(cnt[0], lambda e, a, nm=nm: (e.copy(out=tk[nm][:], in_=pTr[:, 0:128]) if a else e.tensor_copy(out=tk[nm][:], in_=pTr[:, 0:128])),
                          r=[pTr], w=[tk[nm]])
                    cnt[0] += 1
                yield
                mT_s = msk["gt_lo" if rev else "gt_up"]; mT_i = msk["ge_lo" if rev else "ge_up"]; mA = msk["gt_up" if rev else "gt_lo"]
                hm = []
                for hh in range(2):
                    m = mats[(slot % 2) * 2 + hh]
                    hs = slice(hh * 64, hh * 64 + 64)
                    p = pP[hh]
                    kb.PE(lambda e, p=p, hs=hs: e.matmul(p[:, 0:256], lhsT=S["BT"][hs, cs], rhs=AR[hs, lc].rearrange("p a l -> p (a l)"), start=True, stop=True),
                          r=[S["BT"], AR], w=[p])
                    kb.V(lambda e, p=p, m=m: e.tensor_tensor(out=m["AabT"][:], in0=p[:, 0:128], in1=mT_s[:], op=ALU.mult), r=[p, mT_s], w=[m["AabT"]])
                    kb.V(lambda e, p=p, m=m: e.tensor_tensor(out=m["PrbT"][:], in0=p[:, 128:256], in1=mT_i[:], op=ALU.mult), r=[p, mT_i], w=[m["PrbT"]])
                    kb.PE(lambda e, p=p, hs=hs: e.matmul(p[:, 256:512], lhsT=S["KT"][hs, cs], rhs=AR[hs, lc].rearrange("p a l -> p (a l)"), start=True, stop=True),
                          r=[S["KT"], AR], w=[p])
                    kb.V(lambda e, p=p, m=m: e.tensor_tensor(out=m["AakT"][:], in0=p[:, 256:384], in1=mT_s[:], op=ALU.mult), r=[p, mT_s], w=[m["AakT"]])
                    kb.V(lambda e, p=p, m=m: e.tensor_tensor(out=m["PrkT"][:], in0=p[:, 384:512], in1=mT_i[:], op=ALU.mult), r=[p, mT_i], w=[m["PrkT"]])
                    pi = pI[hh]
                    kb.PE(lambda e, pi=pi, hs=hs: e.matmul(pi[:], lhsT=AR[hs, lc, 0, :], rhs=S["BT"][hs, cs], start=True, stop=True), r=[S["BT"], AR], w=[pi])
                    kb.V(lambda e, pi=pi, m=m: e.tensor_tensor(out=m["A"][:], in0=pi[:], in1=mA[:], op=ALU.mult), r=[pi, mA], w=[m["A"]])
                    kb.G(lambda e, m=m: e.tensor_tensor(out=m["TT0"][:], in0=m["AabT"][:], in1=idf[:], op=ALU.add), r=[m["AabT"], idf], w=[m["TT0"]])
                    hm.append(m)
                    yield
                cur = [{"B": m["AabT"], "A": m["A"], "TT": m["TT0"]} for m in hm]
                yield
                for lv in range(6):
                    nxt = [{"B": hm[hh]["Bp%d" % (lv % 2)], "A": hm[hh]["Ap%d" % (lv % 2)], "TT": hm[hh]["TT%d" % ((lv + 1) % 2)]} for hh in range(2)]
                    for hh in range(2):
                        c_ = cur[hh]; pi = pI[hh]
                        kb.PE(lambda e, pi=pi, c_=c_: e.matmul(pi[:], lhsT=c_["B"][:], rhs=c_["A"][:], start=True, stop=True), r=[c_["B"], c_["A"]], w=[pi])
                    yield
                    for hh in range(2):
                        c_ = cur[hh]; pi = pI[hh]; A2 = nxt[hh]["A"]
                        kb.A(lambda e, pi=pi, A2=A2: e.copy(out=A2[:], in_=pi[:]), r=[pi], w=[A2])
                        if lv < 5:
                            kb.PE(lambda e, pi=pi, c_=c_: e.matmul(pi[:], lhsT=c_["A"][:], rhs=c_["B"][:], start=True, stop=True), r=[c_["B"], c_["A"]], w=[pi])
                    yield
                    for hh in range(2):
                        c_ = cur[hh]; pi = pI[hh]; A2 = nxt[hh]["A"]; B2 = nxt[hh]["B"]
                        if lv < 5:
                            kb.A(lambda e, pi=pi, B2=B2: e.copy(out=B2[:], in_=pi[:]), r=[pi], w=[B2])
                        kb.PE(lambda e, pi=pi, A2=A2, c_=c_: e.matmul(pi[:], lhsT=A2[:], rhs=c_["TT"][:], start=True, stop=True), r=[A2, c_["TT"]], w=[pi])
                    yield
                    for hh in range(2):
                        c_ = cur[hh]; pi = pI[hh]; TTn = nxt[hh]["TT"]
                        kb.V(lambda e, pi=pi, TTn=TTn, c_=c_: e.tensor_tensor(out=TTn[:], in0=pi[:], in1=c_["TT"][:], op=ALU.add), r=[pi, c_["TT"]], w=[TTn])
                    yield
                    cur = nxt
                return {"tk": tk, "m": hm, "TT": [c_["TT"] for c_ in cur]}

            def chain(lc, c, d, hp, pr, slot):
                tk, hm, TT = pr["tk"], pr["m"], pr["TT"]
                Xt = Xs[slot % 2]; Ut = Us[slot % 2]; ob = obs[slot % 2]
                kb.PE(lambda e: e.matmul(pXU[:], lhsT=AR[:, lc, 0, :], rhs=Hbd[:], start=True, stop=False), r=[AR, Hbd], w=[pXU])
                for hh in range(2):
                    vs = slice(hh * 64, hh * 64 + 64)
                    kb.PE(lambda e, hh=hh, vs=vs: e.matmul(pXU[:, vs], lhsT=hm[hh]["AakT"][:], rhs=tk["Vtok"][:, vs], start=False, stop=(hh == 1)),
                          r=[hm[hh]["AakT"], tk["Vtok"], pXU], w=[pXU])
                yield
                kb.A(lambda e: e.copy(out=Xt[:], in_=pXU[:]), r=[pXU], w=[Xt])
                for hh in range(2):
                    vs = slice(hh * 64, hh * 64 + 64)
                    kb.PE(lambda e, hh=hh, vs=vs: e.matmul(pXU[:, vs], lhsT=TT[hh][:], rhs=Xt[:, vs], start=True, stop=True), r=[TT[hh], Xt], w=[pXU])
                yield
                kb.V(lambda e: e.tensor_copy(out=Ut[:], in_=pXU[:]), r=[pXU], w=[Ut])
                kb.PE(lambda e: e.matmul(pY[:], lhsT=AR[:, lc, 1, :], rhs=Hbd[:], start=True, stop=False), r=[AR, Hbd], w=[pY])
                for hh in range(2):
                    vs = slice(hh * 64, hh * 64 + 64)
                    kb.PE(lambda e, hh=hh, vs=vs: e.matmul(pY[:, vs], lhsT=hm[hh]["PrbT"][:], rhs=Ut[:, vs], start=False, stop=False),
                          r=[hm[hh]["PrbT"], Ut, pY], w=[pY])
                    kb.PE(lambda e, hh=hh, vs=vs: e.matmul(pY[:, vs], lhsT=hm[hh]["PrkT"][:], rhs=tk["Vtok"][:, vs], start=False, stop=(hh == 1)),
                          r=[hm[hh]["PrkT"], tk["Vtok"], pY], w=[pY])
                yield
                kb.A(lambda e: e.copy(out=ob[:], in_=pY[:]), r=[pY], w=[ob])
                kb.dma(OD[d][c * RC:(c + 1) * RC, hp * 128:(hp + 1) * 128], ob[:], r=[ob], w=[od_bufs[d][c]])
                kb.PE(lambda e: e.matmul(pH[:], lhsT=tk["Btok"][:], rhs=Ut[:], start=True, stop=False), r=[tk["Btok"], Ut], w=[pH])
                kb.PE(lambda e: e.matmul(pH[:], lhsT=tk["Ktok"][:], rhs=tk["Vtok"][:], start=False, stop=True), r=[tk["Ktok"], tk["Vtok"], pH], w=[pH])
                yield
                kb.G(lambda e: e.tensor_scalar(out=Hbd[:], in0=Hbd[:], scalar1=GL[:, lc:lc + 1], scalar2=None, op0=ALU.mult), r=[Hbd, GL], w=[Hbd])
                for hh in range(2):
                    vs = slice(hh * 64, hh * 64 + 64)
                    kb.V(lambda e, vs=vs: e.scalar_tensor_tensor(out=Hbd[vs, vs], in0=pH[vs, vs], scalar=GL[vs, lc:lc + 1], in1=Hbd[vs, vs],
                                                                 op0=ALU.mult, op1=ALU.add), r=[pH, GL, Hbd], w=[Hbd])


            def gen(hp):
                kb.V(lambda e: e.memset(Hbd[:], 0.0), w=[Hbd])
                slot = 0
                for run in rw_runs(d == 1):
                    c_lo = min(run); nch = len(run)
                    prep(hp, d, c_lo, nch)
                    prs = {}
                    prs[0] = yield from pre(run[0] - c_lo, d == 1, slot)
                    for i, c in enumerate(run):
                        if i + 1 < nch:
                            prs[i + 1] = yield from pre(run[i + 1] - c_lo, d == 1, slot + i + 1)
                        yield from chain(c - c_lo, c, d, hp, prs.pop(i), slot + i)
                    slot += nch
            return gen

        streams = [make_stream(0, 0), make_stream(1, 4)]
        for hp in range(npair):
            gens = [st_(hp) for st_ in streams]
            live = list(gens)
            while live:
                for g_ in list(live):
                    try:
                        next(g_)
                    except StopIteration:
                        live.remove(g_)


def phase_rwkv_merge(kb, OD, od_bufs, BN, bn_bufs, PT, pt_bufs, prm, Y, y_buf, npair=4):
    with kb.phase():
        idf = kb.identity(F32)
        pBn = [kb.ps([128, 512]) for _ in range(3)]; pG = [kb.ps([128, 512]) for _ in range(3)]
        w3 = rw_consts(kb, idf, pBn[0], prm["rw_mu"])
        W_ = npair * 128; NH_ = npair * 2
        g2 = kb.sb([128, W_]); lnw = kb.sb([128, W_]); lnb = kb.sb([128, W_])
        kb.dma(g2[:], prm["rw_g2"][:, 0:W_], w=[g2])
        kb.dma(lnw[:], prm["rw_ln_w"].rearrange("(o n) -> o n", o=1)[:, 0:W_].partition_broadcast(128), w=[lnw])
        kb.dma(lnb[:], prm["rw_ln_b"].rearrange("(o n) -> o n", o=1)[:, 0:W_].partition_broadcast(128), w=[lnb])
        X = kb.sb([128, 130])
        o0s = [kb.sb([128, NH_, 64]) for _ in range(3)]; o1s = [kb.sb([128, NH_, 64]) for _ in range(3)]
        b0s = [kb.sb([128, npair, 128]) for _ in range(3)]; b1s = [kb.sb([128, npair, 128]) for _ in range(3)]
        gls = [kb.sb([128, 128]) for _ in range(3)]
        sqs = [kb.sb([128, NH_, 64]) for _ in range(3)]
        mus = [kb.sb([128, NH_]) for _ in range(3)]; vrs = [kb.sb([128, NH_]) for _ in range(3)]
        ys = [kb.sb([128, W_]) for _ in range(3)]
        def tile(i):
            t0 = i * 128
            o0 = o0s[i % 3]; o1 = o1s[i % 3]; b0 = b0s[i % 3]; b1 = b1s[i % 3]; gl = gls[i % 3]; sq = sqs[i % 3]
            mu = mus[i % 3]; vr = vrs[i % 3]; y = ys[i % 3]; pb = pBn[i % 3]; pg = pG[i % 3]
            f = lambda x: x[:].rearrange("p h c -> p (h c)")
            kb.dma(f(o0), OD[0][t0:t0 + 128, 0:W_], r=[od_bufs[0][i]], w=[o0])
            kb.dma(f(o1), OD[1][t0:t0 + 128, 0:W_], r=[od_bufs[1][i]], w=[o1])
            kb.dma(b0[:], BN[0][0:W_, t0:t0 + 128].rearrange("(k p) t -> p k t", p=128), r=bn_bufs[0][0:npair], w=[b0])
            kb.dma(b1[:], BN[1][0:W_, t0:t0 + 128].rearrange("(k p) t -> p k t", p=128), r=bn_bufs[1][0:npair], w=[b1])
            yield
            kb.G(lambda e, b0=b0, b1=b1: e.tensor_tensor(out=b0[:], in0=b0[:], in1=b1[:], op=ALU.add), r=[b0, b1], w=[b0])
            for k in range(npair):
                kb.PE(lambda e, k=k, b0=b0, pb=pb: e.transpose(out=pb[:, k * 128:(k + 1) * 128], in_=b0[:, k, :], identity=idf[:]), r=[b0, idf], w=[pb])
            yield
            shiftload(kb, gl[:], X, PT, pt_bufs, RW0 + 14 * 128, t0, 128, w3, 14, gl)
            yield
            kb.A(lambda e, gl=gl: e.activation(out=gl[:], in_=gl[:], func=AF.Exp, scale=-1.0), r=[gl], w=[gl])
            yield
            kb.V(lambda e, gl=gl: e.tensor_scalar(out=gl[:], in0=gl[:], scalar1=1.0, scalar2=None, op0=ALU.add), r=[gl], w=[gl])
            kb.V(lambda e, gl=gl: e.reciprocal(out=gl[:], in_=gl[:]), r=[gl], w=[gl])
            kb.PE(lambda e, gl=gl, pg=pg: e.matmul(pg[:, 0:W_], lhsT=gl[:], rhs=g2[:], start=True, stop=True), r=[gl, g2], w=[pg])
            yield
            kb.G(lambda e, o0=o0, o1=o1: e.tensor_tensor(out=o0[:], in0=o0[:], in1=o1[:], op=ALU.add), r=[o0, o1], w=[o0])
            yield
            kb.V(lambda e, o0=o0, mu=mu: e.reduce_sum(out=mu[:], in_=o0[:], axis=AX.X), r=[o0], w=[mu])
            kb.V(lambda e, mu=mu: e.tensor_scalar(out=mu[:], in0=mu[:], scalar1=1.0 / 64, scalar2=None, op0=ALU.mult), r=[mu], w=[mu])
            kb.V(lambda e, o0=o0, mu=mu: e.tensor_tensor(out=o0[:], in0=o0[:], in1=mu[:].unsqueeze(2).to_broadcast([128, NH_, 64]), op=ALU.subtract),
                 r=[o0, mu], w=[o0])
            yield
            kb.G(lambda e, o0=o0, sq=sq: e.tensor_tensor(out=sq[:], in0=o0[:], in1=o0[:], op=ALU.mult), r=[o0], w=[sq])
            yield
            kb.V(lambda e, sq=sq, vr=vr: e.reduce_sum(out=vr[:], in_=sq[:], axis=AX.X), r=[sq], w=[vr])
            kb.V(lambda e, vr=vr: e.tensor_scalar(out=vr[:], in0=vr[:], scalar1=1.0 / 64, scalar2=64e-5, op0=ALU.mult, op1=ALU.add), r=[vr], w=[vr])
            yield
            kb.A(lambda e, vr=vr: e.activation(out=vr[:], in_=vr[:], func=AF.Ln), r=[vr], w=[vr])
            kb.A(lambda e, vr=vr: e.activation(out=vr[:], in_=vr[:], func=AF.Exp, scale=-0.5), r=[vr], w=[vr])
            yield
            kb.V(lambda e, o0=o0, vr=vr: e.tensor_tensor(out=o0[:], in0=o0[:], in1=vr[:].unsqueeze(2).to_broadcast([128, NH_, 64]), op=ALU.mult),
                 r=[o0, vr], w=[o0])
            yield
            kb.G(lambda e, o0=o0: e.tensor_tensor(out=f(o0), in0=f(o0), in1=lnw[:], op=ALU.mult), r=[o0, lnw], w=[o0])
            kb.G(lambda e, o0=o0: e.tensor_tensor(out=f(o0), in0=f(o0), in1=lnb[:], op=ALU.add), r=[o0, lnb], w=[o0])
            yield
            kb.V(lambda e, o0=o0, pb=pb: e.tensor_tensor(out=f(o0), in0=f(o0), in1=pb[:, 0:W_], op=ALU.add), r=[o0, pb], w=[o0])
            kb.V(lambda e, o0=o0, pg=pg, y=y: e.tensor_tensor(out=y[:], in0=f(o0), in1=pg[:, 0:W_], op=ALU.mult), r=[o0, pg], w=[y])
            kb.dma(Y[t0:t0 + 128, 512:512 + W_], y[:], r=[y], w=[y_buf[i]])
            yield

        for i0_ in range(0, NT, 3):
            run_interleaved([tile(i) for i in range(i0_, min(NT, i0_ + 3))])

EV_BLOCKS = [(0, 128, 0), (128, 128, 512), (256, 128, 1024), (384, 128, 1536), (512, 16, 2048), (528, 128, 2064), (656, 128, 2576),
             (784, 128, 3088), (912, 128, 3600), (1040, 128, 3728), (1168, 128, 3856)]
EV_NC = 1296
OD_BLOCKS = [(0, 128, 0), (128, 128, 512), (256, 128, 1024), (384, 128, 1536), (512, 128, 2048), (640, 128, 2560), (768, 128, 2688),
             (896, 128, 2816), (1024, 64, 2944)]
OD_NC = 1088
GROUPS = [[0, 1, 2, 3], [4, 5, 6, 7]]


def rope_table():
    inv = 10000.0 ** (-np.arange(16, dtype=np.float32) / 16)
    tab = np.zeros((64, 2, 128), np.float32)
    for half, n in ((0, 128), (1, 64)):
        ang = np.arange(n, dtype=np.float32)[None, :] * inv[:, None]
        for q in range(2):
            p0 = half * 32 + q * 16
            tab[p0:p0 + 16, 0, :n] = np.cos(ang)
            tab[p0:p0 + 16, 1, :n] = np.sin(ang)
    return tab


RW_SHAPES = {"rw_mu": [1920], "rw_w0": [2, 512], "rw_w2": [2, 64, 512], "rw_a0": [2, 512], "rw_a2": [2, 64, 512], "rw_g2": [128, 512],
             "rw_kk": [512], "rw_ka": [512], "rw_rk": [512], "rw_ln_w": [512], "rw_ln_b": [512]}
SMALL = {"ml_conv": [3, 1024], "ml_gate_b": [4, 4], "ml_norm": [512], "hg_lb": [2, 512], "hg_norm": [512], "mla_q_norm": [256],
         "mla_w_qb": [256, 768], "mla_kv_norm": [128], "mla_w_kvb": [128, 1024]}


def build_nc():
    nc = bass.Bass("TRN2", target_bir_lowering=False)
    di = lambda n, s: nc.dram_tensor(n, list(s), F32, kind="ExternalInput").ap()
    x = di("x", [8192, D]); ctx = di("ctx", [NCTX, D]); cvec = di("cvec", [2, D])
    ada_w = di("ada_w", [2, D, 6 * D]); ada_b = di("ada_b", [2, 6 * D]); norm_g = di("norm_g", [2, 4, D])
    ow = di("ow", [2, 256, D]); w1c = di("w1c", [2, D, D]); w2c = di("w2c", [2, D, D])
    ev_w = di("ev_w", [D, EV_NC]); od_w = di("od_w", [D, OD_NC])
    prm = {k: di(k, v) for k, v in RW_SHAPES.items()}
    sm = {k: di(k, v) for k, v in SMALL.items()}
    rope_tab = di("rope_tab", [64, 2, 128])
    out = nc.dram_tensor("out", [8192, D], F32, kind="ExternalOutput").ap()
    with contextlib.ExitStack() as st:
        kb = KB(nc, st)
        kb.eps_ap(EPS)
        XS = kb.dram("XS", [T, D]); PT = kb.dram("PT", [3984, T]); Y = kb.dram("Y", [T, D])
        ARi = kb.dram("ARi", [T, D]); ARo = kb.dram("ARo", [T, D])
        modrow = kb.dram("modrow", [2, 4, D])
        OD = [kb.dram("OD%d" % d, [T, 512]) for d in range(2)]
        BN = [kb.dram("BN%d" % d, [512, T]) for d in range(2)]
        xs_buf = [Buf() for _ in range(NT)]; y_buf = [Buf() for _ in range(NT)]; pt_bufs = [Buf() for _ in range(16)]
        ari_b = [Buf() for _ in range(NT)]; aro_b = [Buf() for _ in range(NT)]
        kb.dma(XS[0:NCTX, :], ctx[:, :], w=xs_buf[0:2])
        for i in range(8):
            kb.dma(XS[NCTX + i * 1024:NCTX + (i + 1) * 1024, :], x[i * 1024:(i + 1) * 1024, :], w=xs_buf[2 + i * 8:2 + (i + 1) * 8])
        keep = {k: kb.sb([128, 2, 8]) for k in ("A_pre", "B_pre", "A_mlp", "B_mlp")}
        for l in range(2):
            mb = phase_mod(kb, l, cvec, ada_w, ada_b, norm_g, modrow, keep)
            if l == 0:
                phase_proj(kb, XS, ev_w, EV_NC, EV_BLOCKS, PT, keep, xs_buf, pt_bufs)
                od_b = [[Buf() for _ in range(NCH)] for _ in range(2)]
                phase_mlstm(kb, PT, pt_bufs, sm["ml_conv"], sm["ml_gate_b"], OD, od_b, nh=1)
                phase_gla_merge(kb, OD, od_b, PT, pt_bufs, 1536, AF.Sigmoid, sm["ml_norm"], Y, y_buf, nh=1)
                od2 = [[Buf() for _ in range(NT)] for _ in range(2)]; bn_b = [[Buf() for _ in range(4)] for _ in range(2)]
                phase_rwkv(kb, PT, pt_bufs, prm, OD, od2, BN, bn_b, npair=1)
                phase_rwkv_merge(kb, OD, od2, BN, bn_b, PT, pt_bufs, prm, Y, y_buf, npair=1)
            else:
                phase_proj(kb, XS, od_w, OD_NC, OD_BLOCKS, PT, keep, xs_buf, pt_bufs)
                od_b = [[Buf() for _ in range(NCH)] for _ in range(2)]
                phase_hgrn(kb, PT, pt_bufs, sm["hg_lb"], OD, od_b, nh=1)
                phase_gla_merge(kb, OD, od_b, PT, pt_bufs, 2048, AF.Silu, sm["hg_norm"], Y, y_buf, nh=1)
                phase_mla(kb, PT, pt_bufs, sm["mla_q_norm"], sm["mla_w_qb"], sm["mla_kv_norm"], sm["mla_w_kvb"], rope_tab, Y, y_buf, nh=1)
            last = l == 1
            phase_post_tp(kb, l, XS, Y, out if last else None, ow[l], w1c[l], w2c[l], modrow, keep, xs_buf, y_buf, mb,
                          NCTX if last else 0, ARi, ARo, ari_b, aro_b, GROUPS)
        evs = [b.w for b in xs_buf[2:]]
        for e in ("sync", "gpsimd"):
            kb.P.wait_all(e, evs)
    return nc


def _roll(a, axis_shape, axis, g, base_axis):
    sh = list(a.shape)
    new = sh[:base_axis] + list(axis_shape) + sh[base_axis + 1:]
    return np.ascontiguousarray(np.roll(a.reshape(new), -g, axis=base_axis + axis).reshape(sh))


def kernel(**inputs):
    f = lambda k: np.ascontiguousarray(np.asarray(inputs[k], dtype=np.float32))
    nc = build_nc()
    xs, cs, ctxs, cc = f("x"), f("c"), f("ctx"), f("c_ctx")
    ev, od = f("ev_w_in")[0], f("od_w_in")[0]
    out_w, w1, w2 = f("out_w"), f("mlp_w1"), f("mlp_w2")
    common = {"ada_w": f("ada_w"), "ada_b": f("ada_b"), "norm_g": f("norm_g"), "rope_tab": rope_table(),
              "mla_q_norm": f("mla_q_norm")[0], "mla_kv_norm": f("mla_kv_norm")[0]}
    in_maps = []
    for core in range(8):
        b, g = core // 4, core % 4
        m = dict(common)
        m.update({"x": xs[b], "ctx": ctxs[b], "cvec": np.stack([cs[b], cc])})
        gates = _roll(ev[:, 2048:2064], (4, 4), 1, g, 1)
        s128 = lambda a, o: a[:, o + g * 128:o + (g + 1) * 128]
        m["ev_w"] = np.ascontiguousarray(np.concatenate(
            [s128(ev, 0), s128(ev, 512), s128(ev, 1024), s128(ev, 1536), gates, s128(ev, 2064), s128(ev, 2576), s128(ev, 3088),
             ev[:, 3600:3984]], axis=1))
        m["od_w"] = np.ascontiguousarray(np.concatenate(
            [s128(od, 0), s128(od, 512), s128(od, 1024), s128(od, 1536), s128(od, 2048), od[:, 2560:3008]], axis=1))
        m["ow"] = np.ascontiguousarray(np.stack([np.concatenate([out_w[l][g * 128:(g + 1) * 128], out_w[l][512 + g * 128:512 + (g + 1) * 128]])
                                                 for l in range(2)]))
        m["w1c"] = np.ascontiguousarray(w1[:, :, g * 1024:(g + 1) * 1024])
        m["w2c"] = np.ascontiguousarray(w2[:, g * 1024:(g + 1) * 1024, :])
        m["ml_conv"] = _roll(f("ml_conv")[0], (2, 4, 128), 1, g, 1)
        m["ml_gate_b"] = _roll(f("ml_gate_b")[0], (4,), 0, g, 1)
        m["ml_norm"] = _roll(f("ml_norm")[0], (4, 128), 0, g, 0)
        mu = f("rw_mu")[0]
        m["rw_mu"] = np.ascontiguousarray(np.concatenate([_roll(mu[0:1536], (3, 4, 128), 1, g, 0), mu[1536:]]))
        for k in ("rw_kk", "rw_ka", "rw_rk", "rw_ln_w", "rw_ln_b"):
            m[k] = _roll(f(k)[0], (4, 128), 0, g, 0)
        for k in ("rw_w0", "rw_a0"):
            m[k] = _roll(f(k)[0], (4, 128), 0, g, 1)
        for k in ("rw_w2", "rw_a2"):
            m[k] = _roll(f(k)[0], (4, 128), 0, g, 2)
        m["rw_g2"] = _roll(f("rw_g2")[0], (4, 128), 0, g, 1)
        m["hg_lb"] = _roll(f("hg_lb"), (4, 128), 0, g, 1)
        m["hg_norm"] = _roll(f("hg_norm")[0], (4, 128), 0, g, 0)
        m["mla_w_qb"] = _roll(f("mla_w_qb")[0], (4, 192), 0, g, 1)
        m["mla_w_kvb"] = _roll(f("mla_w_kvb")[0], (4, 256), 0, g, 1)
        in_maps.append(m)
    res = run_bass_kernel_spmd(nc, in_maps, core_ids=list(range(8)))
    return np.stack([res.results[4 * b]["out"] for b in range(2)]).astype(np.float32)
```

```python
import contextlib
from concourse.bass_utils import run_bass_kernel_spmd
import numpy as np
import concourse.bass as bass
import concourse.mybir as mybir

F32 = mybir.dt.float32
BF16 = mybir.dt.bfloat16
AF = mybir.ActivationFunctionType
ALU = mybir.AluOpType
AX = mybir.AxisListType

ENGS = ("tensor", "vector", "scalar", "gpsimd", "sync")
EPOCH = 30000
NDMA = 24
NCC = 4


class Buf:
    __slots__ = ("name", "w", "r")

    def __init__(self, name=""):
        self.name = name
        self.w = None
        self.r = []


class Prog:
    def __init__(self, nc, stack):
        self.nc = nc
        self.stack = stack
        self.ops = {e: [] for e in ENGS}
        self.cnt = {e: 0 for e in ENGS}
        self.known = {e: {} for e in ENGS}
        self.clock = {}
        self.dma_slot = 0
        self.dma_val = [0] * NDMA
        self.cc_slot = 0
        self.cc_val = [0] * NCC
        self.sem_keys = set()
        self.sems = {}
        self.last = {}

    def _need(self, eng, ev, waits):
        key, val = ev
        kn = self.known[eng]
        if kn.get(key, 0) >= val:
            return
        if key[0] == "e":
            for k2, v2 in kn.items():
                if k2[0] == "e" and k2[1] == key[1] and k2[2] > key[2]:
                    return
        waits[key] = max(waits.get(key, 0), val)

    def _merge(self, eng, ev):
        kn = self.known[eng]
        key, val = ev
        if kn.get(key, 0) < val:
            kn[key] = val
        ck = self.clock.get(ev)
        if ck:
            for k, v in ck.items():
                if kn.get(k, 0) < v:
                    kn[k] = v

    def op(self, eng, fn, reads=(), writes=(), dma=False, cc=False):
        deps = []
        for b in reads:
            if b.w is not None:
                deps.append(b.w)
        for b in writes:
            if b.w is not None:
                deps.append(b.w)
            deps.extend(b.r)
        if cc:
            slot = self.cc_slot
            self.cc_slot = (slot + 1) % NCC
            key = ("c", slot)
            if self.cc_val[slot] > 0:
                deps.append((key, self.cc_val[slot]))
            self.cc_val[slot] += 1
            ev = (key, self.cc_val[slot])
            inc = 1
            dma = True
        elif dma:
            slot = self.dma_slot
            self.dma_slot = (slot + 1) % NDMA
            key = ("d", slot)
            if self.dma_val[slot] > 0:
                deps.append((key, 16 * self.dma_val[slot]))
            self.dma_val[slot] += 1
            ev = (key, 16 * self.dma_val[slot])
            inc = 16
        else:
            self.cnt[eng] += 1
            c = self.cnt[eng]
            key = ("e", eng, (c - 1) // EPOCH)
            ev = (key, (c - 1) % EPOCH + 1)
            inc = 1
        waits = {}
        for d in deps:
            self._need(eng, d, waits)
        for k, v in waits.items():
            self._merge(eng, (k, v))
        if not dma:
            ck = {k: v for k, v in self.known[eng].items() if k[0] == "e"}
            self.clock[ev] = ck
        else:
            self.clock[ev] = {k: v for k, v in self.known[eng].items() if k[0] == "e"}
        self.last[eng if not dma else key] = ev
        E = getattr(self.nc, eng)
        for k, v in waits.items():
            E.wait_ge(self._sem(k), v)
        ins = fn(E)
        ins.then_inc(self._sem(key), inc)
        for b in reads:
            b.r.append(ev)
        for b in writes:
            b.w = ev
            b.r = []
        return ev

    def wait_all(self, eng, evs):
        waits = {}
        for ev in evs:
            self._need(eng, ev, waits)
        for k, v in waits.items():
            self._merge(eng, (k, v))
        E = getattr(self.nc, eng)
        for k, v in waits.items():
            E.wait_ge(self._sem(k), v)

    def _sem(self, key):
        if key not in self.sems:
            self.sems[key] = self.stack.enter_context(self.nc.semaphore("s%d" % len(self.sems)))
        return self.sems[key]

    def barrier(self):
        evs = list(self.last.values())
        for e in ENGS:
            self.wait_all(e, evs)


D = 1024
T = 8448
NCTX = 256
NT = T // 128


class Tn:
    __slots__ = ("t", "b")

    def __init__(self, t, b=None):
        self.t = t
        self.b = b if b is not None else Buf()

    def __getitem__(self, k):
        return self.t[k]


class KB:
    def __init__(self, nc, st):
        self.nc = nc
        self.st = st
        self.P = Prog(nc, st)
        self.cur = st
        self.n = 0
        self.rr = 0

    def _nm(self, p):
        self.n += 1
        return "%s%d" % (p, self.n)

    def sb(self, shape, dt=F32):
        return Tn(self.cur.enter_context(self.nc.sbuf_tensor(self._nm("sb"), list(shape), dt)))

    def ps(self, shape, dt=F32):
        return Tn(self.cur.enter_context(self.nc.psum_tensor(self._nm("ps"), list(shape), dt)))

    def dram(self, name, shape, dt=F32, kind="Internal"):
        return self.nc.dram_tensor(name, list(shape), dt, kind=kind).ap()

    @contextlib.contextmanager
    def phase(self):
        prev = self.cur
        with contextlib.ExitStack() as ph:
            self.cur = ph
            yield
            self.P.barrier()
        self.cur = prev

    def op(self, eng, fn, r=(), w=(), dma=False):
        return self.P.op(eng, fn, reads=[x.b if isinstance(x, Tn) else x for x in r],
                         writes=[x.b if isinstance(x, Tn) else x for x in w], dma=dma)

    def V(self, fn, r=(), w=()):
        return self.op("vector", fn, r, w)

    def A(self, fn, r=(), w=()):
        return self.op("scalar", fn, r, w)

    def G(self, fn, r=(), w=()):
        return self.op("gpsimd", fn, r, w)

    def PE(self, fn, r=(), w=()):
        return self.op("tensor", fn, r, w)

    def dma(self, out, in_, r=(), w=(), q=None):
        if q is None:
            q = "gpsimd" if (str(out.space) == "DRAM" and str(in_.space) != "DRAM") else "sync"
            self.rr += 1
        return self.op(q, lambda e: e.dma_start(out=out, in_=in_), r, w, dma=True)

    def allreduce(self, out, in_, groups, r=(), w=()):
        return self.P.op("gpsimd", lambda e: e.collective_compute("AllReduce", ALU.add, replica_groups=groups, ins=[in_], outs=[out]),
                         reads=[x.b if isinstance(x, Tn) else x for x in r], writes=[x.b if isinstance(x, Tn) else x for x in w], cc=True)

    def VA(self, i, fn, r=(), w=()):
        if i % 2 == 0:
            return self.V(lambda e: fn(e, False), r, w)
        return self.A(lambda e: fn(e, True), r, w)

    def identity(self, dt):
        idf = self.sb([128, 128], F32)
        self.G(lambda e: e.memset(idf[:], 0.0), w=[idf])
        self.G(lambda e: e.affine_select(out=idf[:], in_=idf[:], pattern=[[-1, 128]], compare_op=ALU.not_equal,
                                         fill=1.0, base=0, channel_multiplier=1), r=[idf], w=[idf])
        if dt == F32:
            return idf
        idb = self.sb([128, 128], dt)
        self.V(lambda e: e.tensor_copy(out=idb[:], in_=idf[:]), r=[idf], w=[idb])
        return idb

    def load_w_bf16(self, wd, kdim, n, stage, col0=0, grp=512):
        kc = kdim // 128
        wb = self.sb([128, kc, n], BF16)
        src = wd.rearrange("(k p) n -> p k n", p=128)
        for gi, c0 in enumerate(range(0, n, grp)):
            cn = min(grp, n - c0)
            for k0 in range(0, kc, 8):
                kn = min(8, kc - k0)
                stg = stage[self.rr % len(stage)]
                self.dma(stg[:, 0:kn, 0:cn], src[:, k0:k0 + kn, col0 + c0:col0 + c0 + cn], w=[stg])
                self.VA(self.rr, lambda e, a, stg=stg, k0=k0, kn=kn, c0=c0, cn=cn:
                        (e.copy(out=wb[:, k0:k0 + kn, c0:c0 + cn], in_=stg[:, 0:kn, 0:cn]) if a else
                         e.tensor_copy(out=wb[:, k0:k0 + kn, c0:c0 + cn], in_=stg[:, 0:kn, 0:cn])),
                        r=[stg], w=[wb])
        return wb

    def eps_ap(self, eps):
        if not hasattr(self, "_eps"):
            self._eps = {}
        if eps not in self._eps:
            t = Tn(self.st.enter_context(self.nc.sbuf_tensor(self._nm("eps"), [128, 1], F32)))
            self.V(lambda e: e.memset(t[:], eps), w=[t])
            self._eps[eps] = t
        return self._eps[eps][:, 0:1]

    def rstd(self, ss, n, eps):
        self.A(lambda e: e.activation(out=ss[:], in_=ss[:], func=AF.Sqrt, scale=1.0 / n, bias=self.eps_ap(eps)), r=[ss], w=[ss])
        self.V(lambda e: e.reciprocal(out=ss[:], in_=ss[:]), r=[ss], w=[ss])

EPS = 1e-6


def phase_mod(kb, l, cvec, ada_w, ada_b, norm_g, modrow, keep):
    with kb.phase():
        idf = kb.identity(F32)
        cvr = kb.sb([16, 128]); cvT = kb.sb([128, 2, 8])
        kb.dma(cvr[:], cvec.rearrange("r (k p) -> (r k) p", p=128), w=[cvr])
        kb.A(lambda e: e.activation(out=cvr[:], in_=cvr[:], func=AF.Silu), r=[cvr], w=[cvr])
        pt = kb.ps([128, 64])
        kb.PE(lambda e: e.transpose(out=pt[:, 0:16], in_=cvr[:], identity=idf[0:16, 0:16]), r=[cvr, idf], w=[pt])
        kb.V(lambda e: e.tensor_copy(out=cvT[:].rearrange('p r k -> p (r k)'), in_=pt[:, 0:16]), r=[pt], w=[cvT])
        abr = kb.sb([48, 128]); ngr = kb.sb([32, 128]); abT = kb.sb([128, 48]); ngT = kb.sb([128, 32])
        kb.dma(abr[:], ada_b[l].rearrange("(j p) -> j p", p=128), w=[abr])
        kb.dma(ngr[:], norm_g[l].rearrange("i (k p) -> (i k) p", p=128), w=[ngr])
        pt2 = kb.ps([128, 64]); pt3 = kb.ps([128, 64])
        kb.PE(lambda e: e.transpose(out=pt2[:, 0:48], in_=abr[:], identity=idf[0:48, 0:48]), r=[abr, idf], w=[pt2])
        kb.V(lambda e: e.tensor_copy(out=abT[:], in_=pt2[:, 0:48]), r=[pt2], w=[abT])
        kb.PE(lambda e: e.transpose(out=pt3[:, 0:32], in_=ngr[:], identity=idf[0:32, 0:32]), r=[ngr, idf], w=[pt3])
        kb.V(lambda e: e.tensor_copy(out=ngT[:], in_=pt3[:, 0:32]), r=[pt3], w=[ngT])
        pm = kb.ps([128, 48, 2])
        wst = [kb.sb([128, 8, 1024]) for _ in range(2)]
        src = ada_w[l].rearrange("(k p) n -> p k n", p=128)
        for g in range(6):
            ws = wst[g % 2]
            kb.dma(ws[:], src[:, :, g * 1024:(g + 1) * 1024], w=[ws])
            for jj in range(8):
                j = g * 8 + jj
                for k in range(8):
                    kb.PE(lambda e, ws=ws, jj=jj, j=j, k=k: e.matmul(pm[:, j, :], lhsT=ws[:, k, jj * 128:(jj + 1) * 128],
                                                                   rhs=cvT[:, :, k], start=(k == 0), stop=(k == 7)),
                          r=[ws, cvT], w=[pm])
        mod = kb.sb([128, 2, 48])
        for r in range(2):
            kb.V(lambda e, r=r: e.tensor_tensor(out=mod[:, r, :], in0=pm[:, :, r], in1=abT[:], op=ALU.add), r=[pm, abT], w=[mod])
        for nm, gi, sc, sh in (("pre", 0, 1, 0), ("mlp", 2, 4, 3)):
            A_, B_ = keep["A_" + nm], keep["B_" + nm]
            for r in range(2):
                kb.V(lambda e, r=r, A_=A_, sc=sc, gi=gi: e.scalar_tensor_tensor(
                    out=A_[:, r, :], in0=mod[:, r, sc * 8:sc * 8 + 8], scalar=1.0, in1=ngT[:, gi * 8:gi * 8 + 8],
                    op0=ALU.add, op1=ALU.mult), r=[mod, ngT], w=[A_])
                kb.V(lambda e, r=r, B_=B_, sh=sh: e.tensor_copy(out=B_[:, r, :], in_=mod[:, r, sh * 8:sh * 8 + 8]), r=[mod], w=[B_])
        gg = kb.sb([128, 32]); ggr = kb.sb([32, 128])
        for r in range(2):
            for wi, (gi, gs) in enumerate(((1, 2), (3, 5))):
                c0 = (r * 2 + wi) * 8
                kb.V(lambda e, c0=c0, r=r, gi=gi, gs=gs: e.tensor_tensor(out=gg[:, c0:c0 + 8], in0=mod[:, r, gs * 8:gs * 8 + 8],
                                                                      in1=ngT[:, gi * 8:gi * 8 + 8], op=ALU.mult), r=[mod, ngT], w=[gg])
        pt4 = kb.ps([128, 128])
        kb.PE(lambda e: e.transpose(out=pt4[0:32, :], in_=gg[:], identity=idf[:]), r=[gg, idf], w=[pt4])
        kb.V(lambda e: e.tensor_copy(out=ggr[:], in_=pt4[0:32, :]), r=[pt4], w=[ggr])
        mb = Buf()
        kb.dma(modrow[l].rearrange("q (k p) -> (q k) p", p=128), ggr[:], r=[ggr], w=[mb])
        return mb


def norm_T(kb, xt, r, A_, B_, idb, xn, ptr, hT, col0, ss, junk, cnt):
    kb.A(lambda e: e.activation(out=junk[:], in_=xt[:], func=AF.Square, accum_out=ss[:]), r=[xt], w=[junk, ss])
    kb.rstd(ss, D, EPS)
    kb.V(lambda e: e.tensor_scalar(out=xn[:], in0=xt[:], scalar1=ss[:, 0:1], scalar2=None, op0=ALU.mult), r=[xt, ss], w=[xn])
    for k in range(8):
        kb.PE(lambda e, k=k: e.transpose(out=ptr[:, k, :], in_=xn[:, k * 128:(k + 1) * 128], identity=idb[:]), r=[xn, idb], w=[ptr])
    for k in range(8):
        kb.VA(k + cnt, lambda e, a, k=k: (
            e.activation(out=hT[:, k, col0:col0 + 128], in_=ptr[:, k, :], func=AF.Identity, scale=A_[:, r, k:k + 1], bias=B_[:, r, k:k + 1])
            if a else
            e.tensor_scalar(out=hT[:, k, col0:col0 + 128], in0=ptr[:, k, :], scalar1=A_[:, r, k:k + 1], scalar2=B_[:, r, k:k + 1],
                            op0=ALU.mult, op1=ALU.add)), r=[ptr, A_, B_], w=[hT])


def norm_T_gen(kb, xt, r, A_, B_, idb, xn, ptr, hT, col0, ss, junk, cnt):
    kb.A(lambda e: e.activation(out=junk[:], in_=xt[:], func=AF.Square, accum_out=ss[:]), r=[xt], w=[junk, ss])
    yield
    kb.A(lambda e: e.activation(out=ss[:], in_=ss[:], func=AF.Sqrt, scale=1.0 / D, bias=kb.eps_ap(EPS)), r=[ss], w=[ss])
    yield
    kb.V(lambda e: e.reciprocal(out=ss[:], in_=ss[:]), r=[ss], w=[ss])
    kb.V(lambda e: e.tensor_scalar(out=xn[:], in0=xt[:], scalar1=ss[:, 0:1], scalar2=None, op0=ALU.mult), r=[xt, ss], w=[xn])
    yield
    for k in range(8):
        kb.PE(lambda e, k=k: e.transpose(out=ptr[:, k, :], in_=xn[:, k * 128:(k + 1) * 128], identity=idb[:]), r=[xn, idb], w=[ptr])
    yield
    for k in range(8):
        kb.VA(k + cnt, lambda e, a, k=k: (
            e.activation(out=hT[:, k, col0:col0 + 128], in_=ptr[:, k, :], func=AF.Identity, scale=A_[:, r, k:k + 1], bias=B_[:, r, k:k + 1])
            if a else
            e.tensor_scalar(out=hT[:, k, col0:col0 + 128], in0=ptr[:, k, :], scalar1=A_[:, r, k:k + 1], scalar2=B_[:, r, k:k + 1],
                            op0=ALU.mult, op1=ALU.add)), r=[ptr, A_, B_], w=[hT])
    yield


def round_robin(gens):
    live = list(gens)
    while live:
        for g_ in list(live):
            try:
                next(g_)
            except StopIteration:
                live.remove(g_)


def phase_proj(kb, XS, w_in, ncols, blocks, PT, keep, xs_buf, pt_bufs):
    with kb.phase():
        idb = kb.identity(BF16)
        stage = [kb.sb([128, 8, 512]) for _ in range(2)]
        wb = kb.load_w_bf16(w_in, D, ncols, stage)
        xts = [kb.sb([128, D]) for _ in range(4)]
        xns = [kb.sb([128, D], BF16) for _ in range(4)]
        junks = [kb.sb([128, D]) for _ in range(2)]
        sss = [kb.sb([128, 1]) for _ in range(4)]
        ptrs = [kb.ps([128, 8, 128], BF16) for _ in range(4)]
        hTs = [kb.sb([128, 8, 512], BF16) for _ in range(2)]
        pos = [kb.ps([128, 512]) for _ in range(4)]
        obs = [kb.sb([128, 512]) for _ in range(4)]
        ti = 0
        oc = 0
        sts = [(0, 256)] + [(256 + 512 * i, 512) for i in range(16)]
        for si, (t0, tn) in enumerate(sts):
            hT = hTs[si % 2]
            gens = []
            for j in range(tn // 128):
                xt = xts[ti % 4]
                kb.dma(xt[:], XS[t0 + j * 128:t0 + (j + 1) * 128, :], r=[xs_buf[(t0 + j * 128) // 128]], w=[xt])
                gens.append(norm_T_gen(kb, xt, 1 if t0 == 0 else 0, keep["A_pre"], keep["B_pre"], idb, xns[ti % 4], ptrs[ti % 4], hT, j * 128,
                                       sss[ti % 4], junks[ti % 2], ti))
                ti += 1
            round_robin(gens)
            for cb, (wc0, cn, prow) in enumerate(blocks):
                po = pos[oc % 4]; ob = obs[oc % 4]
                for k in range(8):
                    kb.PE(lambda e, po=po, wc0=wc0, cn=cn, k=k, hT=hT, tn=tn: e.matmul(
                        po[0:cn, 0:tn], lhsT=wb[:, k, wc0:wc0 + cn], rhs=hT[:, k, 0:tn], start=(k == 0), stop=(k == 7)),
                        r=[wb, hT], w=[po])
                kb.VA(oc, lambda e, a, po=po, ob=ob, cn=cn, tn=tn: (e.copy(out=ob[0:cn, 0:tn], in_=po[0:cn, 0:tn]) if a else
                                                                     e.tensor_copy(out=ob[0:cn, 0:tn], in_=po[0:cn, 0:tn])), r=[po], w=[ob])
                kb.dma(PT[prow:prow + cn, t0:t0 + tn], ob[0:cn, 0:tn], r=[ob], w=[pt_bufs[cb]])
                oc += 1


def phase_post(kb, l, XS, Y, OUT, out_w, w1, w2, modrow, keep, xs_buf, y_buf, mod_buf, t_start, t_end=T):
    def gam(wi):
        GAM = {}
        for r in range(2):
            g = kb.sb([128, D])
            q = r * 2 + wi
            kb.dma(g[:], modrow[l][q:q + 1, :].partition_broadcast(128), r=[mod_buf], w=[g])
            GAM[r] = g
        return GAM

    with kb.phase():
        idb = kb.identity(BF16)
        stage = [kb.sb([128, 8, 512]) for _ in range(2)]
        owb = kb.load_w_bf16(out_w, D, D, stage)
        GA = gam(0)
        yts = [kb.sb([128, D]) for _ in range(2)]
        ybs = [kb.sb([128, D], BF16) for _ in range(2)]
        yTs = [kb.sb([128, 8, 128], BF16) for _ in range(2)]
        xts = [kb.sb([128, D]) for _ in range(4)]
        x1s = [kb.sb([128, D]) for _ in range(3)]
        tmps = [kb.sb([128, D]) for _ in range(2)]; junk = kb.sb([128, D])
        sss = [kb.sb([128, 1]) for _ in range(8)]
        ptrs = [kb.ps([128, 8, 128], BF16) for _ in range(4)]
        pOs = [[kb.ps([128, 512]) for _ in range(2)] for _ in range(2)]
        cnt = 0
        for tt in range(t_start, t_end, 128):
            r = 1 if tt < NCTX else 0
            yt = yts[cnt % 2]; yb = ybs[cnt % 2]; yT = yTs[cnt % 2]; xt = xts[cnt % 3]; x1 = x1s[cnt % 3]
            ptr = ptrs[cnt % 2]; pO = pOs[cnt % 2]; tmp = tmps[cnt % 2]
            kb.dma(yt[:], Y[tt:tt + 128, :], r=[y_buf[tt // 128]], w=[yt])
            kb.dma(xt[:], XS[tt:tt + 128, :], r=[xs_buf[tt // 128]], w=[xt])
            kb.G(lambda e, yb=yb, yt=yt: e.tensor_copy(out=yb[:], in_=yt[:]), r=[yt], w=[yb])
            for k in range(8):
                kb.PE(lambda e, k=k, yb=yb, ptr=ptr: e.transpose(out=ptr[:, k, :], in_=yb[:, k * 128:(k + 1) * 128], identity=idb[:]),
                      r=[yb, idb], w=[ptr])
            kb.V(lambda e, yT=yT, ptr=ptr: e.tensor_copy(out=yT[:], in_=ptr[:]), r=[ptr], w=[yT])
            for h in range(2):
                for k in range(8):
                    kb.PE(lambda e, h=h, k=k, yT=yT, pO=pO: e.matmul(pO[h][:], lhsT=yT[:, k, :], rhs=owb[:, k, h * 512:(h + 1) * 512],
                                                                   start=(k == 0), stop=(k == 7)), r=[yT, owb], w=[pO[h]])
            ss = sss[(cnt * 2) % 8]; ssb = sss[(cnt * 2 + 1) % 8]
            kb.A(lambda e, ss=ss, pO=pO: e.activation(out=junk[:, 0:512], in_=pO[0][:], func=AF.Square, accum_out=ss[:]), r=[pO[0]], w=[junk, ss])
            kb.A(lambda e, ssb=ssb, pO=pO: e.activation(out=junk[:, 512:1024], in_=pO[1][:], func=AF.Square, accum_out=ssb[:]), r=[pO[1]], w=[junk, ssb])
            kb.V(lambda e, ss=ss, ssb=ssb: e.tensor_tensor(out=ss[:], in0=ss[:], in1=ssb[:], op=ALU.add), r=[ss, ssb], w=[ss])
            kb.rstd(ss, D, EPS)
            ga = GA[r]
            for h in range(2):
                kb.V(lambda e, h=h, ss=ss, ga=ga, pO=pO, tmp=tmp: e.scalar_tensor_tensor(
                    out=tmp[:, h * 512:(h + 1) * 512], in0=pO[h][:], scalar=ss[:, 0:1], in1=ga[:, h * 512:(h + 1) * 512],
                    op0=ALU.mult, op1=ALU.mult), r=[pO[h], ss, ga], w=[tmp])
            kb.G(lambda e, x1=x1, xt=xt, tmp=tmp: e.tensor_tensor(out=x1[:], in0=xt[:], in1=tmp[:], op=ALU.add), r=[xt, tmp], w=[x1])
            kb.dma(XS[tt:tt + 128, :], x1[:], r=[x1], w=[xs_buf[tt // 128]])
            cnt += 1

    with kb.phase():
        idb = kb.identity(BF16)
        stage = [kb.sb([128, 8, 128]) for _ in range(2)]
        w1b = kb.load_w_bf16(w1, D, 4 * D, stage, grp=128)
        w2b = kb.load_w_bf16(w2, 4 * D, D, stage, grp=128)
        GM = gam(1)
        x1s = [kb.sb([128, D]) for _ in range(4)]
        xns = [kb.sb([128, D], BF16) for _ in range(4)]
        tmps = [kb.sb([128, D]) for _ in range(2)]; junk = kb.sb([128, D])
        sss = [kb.sb([128, 1]) for _ in range(8)]
        ptrs = [kb.ps([128, 8, 128], BF16) for _ in range(4)]
        pU = [kb.ps([128, 256]) for _ in range(2)]
        p2s = [[kb.ps([128, 512]) for _ in range(2)] for _ in range(2)]
        hTs = [kb.sb([128, 8, 256], BF16) for _ in range(2)]
        UT = kb.sb([128, 32, 256], BF16)
        ur = [kb.sb([128, 256]) for _ in range(2)]
        sc = 0
        cnt = 0
        for t0 in range(t_start, t_end, 256):
            r = 1 if t0 < NCTX else 0
            hT = hTs[sc % 2]
            x1l = []
            for j in range(2):
                tt = t0 + j * 128
                x1 = x1s[cnt % 4]
                kb.dma(x1[:], XS[tt:tt + 128, :], r=[xs_buf[tt // 128]], w=[x1])
                norm_T(kb, x1, r, keep["A_mlp"], keep["B_mlp"], idb, xns[cnt % 2], ptrs[cnt % 2], hT, j * 128, sss[cnt % 8], junk, cnt)
                x1l.append(x1)
                cnt += 1
            for fb in range(32):
                pu = pU[fb % 2]; u = ur[fb % 2]
                for k in range(8):
                    kb.PE(lambda e, pu=pu, fb=fb, k=k, hT=hT: e.matmul(pu[:], lhsT=w1b[:, k, fb * 128:(fb + 1) * 128], rhs=hT[:, k, :],
                                                                     start=(k == 0), stop=(k == 7)), r=[w1b, hT], w=[pu])
                kb.A(lambda e, pu=pu, u=u: e.activation(out=u[:], in_=pu[:], func=AF.Relu), r=[pu], w=[u])
                kb.G(lambda e, u=u, fb=fb: e.tensor_tensor(out=UT[:, fb, :], in0=u[:], in1=u[:], op=ALU.mult), r=[u], w=[UT])
            for j in range(2):
                tt = t0 + j * 128
                x1 = x1l[j]; p2 = p2s[j]; tmp = tmps[j]
                for h in range(2):
                    for fb in range(32):
                        kb.PE(lambda e, h=h, fb=fb, j=j, p2=p2: e.matmul(p2[h][:], lhsT=UT[:, fb, j * 128:(j + 1) * 128],
                                                                       rhs=w2b[:, fb, h * 512:(h + 1) * 512],
                                                                       start=(fb == 0), stop=(fb == 31)), r=[UT, w2b], w=[p2[h]])
                ss = sss[(sc * 4 + j * 2) % 8]; ssb = sss[(sc * 4 + j * 2 + 1) % 8]
                kb.A(lambda e, ss=ss, p2=p2: e.activation(out=junk[:, 0:512], in_=p2[0][:], func=AF.Square, accum_out=ss[:]), r=[p2[0]], w=[junk, ss])
                kb.A(lambda e, ssb=ssb, p2=p2: e.activation(out=junk[:, 512:1024], in_=p2[1][:], func=AF.Square, accum_out=ssb[:]), r=[p2[1]], w=[junk, ssb])
                kb.V(lambda e, ss=ss, ssb=ssb: e.tensor_tensor(out=ss[:], in0=ss[:], in1=ssb[:], op=ALU.add), r=[ss, ssb], w=[ss])
                kb.rstd(ss, D, EPS)
                gm = GM[r]
                for h in range(2):
                    kb.V(lambda e, h=h, ss=ss, gm=gm, p2=p2, tmp=tmp: e.scalar_tensor_tensor(
                        out=tmp[:, h * 512:(h + 1) * 512], in0=p2[h][:], scalar=ss[:, 0:1], in1=gm[:, h * 512:(h + 1) * 512],
                        op0=ALU.mult, op1=ALU.mult), r=[p2[h], ss, gm], w=[tmp])
                kb.G(lambda e, x1=x1, tmp=tmp: e.tensor_tensor(out=x1[:], in0=x1[:], in1=tmp[:], op=ALU.add), r=[x1, tmp], w=[x1])
                if OUT is not None:
                    kb.dma(OUT[tt - NCTX:tt - NCTX + 128, :], x1[:], r=[x1], w=[xs_buf[tt // 128]])
                else:
                    kb.dma(XS[tt:tt + 128, :], x1[:], r=[x1], w=[xs_buf[tt // 128]])
            sc += 1


ARCH = 1024


def allreduce_rows(kb, ARi, ARo, ari_b, aro_b, t_start, groups):
    for r0 in range(t_start, T, ARCH):
        r1 = min(T, r0 + ARCH)
        kb.allreduce(ARo[r0:r1, :], ARi[r0:r1, :], groups, r=ari_b[r0 // 128:r1 // 128], w=aro_b[r0 // 128:r1 // 128])


def phase_post_tp(kb, l, XS, Y, OUT, ow, w1c, w2c, modrow, keep, xs_buf, y_buf, mod_buf, t_start, ARi, ARo, ari_b, aro_b, groups):
    def gam(wi):
        GAM = {}
        for r in range(2):
            g = kb.sb([128, D])
            q = r * 2 + wi
            kb.dma(g[:], modrow[l][q:q + 1, :].partition_broadcast(128), r=[mod_buf], w=[g])
            GAM[r] = g
        return GAM

    with kb.phase():
        idb = kb.identity(BF16)
        stage = [kb.sb([128, 8, 128]) for _ in range(2)]
        owb = kb.load_w_bf16(ow, 256, D, stage, grp=128)
        w1b = kb.load_w_bf16(w1c, D, D, stage, grp=128)
        w2b = kb.load_w_bf16(w2c, D, D, stage, grp=128)
        GA = gam(0); GM = gam(1)
        yts = [kb.sb([128, 256]) for _ in range(2)]; ybs = [kb.sb([128, 256], BF16) for _ in range(2)]
        yTs = [kb.sb([128, 2, 128], BF16) for _ in range(2)]
        ots = [kb.sb([128, D]) for _ in range(2)]
        ptrA = kb.ps([128, 2, 128], BF16)
        pO = [kb.ps([128, 512])] * 2
        Os = [kb.sb([128, D]) for _ in range(2)]; xts = [kb.sb([128, D]) for _ in range(2)]
        x1s = [kb.sb([128, D]) for _ in range(3)]
        xns = [kb.sb([128, D], BF16) for _ in range(2)]
        tmps = [kb.sb([128, D]) for _ in range(2)]; junks = [kb.sb([128, D]) for _ in range(2)]
        sss = [kb.sb([128, 1]) for _ in range(8)]
        ptrB = [kb.ps([128, 8, 128], BF16) for _ in range(2)]
        pU = [kb.ps([128, 512]) for _ in range(2)]
        p2 = [kb.ps([128, 512]) for _ in range(2)]
        hTs = [kb.sb([128, 8, 512], BF16) for _ in range(2)]
        UT = [kb.sb([128, 8, 512], BF16) for _ in range(2)]
        ur = [kb.sb([128, 512]) for _ in range(2)]
        mts = [kb.sb([128, D]) for _ in range(2)]
        Ms = [kb.sb([128, D]) for _ in range(2)]; xcs = [kb.sb([128, D]) for _ in range(2)]
        tmpc = [kb.sb([128, D]) for _ in range(2)]
        ssc = [kb.sb([128, 1]) for _ in range(4)]
        ca = [0]; cb = [0]; cc_ = [0]

        def A_tile(tt):
            cnt = ca[0]; ca[0] += 1
            yt = yts[cnt % 2]; yb = ybs[cnt % 2]; yT = yTs[cnt % 2]; ot = ots[cnt % 2]
            kb.dma(yt[:, 0:128], Y[tt:tt + 128, 0:128], r=[y_buf[tt // 128]], w=[yt])
            kb.dma(yt[:, 128:256], Y[tt:tt + 128, 512:640], r=[y_buf[tt // 128]], w=[yt])
            yield
            kb.G(lambda e: e.tensor_copy(out=yb[:], in_=yt[:]), r=[yt], w=[yb])
            yield
            for k in range(2):
                kb.PE(lambda e, k=k: e.transpose(out=ptrA[:, k, :], in_=yb[:, k * 128:(k + 1) * 128], identity=idb[:]), r=[yb, idb], w=[ptrA])
            kb.V(lambda e: e.tensor_copy(out=yT[:], in_=ptrA[:]), r=[ptrA], w=[yT])
            yield
            for h in range(2):
                for k in range(2):
                    kb.PE(lambda e, h=h, k=k: e.matmul(pO[h][:], lhsT=yT[:, k, :], rhs=owb[:, k, h * 512:(h + 1) * 512],
                                                       start=(k == 0), stop=(k == 1)), r=[yT, owb], w=[pO[h]])
                if h == 0:
                    kb.A(lambda e: e.copy(out=ot[:, 0:512], in_=pO[0][:]), r=[pO[0]], w=[ot])
                else:
                    kb.V(lambda e: e.tensor_copy(out=ot[:, 512:1024], in_=pO[1][:]), r=[pO[1]], w=[ot])
            kb.dma(ARi[tt:tt + 128, :], ot[:], r=[ot], w=[ari_b[tt // 128]])
            yield

        def B_tile(sc, t0, j, hT, cnt):
            tt = t0 + j * 128
            r = 1 if tt < NCTX else 0
            O = Os[cnt % 2]; xt = xts[cnt % 2]; x1 = x1s[cnt % 3]; tmp = tmps[cnt % 2]
            ss = sss[(cnt * 2) % 8]; ss2 = sss[(cnt * 2 + 1) % 8]; xn = xns[cnt % 2]; ptr = ptrB[cnt % 2]
            jk = junks[cnt % 2]
            kb.dma(O[:], ARo[tt:tt + 128, :], r=[aro_b[tt // 128]], w=[O])
            kb.dma(xt[:], XS[tt:tt + 128, :], r=[xs_buf[tt // 128]], w=[xt])
            yield
            kb.A(lambda e: e.activation(out=jk[:], in_=O[:], func=AF.Square, accum_out=ss[:]), r=[O], w=[jk, ss])
            yield
            kb.A(lambda e: e.activation(out=ss[:], in_=ss[:], func=AF.Sqrt, scale=1.0 / D, bias=kb.eps_ap(EPS)), r=[ss], w=[ss])
            yield
            kb.V(lambda e: e.reciprocal(out=ss[:], in_=ss[:]), r=[ss], w=[ss])
            ga = GA[r]
            kb.V(lambda e: e.scalar_tensor_tensor(out=tmp[:], in0=O[:], scalar=ss[:, 0:1], in1=ga[:], op0=ALU.mult, op1=ALU.mult),
                 r=[O, ss, ga], w=[tmp])
            yield
            kb.V(lambda e: e.tensor_tensor(out=x1[:], in0=xt[:], in1=tmp[:], op=ALU.add), r=[xt, tmp], w=[x1])
            kb.dma(XS[tt:tt + 128, :], x1[:], r=[x1], w=[xs_buf[tt // 128]])
            yield
            kb.A(lambda e: e.activation(out=jk[:], in_=x1[:], func=AF.Square, accum_out=ss2[:]), r=[x1], w=[jk, ss2])
            yield
            kb.A(lambda e: e.activation(out=ss2[:], in_=ss2[:], func=AF.Sqrt, scale=1.0 / D, bias=kb.eps_ap(EPS)), r=[ss2], w=[ss2])
            yield
            kb.V(lambda e: e.reciprocal(out=ss2[:], in_=ss2[:]), r=[ss2], w=[ss2])
            kb.V(lambda e: e.tensor_scalar(out=xn[:], in0=x1[:], scalar1=ss2[:, 0:1], scalar2=None, op0=ALU.mult), r=[x1, ss2], w=[xn])
            yield
            for k in range(8):
                kb.PE(lambda e, k=k: e.transpose(out=ptr[:, k, :], in_=xn[:, k * 128:(k + 1) * 128], identity=idb[:]), r=[xn, idb], w=[ptr])
            yield
            A_, B_ = keep["A_mlp"], keep["B_mlp"]
            col0 = j * 128
            for k in range(8):
                kb.VA(k + cnt, lambda e, a, k=k: (
                    e.activation(out=hT[:, k, col0:col0 + 128], in_=ptr[:, k, :], func=AF.Identity, scale=A_[:, r, k:k + 1], bias=B_[:, r, k:k + 1])
                    if a else
                    e.tensor_scalar(out=hT[:, k, col0:col0 + 128], in0=ptr[:, k, :], scalar1=A_[:, r, k:k + 1], scalar2=B_[:, r, k:k + 1],
                                    op0=ALU.mult, op1=ALU.add)), r=[ptr, A_, B_], w=[hT])
            yield

        def B_super(t0, tn):
            sc = cb[0]; cb[0] += 1
            hT = hTs[sc % 2]; ut = UT[sc % 2]
            nt = tn // 128
            for j0 in range(0, nt, 2):
                gens = [B_tile(sc, t0, j, hT, sc * 4 + j) for j in range(j0, min(nt, j0 + 2))]
                live = list(gens)
                while live:
                    for g_ in list(live):
                        try:
                            next(g_)
                        except StopIteration:
                            live.remove(g_)
            for fb in range(8):
                pu = pU[fb % 2]; u = ur[fb % 2]
                for k in range(8):
                    kb.PE(lambda e, pu=pu, fb=fb, k=k: e.matmul(pu[:, 0:tn], lhsT=w1b[:, k, fb * 128:(fb + 1) * 128], rhs=hT[:, k, 0:tn],
                                                              start=(k == 0), stop=(k == 7)), r=[w1b, hT], w=[pu])
                kb.A(lambda e, pu=pu, u=u: e.activation(out=u[:, 0:tn], in_=pu[:, 0:tn], func=AF.Relu), r=[pu], w=[u])
                kb.G(lambda e, u=u, fb=fb: e.tensor_tensor(out=ut[:, fb, 0:tn], in0=u[:, 0:tn], in1=u[:, 0:tn], op=ALU.mult), r=[u], w=[ut])
            for j in range(nt):
                tt = t0 + j * 128
                mt = mts[j % 2]
                for h in range(2):
                    for fb in range(8):
                        kb.PE(lambda e, h=h, fb=fb, j=j: e.matmul(p2[h][:], lhsT=ut[:, fb, j * 128:(j + 1) * 128], rhs=w2b[:, fb, h * 512:(h + 1) * 512],
                                                                start=(fb == 0), stop=(fb == 7)), r=[ut, w2b], w=[p2[h]])
                kb.A(lambda e, mt=mt: e.copy(out=mt[:, 0:512], in_=p2[0][:]), r=[p2[0]], w=[mt])
                kb.V(lambda e, mt=mt: e.tensor_copy(out=mt[:, 512:1024], in_=p2[1][:]), r=[p2[1]], w=[mt])
                kb.dma(ARi[tt:tt + 128, :], mt[:], r=[mt], w=[ari_b[tt // 128]])

        def C_tile(tt):
            cnt = cc_[0]; cc_[0] += 1
            r = 1 if tt < NCTX else 0
            M = Ms[cnt % 2]; x1 = xcs[cnt % 2]; tmp = tmpc[cnt % 2]; ss = ssc[cnt % 4]
            kb.dma(M[:], ARo[tt:tt + 128, :], r=[aro_b[tt // 128]], w=[M])
            kb.dma(x1[:], XS[tt:tt + 128, :], r=[xs_buf[tt // 128]], w=[x1])
            yield
            kb.A(lambda e: e.activation(out=tmp[:], in_=M[:], func=AF.Square, accum_out=ss[:]), r=[M], w=[tmp, ss])
            yield
            kb.A(lambda e: e.activation(out=ss[:], in_=ss[:], func=AF.Sqrt, scale=1.0 / D, bias=kb.eps_ap(EPS)), r=[ss], w=[ss])
            yield
            kb.V(lambda e: e.reciprocal(out=ss[:], in_=ss[:]), r=[ss], w=[ss])
            gm = GM[r]
            kb.V(lambda e: e.scalar_tensor_tensor(out=tmp[:], in0=M[:], scalar=ss[:, 0:1], in1=gm[:], op0=ALU.mult, op1=ALU.mult),
                 r=[M, ss, gm], w=[tmp])
            yield
            kb.V(lambda e: e.tensor_tensor(out=x1[:], in0=x1[:], in1=tmp[:], op=ALU.add), r=[x1, tmp], w=[x1])
            if OUT is not None:
                kb.dma(OUT[tt - NCTX:tt - NCTX + 128, :], x1[:], r=[x1], w=[xs_buf[tt // 128]])
            else:
                kb.dma(XS[tt:tt + 128, :], x1[:], r=[x1], w=[xs_buf[tt // 128]])
            yield

        chunks = [(r0, min(T, r0 + ARCH)) for r0 in range(t_start, T, ARCH)]

        def AR(ci):
            r0, r1 = chunks[ci]
            kb.allreduce(ARo[r0:r1, :], ARi[r0:r1, :], groups, r=ari_b[r0 // 128:r1 // 128], w=aro_b[r0 // 128:r1 // 128])

        def stA(ci):
            tts = list(range(chunks[ci][0], chunks[ci][1], 128))
            for i in range(0, len(tts), 2):
                round_robin([A_tile(tt) for tt in tts[i:i + 2]])
            AR(ci)

        def stB(ci):
            for t0 in range(chunks[ci][0], chunks[ci][1], 512):
                B_super(t0, min(512, chunks[ci][1] - t0))
            AR(ci)

        def stC(ci):
            tts = list(range(chunks[ci][0], chunks[ci][1], 128))
            for i in range(0, len(tts), 2):
                round_robin([C_tile(tt) for tt in tts[i:i + 2]])

        n = len(chunks)
        for step in range(n + 3):
            if step < n:
                stA(step)
            if 0 <= step - 2 < n:
                stB(step - 2)
            if 0 <= step - 3 < n:
                stC(step - 3)

NCH = T // 64
ORDER_FW = list(range(NCH))
ORDER_BW = [3, 2, 1, 0] + list(range(NCH - 1, 3, -1))


def flat(t):
    return t[:].rearrange("p c l -> p (c l)")


def make_masks(kb):
    ones = kb.sb([64, 64]); mf = kb.sb([64, 64]); mb = kb.sb([64, 64])
    kb.G(lambda e: e.memset(ones[:], 1.0), w=[ones])
    kb.G(lambda e: e.affine_select(out=mf[:], in_=ones[:], pattern=[[1, 64]], compare_op=ALU.is_ge, fill=0.0, base=0,
                                   channel_multiplier=-1), r=[ones], w=[mf])
    kb.G(lambda e: e.affine_select(out=mb[:], in_=ones[:], pattern=[[-1, 64]], compare_op=ALU.is_ge, fill=0.0, base=0,
                                   channel_multiplier=1), r=[ones], w=[mb])
    return mf, mb


def chunk_cumsum(kb, A, B, rev):
    src, dst = A, B
    for s in (1, 2, 4, 8, 16, 32):
        if not rev:
            kb.V(lambda e, s=s, src=src, dst=dst: e.tensor_tensor(out=dst[:, :, s:], in0=src[:, :, s:], in1=src[:, :, :64 - s], op=ALU.add),
                 r=[src], w=[dst])
            kb.G(lambda e, s=s, src=src, dst=dst: e.tensor_copy(out=dst[:, :, :s], in_=src[:, :, :s]), r=[src], w=[dst])
        else:
            kb.V(lambda e, s=s, src=src, dst=dst: e.tensor_tensor(out=dst[:, :, :64 - s], in0=src[:, :, :64 - s], in1=src[:, :, s:], op=ALU.add),
                 r=[src], w=[dst])
            kb.G(lambda e, s=s, src=src, dst=dst: e.tensor_copy(out=dst[:, :, 64 - s:], in_=src[:, :, 64 - s:]), r=[src], w=[dst])
        src, dst = dst, src
    assert src is A


HC = NCH // 3
HT = HC * 64


def gla_engine(kb, tl, load, Vp, NV, rev, mask, idb, OD, od_bufs, col0, norm_den):
    A, B, C, Cx, QT, KT, ed, Cf, Cb = (tl[k] for k in ("A", "B", "C", "Cx", "QT", "KT", "ed", "Cf", "Cb"))
    pT = tl["pT"]; pG = tl["pG"]; pN = tl["pN"]; pC = tl["pC"]
    kb.V(lambda e: e.memset(Cf[:], 0.0), w=[Cf])
    kb.G(lambda e: e.memset(Cb[:], 0.0), w=[Cb])
    order = ORDER_BW if rev else ORDER_FW
    runs = []
    for c in order:
        hh = (c // HC) * HC
        if not runs or runs[-1][0] != hh:
            runs.append((hh, []))
        runs[-1][1].append(c)
    i = 0
    for c0, clist in runs:
        load("lf", A, Cx, c0)
        chunk_cumsum(kb, A, B, rev)
        pos = 0 if rev else 63
        kb.A(lambda e: e.activation(out=ed[:], in_=A[:, :, pos], func=AF.Exp), r=[A], w=[ed])
        yield
        load("q", B, Cx, c0)
        kb.A(lambda e: e.activation(out=flat(C), in_=flat(A), func=AF.Exp), r=[A], w=[C])
        kb.V(lambda e: e.tensor_tensor(out=flat(QT), in0=flat(B), in1=flat(C), op=ALU.mult), r=[B, C], w=[QT])
        yield
        load("k", B, Cx, c0)
        if load("ig", C, None, c0):
            kb.V(lambda e: e.tensor_tensor(out=flat(C), in0=flat(C), in1=flat(A), op=ALU.subtract), r=[C, A], w=[C])
            kb.A(lambda e: e.activation(out=flat(C), in_=flat(C), func=AF.Exp), r=[C], w=[C])
        else:
            kb.A(lambda e: e.activation(out=flat(C), in_=flat(A), func=AF.Exp, scale=-1.0), r=[A], w=[C])
        kb.V(lambda e: e.tensor_tensor(out=flat(KT), in0=flat(B), in1=flat(C), op=ALU.mult), r=[B, C], w=[KT])
        yield
        for c in clist:
            lc = c - c0
            ST = tl["ST"][i % 2]; ob = tl["ob"][i % 3]; Kt = tl["Kt"][i % 2]; dn = tl["dn"][i % 4]
            i += 1
            kb.PE(lambda e: e.matmul(pG[:], lhsT=KT[:, lc, :], rhs=QT[:, lc, :], start=True, stop=True), r=[KT, QT], w=[pG])
            kb.PE(lambda e: e.transpose(out=pT[:], in_=KT[:, lc, :], identity=idb[:]), r=[KT, idb], w=[pT])
            kb.A(lambda e: e.copy(out=Kt[:], in_=pT[:]), r=[pT], w=[Kt])
            yield
            kb.V(lambda e: e.tensor_tensor(out=ST[:], in0=pG[:], in1=mask[:], op=ALU.mult), r=[pG, mask], w=[ST])
            yield
            kb.PE(lambda e: e.matmul(pN[:, 0:NV], lhsT=ST[:], rhs=Vp[:, c, :], start=True, stop=False), r=[ST, Vp], w=[pN])
            kb.PE(lambda e: e.matmul(pN[:, 0:NV], lhsT=QT[:, lc, :], rhs=Cb[:, 0:NV], start=False, stop=True), r=[QT, Cb, pN], w=[pN])
            kb.PE(lambda e: e.matmul(pC[:, 0:NV], lhsT=Kt[:], rhs=Vp[:, c, :], start=True, stop=True), r=[Kt, Vp], w=[pC])
            kb.G(lambda e: e.tensor_scalar(out=Cf[:], in0=Cf[:], scalar1=ed[:, lc:lc + 1], scalar2=None, op0=ALU.mult), r=[Cf, ed], w=[Cf])
            yield
            if norm_den:
                kb.A(lambda e: e.activation(out=dn[:], in_=pN[:, 128:129], func=AF.Abs), r=[pN], w=[dn])
            else:
                kb.A(lambda e: e.copy(out=ob[:], in_=pN[:, 0:128]), r=[pN], w=[ob])
            kb.V(lambda e: e.scalar_tensor_tensor(out=Cf[:, 0:NV], in0=pC[:, 0:NV], scalar=ed[:, lc:lc + 1], in1=Cf[:, 0:NV],
                                                  op0=ALU.mult, op1=ALU.add), r=[pC, ed, Cf], w=[Cf])
            yield
            if norm_den:
                kb.V(lambda e: e.tensor_scalar(out=dn[:], in0=dn[:], scalar1=1.0, scalar2=None, op0=ALU.max), r=[dn], w=[dn])
                kb.V(lambda e: e.reciprocal(out=dn[:], in_=dn[:]), r=[dn], w=[dn])
            kb.A(lambda e: e.copy(out=Cb[:], in_=Cf[:]), r=[Cf], w=[Cb])
            yield
            if norm_den:
                kb.A(lambda e: e.activation(out=ob[:], in_=pN[:, 0:128], func=AF.Identity, scale=dn[:, 0:1]), r=[pN, dn], w=[ob])
            kb.dma(OD[c * 64:(c + 1) * 64, col0:col0 + 128], ob[:], r=[ob], w=[od_bufs[c]])
            yield


def gla_tiles(kb, pT):
    tl = {}
    for k in ("A", "B"):
        tl[k] = kb.sb([128, HC, 64])
    tl["Cx"] = kb.sb([128, HT + 2])
    tl["C"] = Tn(tl["Cx"][:, 0:HT].rearrange("p (c l) -> p c l", l=64), tl["Cx"].b)
    tl["QT"] = kb.sb([128, HC, 64], BF16); tl["KT"] = kb.sb([128, HC, 64], BF16)
    tl["ed"] = kb.sb([128, HC]); tl["Cf"] = kb.sb([128, 132]); tl["Cb"] = kb.sb([128, 132], BF16)
    tl["pG"] = kb.ps([64, 64])
    tl["ST"] = [kb.sb([64, 64], BF16) for _ in range(2)]
    tl["pN"] = kb.ps([64, 132])
    tl["ob"] = [kb.sb([64, 128]) for _ in range(3)]
    tl["pT"] = pT
    tl["Kt"] = [kb.sb([64, 128], BF16) for _ in range(2)]
    tl["pC"] = kb.ps([128, 132])
    tl["dn"] = [kb.sb([64, 1]) for _ in range(4)]
    return tl


def run_interleaved(gens):
    import os
    if os.environ.get("SEQ"):
        for g_ in gens:
            for _ in g_:
                pass
        return
    live = list(gens)
    while live:
        for g_ in list(live):
            try:
                next(g_)
            except StopIteration:
                live.remove(g_)


def build_Vp(kb, PT, row0, Vp, NV, idf, pt_bufs, stg, pV):
    if NV == 129:
        kb.G(lambda e: e.memset(Vp[:, :, 128:129], 1.0), w=[Vp])
    for g in range(T // 512 + 1):
        t0 = g * 512
        tn = min(512, T - t0)
        s = stg[g % 2]
        kb.dma(s[:, 0:tn], PT[row0:row0 + 128, t0:t0 + tn], r=pt_bufs, w=[s])
        for j in range(tn // 64):
            c = (t0 + j * 64) // 64
            p = pV[c % 2]
            kb.PE(lambda e, s=s, j=j, p=p: e.transpose(out=p[:], in_=s[:, j * 64:(j + 1) * 64], identity=idf[:]), r=[s, idf], w=[p])
            kb.VA(c, lambda e, a, c=c, p=p: (e.copy(out=Vp[:, c, 0:128], in_=p[:]) if a else e.tensor_copy(out=Vp[:, c, 0:128], in_=p[:])),
                  r=[p], w=[Vp])


def load_rows_T(kb, src2d, nrows, idf, pt, out, wt):
    r = kb.sb([nrows, 128])
    kb.dma(r[:], src2d, w=[r])
    kb.PE(lambda e: e.transpose(out=pt[:, 0:nrows], in_=r[:], identity=idf[0:nrows, 0:nrows]), r=[r, idf], w=[pt])
    kb.V(lambda e: e.tensor_copy(out=out, in_=pt[:, 0:nrows]), r=[pt], w=[wt])


def conv3(kb, dst, Cx, PT, pt_bufs, row0, w, j0, c0):
    t0 = c0 * 64; t1 = t0 + HT
    la = max(t0 - 1, 0); lb = min(t1 + 1, T)
    kb.dma(Cx[:, la - t0 + 1:lb - t0 + 1], PT[row0:row0 + 128, la:lb], r=pt_bufs, w=[Cx])
    d = flat(dst)
    kb.V(lambda e: e.tensor_scalar(out=d, in0=Cx[:, 1:HT + 1], scalar1=w[:, 1, j0:j0 + 1], scalar2=None, op0=ALU.mult), r=[Cx, w], w=[dst])
    for sa, sb in ((0, NCTX), (NCTX, T)):
        a = max(sa, t0); b = min(sb, t1)
        if a >= b:
            continue
        lo = a + 1 if a == sa else a
        hi = b - 1 if b == sb else b
        kb.V(lambda e, lo=lo, b=b: e.scalar_tensor_tensor(out=d[:, lo - t0:b - t0], in0=Cx[:, lo - t0:b - t0], scalar=w[:, 0, j0:j0 + 1],
                                                         in1=d[:, lo - t0:b - t0], op0=ALU.mult, op1=ALU.add), r=[Cx, w, dst], w=[dst])
        kb.V(lambda e, a=a, hi=hi: e.scalar_tensor_tensor(out=d[:, a - t0:hi - t0], in0=Cx[:, a - t0 + 2:hi - t0 + 2], scalar=w[:, 2, j0:j0 + 1],
                                                         in1=d[:, a - t0:hi - t0], op0=ALU.mult, op1=ALU.add), r=[Cx, w, dst], w=[dst])


def phase_mlstm(kb, PT, pt_bufs, ml_conv, ml_gate_b, OD, od_bufs, nh=4):
    with kb.phase():
        idf = kb.identity(F32)
        idb = kb.sb([128, 128], BF16)
        kb.V(lambda e: e.tensor_copy(out=idb[:], in_=idf[:]), r=[idf], w=[idb])
        mf, mb = make_masks(kb)
        pT_ = kb.ps([64, 128], BF16)
        tls = [gla_tiles(kb, pT_), gla_tiles(kb, pT_)]
        pB = [kb.ps([128, 512])] * 2
        ptm = pB[0]
        cw = kb.sb([128, 3, 8])
        load_rows_T(kb, ml_conv.rearrange("j (c p) -> (j c) p", p=128), 24, idf, ptm, cw[:].rearrange("p j c -> p (j c)"), cw)
        kb.V(lambda e: e.tensor_scalar(out=cw[:, :, 4:8], in0=cw[:, :, 4:8], scalar1=128.0 ** -0.5, scalar2=None, op0=ALU.mult), r=[cw], w=[cw])
        GR = kb.sb([16, T]); gb = kb.sb([16, 1]); msk = kb.sb([16, 1])
        for q, (prow, brow) in enumerate(((2048, 0), (2056, 2), (2052, 1), (2060, 3))):
            kb.dma(GR[q * 4:q * 4 + 4, :], PT[prow:prow + 4, :], r=pt_bufs, w=[GR])
            kb.dma(gb[q * 4:q * 4 + 4, :], ml_gate_b[brow:brow + 1, :].rearrange("o h -> h o"), w=[gb])
        kb.V(lambda e: e.memset(msk[:], 1.0), w=[msk])
        kb.V(lambda e: e.memset(msk[0:8, :], 0.0), r=[msk], w=[msk])
        tmpA = tls[0]["A"]
        GW = HC * 64
        kb.V(lambda e: e.tensor_scalar(out=GR[:], in0=GR[:], scalar1=gb[:, 0:1], scalar2=None, op0=ALU.add), r=[GR, gb], w=[GR])
        for g0 in range(0, T, GW):
            gn = min(GW, T - g0)
            tg = flat(tmpA)[0:16, 0:gn]
            grs = GR[:, g0:g0 + gn]
            kb.A(lambda e, tg=tg, grs=grs: e.activation(out=tg, in_=grs, func=AF.Sigmoid), r=[GR], w=[tmpA])
            kb.A(lambda e, tg=tg: e.activation(out=tg, in_=tg, func=AF.Ln), r=[tmpA], w=[tmpA])
            kb.V(lambda e, tg=tg, grs=grs: e.tensor_tensor(out=tg, in0=tg, in1=grs, op=ALU.subtract), r=[tmpA, GR], w=[tmpA])
            kb.V(lambda e, tg=tg, grs=grs: e.scalar_tensor_tensor(out=grs, in0=tg, scalar=msk[:, 0:1], in1=grs, op0=ALU.mult, op1=ALU.add),
                 r=[tmpA, msk, GR], w=[GR])
        sel = kb.sb([16, 16, 128])
        kb.V(lambda e: e.tensor_copy(out=sel[:], in_=idf[0:16, 0:16].unsqueeze(2).to_broadcast([16, 16, 128])), r=[idf], w=[sel])
        Vp = kb.sb([64, NCH, 129], BF16)
        stg = [kb.sb([128, 512]) for _ in range(2)]
        pV = [Tn(pB[0][0:64, 0:128], pB[0].b)] * 2
        for h in range(nh):
            build_Vp(kb, PT, 1024 + h * 128, Vp, 129, idf, pt_bufs, stg, pV)
            gens = []
            for d in range(2):
                r = d * 4 + h

                def load(kind, dst, tmp, c0, h=h, r=r):
                    if kind in ("lf", "ig"):
                        rr = r + (8 if kind == "lf" else 0)
                        for g in range((HT + 511) // 512):
                            t0 = g * 512; tn = min(512, HT - t0); p = pB[g % 2]
                            kb.PE(lambda e, p=p, t0=t0, tn=tn, rr=rr: e.matmul(p[:, 0:tn], lhsT=sel[:, rr, :], rhs=GR[:, c0 * 64 + t0:c0 * 64 + t0 + tn],
                                                                          start=True, stop=True), r=[sel, GR], w=[p])
                            kb.VA(g, lambda e, a, p=p, t0=t0, tn=tn: (e.copy(out=flat(dst)[:, t0:t0 + tn], in_=p[:, 0:tn]) if a else
                                                                     e.tensor_copy(out=flat(dst)[:, t0:t0 + tn], in_=p[:, 0:tn])), r=[p], w=[dst])
                        return True
                    conv3(kb, dst, tmp, PT, pt_bufs, h * 128 if kind == "q" else 512 + h * 128, cw, h if kind == "q" else 4 + h, c0)
                    return True

                gens.append(gla_engine(kb, tls[d], load, Vp, 129, d == 1, mb if d == 1 else mf, idb, OD[d], od_bufs[d], h * 128, True))
            run_interleaved(gens)


def phase_gla_merge(kb, OD, od_bufs, PT, pt_bufs, gate_row0, gate_func, gain, Y, y_buf, nh=4):
    with kb.phase():
        idf = kb.identity(F32)
        NB_ = 4
        W_ = nh * 128
        gr = kb.sb([128, W_])
        kb.dma(gr[:], gain.rearrange("(o n) -> o n", o=1)[:, 0:W_].partition_broadcast(128), w=[gr])
        o0s = [kb.sb([128, W_]) for _ in range(NB_)]; o1s = [kb.sb([128, W_]) for _ in range(NB_)]
        gts = [kb.sb([128, nh, 128]) for _ in range(NB_)]
        pg = [kb.ps([128, 512]) for _ in range(NB_)]
        gs = [kb.sb([128, W_]) for _ in range(NB_)]
        junk = kb.sb([128, 128]); sss = [kb.sb([128, 4]) for _ in range(NB_)]
        ys = [kb.sb([128, W_]) for _ in range(NB_)]
        def tile(i):
            o0 = o0s[i % NB_]; o1 = o1s[i % NB_]; gt = gts[i % NB_]; p = pg[i % NB_]; g = gs[i % NB_]; ss = sss[i % NB_]; y = ys[i % NB_]
            kb.dma(o0[:], OD[0][i * 128:(i + 1) * 128, 0:W_], r=od_bufs[0][2 * i:2 * i + 2], w=[o0])
            kb.dma(o1[:], OD[1][i * 128:(i + 1) * 128, 0:W_], r=od_bufs[1][2 * i:2 * i + 2], w=[o1])
            kb.dma(gt[:], PT[gate_row0:gate_row0 + W_, i * 128:(i + 1) * 128].rearrange("(h p) t -> p h t", p=128), r=pt_bufs, w=[gt])
            yield
            for h in range(nh):
                kb.PE(lambda e, h=h, gt=gt, p=p: e.transpose(out=p[:, h * 128:(h + 1) * 128], in_=gt[:, h, :], identity=idf[:]), r=[gt, idf], w=[p])
            yield
            kb.A(lambda e, g=g, p=p: e.activation(out=g[:], in_=p[:, 0:W_], func=AF.Exp, scale=-1.0), r=[p], w=[g])
            yield
            kb.V(lambda e, g=g: e.tensor_scalar(out=g[:], in0=g[:], scalar1=1.0, scalar2=None, op0=ALU.add), r=[g], w=[g])
            kb.V(lambda e, g=g: e.reciprocal(out=g[:], in_=g[:]), r=[g], w=[g])
            if gate_func == AF.Silu:
                kb.V(lambda e, g=g, p=p: e.tensor_tensor(out=g[:], in0=g[:], in1=p[:, 0:W_], op=ALU.mult), r=[g, p], w=[g])
            kb.G(lambda e, o0=o0, o1=o1: e.tensor_tensor(out=o0[:], in0=o0[:], in1=o1[:], op=ALU.add), r=[o0, o1], w=[o0])
            yield
            for h in range(nh):
                kb.A(lambda e, h=h, o0=o0, ss=ss: e.activation(out=junk[:], in_=o0[:, h * 128:(h + 1) * 128], func=AF.Square,
                                                             accum_out=ss[:, h:h + 1]), r=[o0], w=[junk, ss])
            yield
            kb.V(lambda e, ss=ss: e.tensor_scalar(out=ss[:], in0=ss[:], scalar1=1.0 / 128, scalar2=EPS, op0=ALU.mult, op1=ALU.add), r=[ss], w=[ss])
            yield
            kb.A(lambda e, ss=ss: e.activation(out=ss[:], in_=ss[:], func=AF.Ln), r=[ss], w=[ss])
            kb.A(lambda e, ss=ss: e.activation(out=ss[:], in_=ss[:], func=AF.Exp, scale=-0.5), r=[ss], w=[ss])
            yield
            kb.V(lambda e, g=g: e.tensor_tensor(out=g[:], in0=g[:], in1=gr[:], op=ALU.mult), r=[g, gr], w=[g])
            for h in range(nh):
                kb.V(lambda e, h=h, o0=o0, ss=ss, g=g, y=y: e.scalar_tensor_tensor(
                    out=y[:, h * 128:(h + 1) * 128], in0=o0[:, h * 128:(h + 1) * 128], scalar=ss[:, h:h + 1], in1=g[:, h * 128:(h + 1) * 128],
                    op0=ALU.mult, op1=ALU.mult), r=[o0, ss, g], w=[y])
            kb.dma(Y[i * 128:(i + 1) * 128, 0:W_], y[:], r=[y], w=[y_buf[i]])


            yield

        for i0_ in range(0, NT, NB_):
            run_interleaved([tile(i) for i in range(i0_, min(NT, i0_ + NB_))])


def phase_hgrn(kb, PT, pt_bufs, hg_lb, OD, od_bufs, nh=4):
    with kb.phase():
        idf = kb.identity(F32)
        idb = kb.sb([128, 128], BF16)
        kb.V(lambda e: e.tensor_copy(out=idb[:], in_=idf[:]), r=[idf], w=[idb])
        mf, mb = make_masks(kb)
        pT_ = kb.ps([64, 128], BF16)
        tls = [gla_tiles(kb, pT_), gla_tiles(kb, pT_)]
        ptm = kb.ps([128, 128])
        pV = [Tn(ptm[0:64, 0:128], ptm.b)] * 2
        lbr = kb.sb([128, 8]); lb = kb.sb([128, 4]); oml = kb.sb([128, 4])
        load_rows_T(kb, hg_lb.rearrange("l (c p) -> (l c) p", p=128), 8, idf, ptm, lbr[:], lbr)
        kb.V(lambda e: e.tensor_tensor(out=lb[:], in0=lbr[:, 4:8], in1=lbr[:, 0:4], op=ALU.subtract), r=[lbr], w=[lb])
        kb.A(lambda e: e.activation(out=lb[:], in_=lb[:], func=AF.Sigmoid), r=[lb], w=[lb])
        kb.V(lambda e: e.tensor_scalar(out=oml[:], in0=lb[:], scalar1=-1.0, scalar2=1.0, op0=ALU.mult, op1=ALU.add), r=[lb], w=[oml])
        Vp = kb.sb([64, NCH, 128], BF16)
        stg = [kb.sb([128, 512]) for _ in range(2)]
        for h in range(nh):
            build_Vp(kb, PT, 1536 + h * 128, Vp, 128, idf, pt_bufs, stg, pV)
            gens = []
            for d in range(2):
                def load(kind, dst, tmp, c0, h=h, d=d):
                    if kind == "ig":
                        return False
                    row0 = h * 128 if kind == "q" else 512 + d * 512 + h * 128
                    fd = flat(dst)
                    kb.dma(fd, PT[row0:row0 + 128, c0 * 64:c0 * 64 + HT], r=pt_bufs, w=[dst])
                    if kind == "q":
                        kb.A(lambda e: e.activation(out=fd, in_=fd, func=AF.Silu), r=[dst], w=[dst])
                        kb.V(lambda e: e.tensor_scalar(out=fd, in0=fd, scalar1=128.0 ** -0.5, scalar2=None, op0=ALU.mult), r=[dst], w=[dst])
                    elif kind == "lf":
                        kb.A(lambda e: e.activation(out=fd, in_=fd, func=AF.Sigmoid), r=[dst], w=[dst])
                        kb.V(lambda e: e.tensor_scalar(out=fd, in0=fd, scalar1=oml[:, h:h + 1], scalar2=lb[:, h:h + 1], op0=ALU.mult, op1=ALU.add),
                             r=[dst, oml, lb], w=[dst])
                        kb.A(lambda e: e.activation(out=fd, in_=fd, func=AF.Ln), r=[dst], w=[dst])
                    else:
                        kb.A(lambda e: e.activation(out=fd, in_=fd, func=AF.Sigmoid, scale=-1.0), r=[dst], w=[dst])
                        kb.V(lambda e: e.tensor_scalar(out=fd, in0=fd, scalar1=oml[:, h:h + 1], scalar2=None, op0=ALU.mult), r=[dst, oml], w=[dst])
                    return True

                gens.append(gla_engine(kb, tls[d], load, Vp, 128, d == 1, mb if d == 1 else mf, idb, OD[d], od_bufs[d], h * 128, False))
            run_interleaved(gens)

TL = T - NCTX
SCALE = 192.0 ** -0.5


def fm_rmsnorm(kb, PT, pt_bufs, row0, nch, t_lo, t_hi, gainT, ones, out, out_off, srcs, sq, pss, rst):
    n = nch * 128
    for bi, t0 in enumerate(range(t_lo, t_hi, 512)):
        tn = min(512, t_hi - t0)
        s = srcs[bi % 2]; ps = pss[bi % 2]; rs = rst[bi % 2]
        kb.dma(s[:, 0:nch, 0:tn], PT[row0:row0 + n, t0:t0 + tn].rearrange("(k p) t -> p k t", p=128), r=pt_bufs, w=[s])
        kb.A(lambda e, s=s, tn=tn: e.activation(out=sq[:, 0:nch, 0:tn], in_=s[:, 0:nch, 0:tn], func=AF.Square), r=[s], w=[sq])
        for k in range(nch):
            kb.PE(lambda e, k=k, ps=ps, tn=tn: e.matmul(ps[:, 0:tn], lhsT=ones[:], rhs=sq[:, k, 0:tn], start=(k == 0), stop=(k == nch - 1)),
                  r=[ones, sq], w=[ps])
        kb.V(lambda e, ps=ps, rs=rs, tn=tn: e.tensor_scalar(out=rs[:, 0:tn], in0=ps[:, 0:tn], scalar1=1.0 / n, scalar2=EPS, op0=ALU.mult, op1=ALU.add),
             r=[ps], w=[rs])
        kb.A(lambda e, rs=rs, tn=tn: e.activation(out=rs[:, 0:tn], in_=rs[:, 0:tn], func=AF.Sqrt), r=[rs], w=[rs])
        kb.V(lambda e, rs=rs, tn=tn: e.reciprocal(out=rs[:, 0:tn], in_=rs[:, 0:tn]), r=[rs], w=[rs])
        for k in range(nch):
            kb.V(lambda e, k=k, s=s, rs=rs, tn=tn, t0=t0: e.scalar_tensor_tensor(
                out=out[:, k, t0 - out_off:t0 - out_off + tn], in0=s[:, k, 0:tn], scalar=gainT[:, k:k + 1], in1=rs[:, 0:tn],
                op0=ALU.mult, op1=ALU.mult), r=[s, rs, gainT], w=[out])


def rope_mul(kb, eng, out3, in3, tab, cs, r0, nr):
    eng(lambda e: e.tensor_tensor(out=out3[0:32], in0=in3[0:32], in1=tab[0:32, cs, r0:r0 + nr].unsqueeze(2).to_broadcast([32, nr, 64]),
                                  op=ALU.mult))
    eng(lambda e: e.tensor_tensor(out=out3[32:64], in0=in3[32:64], in1=tab[32:64, cs, 0:64].unsqueeze(1).to_broadcast([32, nr, 64]),
                                  op=ALU.mult))


def phase_mla(kb, PT, pt_bufs, q_norm, w_qb, kv_norm, w_kvb, rope_tab, Y, y_buf, nh=4):
    STOP = 99
    with kb.phase():
        idf = kb.identity(F32)
        ones = kb.sb([128, 128])
        kb.V(lambda e: e.memset(ones[:], 1.0), w=[ones])
        pA = kb.ps([128, 512]); pBk = kb.ps([128, 512])
        gq = kb.sb([128, 2]); gkv = kb.sb([128, 1])
        load_rows_T(kb, q_norm.rearrange("(k p) -> k p", p=128), 2, idf, pA, gq[:], gq)
        load_rows_T(kb, kv_norm.rearrange("(k p) -> k p", p=128), 1, idf, pA, gkv[:], gkv)
        stage = [kb.sb([128, 2, 512]) for _ in range(2)]
        wq = kb.load_w_bf16(w_qb, 256, 768, stage)
        wkv = kb.load_w_bf16(w_kvb, 128, 1024, stage)
        wrot = kb.sb([128, 2, 4, 64], BF16)
        wq4 = wq[:].rearrange("p k (h c) -> p k h c", h=4)
        for (o0, i0, sg) in ((0, 16, -1.0), (16, 0, 1.0), (32, 48, -1.0), (48, 32, 1.0)):
            kb.V(lambda e, o0=o0, i0=i0, sg=sg: e.tensor_scalar(out=wrot[:, :, :, o0:o0 + 16], in0=wq4[:, :, :, 128 + i0:128 + i0 + 16], scalar1=sg,
                                                               scalar2=None, op0=ALU.mult), r=[wq], w=[wrot])
        rotm = kb.sb([64, 64], BF16)
        for (o0, i0, sg) in ((0, 16, -1.0), (16, 0, 1.0), (32, 48, -1.0), (48, 32, 1.0)):
            kb.V(lambda e, o0=o0, i0=i0, sg=sg: e.tensor_scalar(out=rotm[:, o0:o0 + 16], in0=idf[0:64, i0:i0 + 16], scalar1=sg, scalar2=None,
                                                               op0=ALU.mult), r=[idf], w=[rotm])
        tab = kb.sb([64, 2, 128])
        kb.dma(tab[:], rope_tab[:, :, :], w=[tab])
        if STOP <= 1:
            return
        srcs = [kb.sb([128, 2, 512]) for _ in range(2)]; sq = kb.sb([128, 2, 512]); rst = [kb.sb([128, 512]) for _ in range(2)]
        cqn = kb.sb([128, 2, TL], BF16); ckvn = kb.sb([128, 1, T], BF16)
        fm_rmsnorm(kb, PT, pt_bufs, 2560, 2, NCTX, T, gq, ones, cqn, NCTX, srcs, sq, [pA, pBk], rst)
        fm_rmsnorm(kb, PT, pt_bufs, 2816, 1, 0, T, gkv, ones, ckvn, 0, srcs, sq, [pA, pBk], rst)
        if STOP <= 2:
            return
        krT = kb.sb([64, T], BF16)
        krf = [kb.sb([64, 512]) for _ in range(2)]; krb = [kb.sb([64, 512], BF16) for _ in range(2)]
        t1s = [kb.sb([64, 512]) for _ in range(2)]; t2s = [kb.sb([64, 512]) for _ in range(2)]
        for bi, t0 in enumerate(range(0, T, 512)):
            tn = min(512, T - t0)
            f = krf[bi % 2]; b_ = krb[bi % 2]; t1 = t1s[bi % 2]; t2 = t2s[bi % 2]
            kb.dma(f[:, 0:tn], PT[2944:3008, t0:t0 + tn], r=pt_bufs, w=[f])
            segs = []
            if t0 < NCTX:
                kb.V(lambda e, f=f: e.tensor_copy(out=krT[:, 0:NCTX], in_=f[:, 0:NCTX]), r=[f], w=[krT])
                segs.append((NCTX, tn))
            else:
                segs.append((0, tn))
            for (a, b2) in segs:
                if a >= b2:
                    continue
                n_ = b2 - a
                r0 = (t0 + a - NCTX) // 64; nr = n_ // 64
                kb.A(lambda e, f=f, b_=b_, a=a, b2=b2: e.copy(out=b_[:, a:b2], in_=f[:, a:b2]), r=[f], w=[b_])
                kb.PE(lambda e, b_=b_, a=a, n_=n_: e.matmul(pA[0:64, 0:n_], lhsT=rotm[:], rhs=b_[:, a:a + n_], start=True, stop=True),
                      r=[rotm, b_], w=[pA])
                v3 = lambda x, a=a, n_=n_: x[:, a:a + n_].rearrange("p (r c) -> p r c", c=64)
                rope_mul(kb, lambda fn: kb.V(fn, r=[f, tab], w=[t1]), v3(t1), v3(f), tab, 0, r0, nr)
                rope_mul(kb, lambda fn: kb.V(fn, r=[pA, tab], w=[t2]), v3(t2), pA[0:64, 0:n_].rearrange("p (r c) -> p r c", c=64), tab, 1, r0, nr)
                kb.G(lambda e, t1=t1, t2=t2, a=a, n_=n_, t0=t0: e.tensor_tensor(out=krT[:, t0 + a:t0 + a + n_], in0=t1[:, a:a + n_], in1=t2[:, a:a + n_],
                                                                              op=ALU.add), r=[t1, t2], w=[krT])
        if STOP <= 3:
            return
        knT = kb.sb([128, T], BF16)
        Vp = kb.sb([128, NT, 129], BF16)
        kb.G(lambda e: e.memset(Vp[:, :, 128:129], 1.0), w=[Vp])
        qn = [kb.sb([128, 512], BF16) for _ in range(2)]; qr = [kb.sb([64, 512], BF16) for _ in range(2)]
        PTs = [kb.sb([128, 512], BF16) for _ in range(3)]
        pS = [kb.ps([128, 512]) for _ in range(2)]
        pO = [kb.ps([128, 132]) for _ in range(4)]
        rd = [kb.sb([128, 1]) for _ in range(4)]; ob = [kb.sb([128, 128]) for _ in range(4)]
        cnt = 0
        for h in range(nh):
            for bi, t0 in enumerate(range(0, T, 512)):
                tn = min(512, T - t0)
                p = (pA, pBk)[bi % 2]
                kb.PE(lambda e, p=p, t0=t0, tn=tn: e.matmul(p[:, 0:tn], lhsT=wkv[:, 0, h * 256:h * 256 + 128], rhs=ckvn[:, 0, t0:t0 + tn],
                                                           start=True, stop=True), r=[wkv, ckvn], w=[p])
                kb.VA(bi, lambda e, a, p=p, t0=t0, tn=tn: (e.copy(out=knT[:, t0:t0 + tn], in_=p[:, 0:tn]) if a else
                                                          e.tensor_copy(out=knT[:, t0:t0 + tn], in_=p[:, 0:tn])), r=[p], w=[knT])
            for kt in range(NT):
                p = (pA, pBk)[kt % 2]
                kb.PE(lambda e, p=p, kt=kt: e.matmul(p[:, 0:128], lhsT=ckvn[:, 0, kt * 128:(kt + 1) * 128], rhs=wkv[:, 0, h * 256 + 128:h * 256 + 256],
                                                    start=True, stop=True), r=[wkv, ckvn], w=[p])
                kb.VA(kt, lambda e, a, p=p, kt=kt: (e.copy(out=Vp[:, kt, 0:128], in_=p[:, 0:128]) if a else
                                                    e.tensor_copy(out=Vp[:, kt, 0:128], in_=p[:, 0:128])), r=[p], w=[Vp])
            if STOP <= 4:
                return
            for qb in range(TL // 512):
                q0 = qb * 512
                qn_ = qn[qb % 2]; qr_ = qr[qb % 2]; t1 = t1s[qb % 2]; t2 = t2s[qb % 2]
                for k in range(2):
                    kb.PE(lambda e, k=k: e.matmul(pA[:, :], lhsT=wq[:, k, h * 192:h * 192 + 128], rhs=cqn[:, k, q0:q0 + 512], start=(k == 0), stop=(k == 1)),
                          r=[wq, cqn], w=[pA])
                kb.A(lambda e, qn_=qn_: e.copy(out=qn_[:], in_=pA[:, :]), r=[pA], w=[qn_])
                for k in range(2):
                    kb.PE(lambda e, k=k: e.matmul(pBk[0:64, :], lhsT=wq[:, k, h * 192 + 128:h * 192 + 192], rhs=cqn[:, k, q0:q0 + 512],
                                                  start=(k == 0), stop=(k == 1)), r=[wq, cqn], w=[pBk])
                v3 = lambda x: x[:, 0:512].rearrange("p (r c) -> p r c", c=64)
                rope_mul(kb, lambda fn: kb.V(fn, r=[pBk, tab], w=[t1]), v3(t1), pBk[0:64, :].rearrange("p (r c) -> p r c", c=64), tab, 0, q0 // 64, 8)
                for k in range(2):
                    kb.PE(lambda e, k=k: e.matmul(pBk[0:64, :], lhsT=wrot[:, k, h, :], rhs=cqn[:, k, q0:q0 + 512], start=(k == 0), stop=(k == 1)),
                          r=[wrot, cqn], w=[pBk])
                rope_mul(kb, lambda fn: kb.V(fn, r=[pBk, tab], w=[t2]), v3(t2), pBk[0:64, :].rearrange("p (r c) -> p r c", c=64), tab, 1, q0 // 64, 8)
                kb.G(lambda e, qr_=qr_, t1=t1, t2=t2: e.tensor_tensor(out=qr_[:], in0=t1[:], in1=t2[:], op=ALU.add), r=[t1, t2], w=[qr_])
                if STOP <= 5:
                    return
                def s_mm(kt, ps):
                    kb.PE(lambda e, ps=ps, kt=kt, qn_=qn_: e.matmul(ps[:], lhsT=knT[:, kt * 128:(kt + 1) * 128], rhs=qn_[:], start=True, stop=False),
                          r=[knT, qn_], w=[ps])
                    kb.PE(lambda e, ps=ps, kt=kt, qr_=qr_: e.matmul(ps[:], lhsT=krT[:, kt * 128:(kt + 1) * 128], rhs=qr_[:], start=False, stop=True),
                          r=[krT, qr_, ps], w=[ps])
                s_mm(0, pS[cnt % 2])
                for kt in range(NT):
                    ps = pS[cnt % 2]; pt_ = PTs[cnt % 3]
                    cnt += 1
                    kb.A(lambda e, ps=ps, pt_=pt_: e.activation(out=pt_[:], in_=ps[:], func=AF.Exp, scale=SCALE), r=[ps], w=[pt_])
                    if kt + 1 < NT:
                        s_mm(kt + 1, pS[cnt % 2])
                    for j in range(4):
                        kb.PE(lambda e, j=j, pt_=pt_, kt=kt: e.matmul(pO[j][:, 0:129], lhsT=pt_[:, j * 128:(j + 1) * 128], rhs=Vp[:, kt, :],
                                                                   start=(kt == 0), stop=(kt == NT - 1)), r=[pt_, Vp] + ([pO[j]] if kt else []), w=[pO[j]])
                for j in range(4):
                    kb.V(lambda e, j=j: e.reciprocal(out=rd[j][:], in_=pO[j][:, 128:129]), r=[pO[j]], w=[rd[j]])
                    kb.A(lambda e, j=j: e.activation(out=ob[j][:], in_=pO[j][:, 0:128], func=AF.Identity, scale=rd[j][:, 0:1]), r=[pO[j], rd[j]], w=[ob[j]])
                    tt = NCTX + q0 + j * 128
                    kb.dma(Y[tt:tt + 128, 512 + h * 128:512 + (h + 1) * 128], ob[j][:], r=[ob[j]], w=[y_buf[tt // 128]])
                if STOP <= 6:
                    return
            if STOP <= 7 + h:
                return

RC = 128
NRC = T // RC
SEG = 4
RW0 = 2064
DEC = -(2.718281828459045 ** -0.5)


def rw_runs(rev):
    order = ([1, 0] + list(range(NRC - 1, 1, -1))) if rev else list(range(NRC))
    groups = [[1, 0]] if rev else [[0, 1]]
    lat = order[2:]
    for i in range(0, len(lat), SEG):
        groups.append(lat[i:i + SEG])
    return groups


def shiftload(kb, dst, X, PT, pt_bufs, row0, t0, n, w3, j, wt):
    t1 = t0 + n
    la = max(t0 - 1, 0); lb = min(t1 + 1, T)
    kb.dma(X[:, la - t0 + 1:lb - t0 + 1], PT[row0:row0 + 128, la:lb], r=pt_bufs, w=[X])
    kb.V(lambda e: e.tensor_scalar(out=dst, in0=X[:, 1:n + 1], scalar1=w3[:, 1, j:j + 1], scalar2=None, op0=ALU.mult), r=[X, w3], w=[wt])
    for sa, sb in ((0, NCTX), (NCTX, T)):
        a = max(sa, t0); b = min(sb, t1)
        if a >= b:
            continue
        lo = a + 1 if a == sa else a
        hi = b - 1 if b == sb else b
        kb.V(lambda e, lo=lo, b=b: e.scalar_tensor_tensor(out=dst[:, lo - t0:b - t0], in0=X[:, lo - t0:b - t0], scalar=w3[:, 0, j:j + 1],
                                                         in1=dst[:, lo - t0:b - t0], op0=ALU.mult, op1=ALU.add), r=[X, w3, wt], w=[wt])
        kb.V(lambda e, a=a, hi=hi: e.scalar_tensor_tensor(out=dst[:, a - t0:hi - t0], in0=X[:, a - t0 + 2:hi - t0 + 2], scalar=w3[:, 2, j:j + 1],
                                                         in1=dst[:, a - t0:hi - t0], op0=ALU.mult, op1=ALU.add), r=[X, w3, wt], w=[wt])


def rw_consts(kb, idf, ptm, rw_mu):
    mu = kb.sb([128, 15]); w3 = kb.sb([128, 3, 15])
    load_rows_T(kb, rw_mu.rearrange("(k p) -> k p", p=128), 15, idf, ptm, mu[:], mu)
    kb.V(lambda e: e.tensor_scalar(out=w3[:, 0, :], in0=mu[:], scalar1=0.5, scalar2=None, op0=ALU.mult), r=[mu], w=[w3])
    kb.V(lambda e: e.tensor_scalar(out=w3[:, 2, :], in0=mu[:], scalar1=0.5, scalar2=None, op0=ALU.mult), r=[mu, w3], w=[w3])
    kb.V(lambda e: e.tensor_scalar(out=w3[:, 1, :], in0=mu[:], scalar1=-1.0, scalar2=1.0, op0=ALU.mult, op1=ALU.add), r=[mu, w3], w=[w3])
    return w3


def phase_rwkv(kb, PT, pt_bufs, prm, OD, od_bufs, BN, bn_bufs, npair=4):
    with kb.phase():
        idf = kb.identity(F32)
        pbanks = [kb.ps([128, 512]) for _ in range(8)]
        pP = [pbanks[0], pbanks[0]]
        w3 = rw_consts(kb, idf, pP[0], prm["rw_mu"])
        ones = kb.sb([128, 128]); kb.G(lambda e: e.memset(ones[:], 1.0), w=[ones])
        msk = {}
        for nm, pat, cm, op in (("gt_up", 1, -1, ALU.is_gt), ("ge_up", 1, -1, ALU.is_ge), ("gt_lo", -1, 1, ALU.is_gt), ("ge_lo", -1, 1, ALU.is_ge)):
            m = kb.sb([128, 128])
            kb.G(lambda e, m=m, pat=pat, cm=cm, op=op: e.affine_select(out=m[:], in_=ones[:], pattern=[[pat, 128]], compare_op=op, fill=0.0, base=0,
                                                                       channel_multiplier=cm), r=[ones], w=[m])
            msk[nm] = m
        bones = kb.sb([128, 128])
        kb.G(lambda e: e.memset(bones[:], 0.0), w=[bones])
        kb.G(lambda e: e.memset(bones[0:64, 0:64], 1.0), r=[bones], w=[bones])
        kb.G(lambda e: e.memset(bones[64:128, 64:128], 1.0), r=[bones], w=[bones])
        pv = {}
        for nm in ("rw_kk", "rw_ka", "rw_rk"):
            t_ = kb.sb([128, 4])
            load_rows_T(kb, prm[nm].rearrange("(k p) -> k p", p=128), 4, idf, pP[0], t_[:], t_)
            pv[nm] = t_
        for nm in ("rw_w0", "rw_a0"):
            t_ = kb.sb([128, 2, 4])
            load_rows_T(kb, prm[nm].rearrange("d (k p) -> (d k) p", p=128), 8, idf, pP[0], t_[:].rearrange("p d k -> p (d k)"), t_)
            pv[nm] = t_
        w2 = kb.sb([128, 512]); a2 = kb.sb([128, 512])
        kb.dma(w2[:], prm["rw_w2"].rearrange("d r c -> (d r) c"), w=[w2])
        kb.dma(a2[:], prm["rw_a2"].rearrange("d r c -> (d r) c"), w=[a2])
        def make_stream(d, bank0):
            pP = [pbanks[bank0], pbanks[bank0]]
            pTr = pbanks[bank0]
            pI = [Tn(pbanks[bank0 + 1][:, 0:128], pbanks[bank0 + 1].b), Tn(pbanks[bank0 + 2][:, 0:128], pbanks[bank0 + 2].b)]
            pXU = Tn(pbanks[bank0 + 3][:, 0:128], pbanks[bank0 + 3].b)
            pY = Tn(pbanks[bank0 + 3][:, 128:256], pbanks[bank0 + 3].b); pH = Tn(pbanks[bank0 + 3][:, 256:384], pbanks[bank0 + 3].b)
            ST = SEG * RC
            X = kb.sb([128, ST + 2])
            nm_ = ("R", "K", "V", "WL", "AL", "KK", "LW", "AA", "KD", "BV", "CWa", "CWb", "T1", "T2", "BT", "KT")
            S = {n: kb.sb([128, ST]) for n in nm_}
            AR = kb.sb([128, SEG, 2, RC])
            GL = kb.sb([128, SEG])
            Hbd = kb.sb([128, 128])
            mats = [{n: kb.sb([128, 128]) for n in ("AabT", "PrbT", "AakT", "PrkT", "A", "Bp0", "Bp1", "Ap0", "Ap1", "TT0", "TT1")} for _ in range(4)]
            toks = [{n: kb.sb([128, 128]) for n in ("Btok", "Ktok", "Vtok")} for _ in range(2)]
            Xs = [kb.sb([128, 128]) for _ in range(2)]; Us = [kb.sb([128, 128]) for _ in range(2)]; obs = [kb.sb([128, 128]) for _ in range(2)]
            cnt = [0]

            def prep(hp, d, c_lo, nch):
                t0 = c_lo * RC; n = nch * RC
                s = {k: v[:, 0:n] for k, v in S.items()}
                rev = d == 1
                for nm, ch in (("R", hp), ("K", 4 + hp), ("V", 8 + hp), ("WL", 12), ("AL", 13)):
                    shiftload(kb, s[nm], X, PT, pt_bufs, RW0 + ch * 128, t0, n, w3, ch, S[nm])
                ds_ = slice(d * 64, d * 64 + 64)
                yield
                kb.A(lambda e: e.activation(out=s["WL"], in_=s["WL"], func=AF.Tanh), r=[S["WL"]], w=[S["WL"]])
                yield
                for b0 in range(0, n, 512):
                    bn = min(512, n - b0)
                    p = pP[(b0 // 512) % 2]
                    kb.PE(lambda e, p=p, b0=b0, bn=bn: e.matmul(p[:, 0:bn], lhsT=w2[ds_, hp * 128:(hp + 1) * 128], rhs=S["WL"][ds_, b0:b0 + bn],
                                                               start=True, stop=True), r=[w2, S["WL"]], w=[p])
                    kb.A(lambda e, p=p, b0=b0, bn=bn: e.activation(out=S["LW"][:, b0:b0 + bn], in_=p[:, 0:bn], func=AF.Sigmoid,
                                                                  bias=pv["rw_w0"][:, d, hp:hp + 1]), r=[p, pv["rw_w0"]], w=[S["LW"]])
                yield
                kb.V(lambda e: e.tensor_scalar(out=s["LW"], in0=s["LW"], scalar1=DEC, scalar2=None, op0=ALU.mult), r=[S["LW"]], w=[S["LW"]])
                yield
                for b0 in range(0, n, 512):
                    bn = min(512, n - b0)
                    p = pP[(b0 // 512) % 2]
                    kb.PE(lambda e, p=p, b0=b0, bn=bn: e.matmul(p[:, 0:bn], lhsT=a2[ds_, hp * 128:(hp + 1) * 128], rhs=S["AL"][ds_, b0:b0 + bn],
                                                               start=True, stop=True), r=[a2, S["AL"]], w=[p])
                    kb.A(lambda e, p=p, b0=b0, bn=bn: e.activation(out=S["AA"][:, b0:b0 + bn], in_=p[:, 0:bn], func=AF.Sigmoid,
                                                                  bias=pv["rw_a0"][:, d, hp:hp + 1]), r=[p, pv["rw_a0"]], w=[S["AA"]])
                yield
                kb.V(lambda e: e.tensor_scalar(out=s["KK"], in0=s["K"], scalar1=pv["rw_kk"][:, hp:hp + 1], scalar2=None, op0=ALU.mult),
                     r=[S["K"], pv["rw_kk"]], w=[S["KK"]])
                yield
                kb.G(lambda e: e.tensor_tensor(out=s["T1"], in0=s["KK"], in1=s["KK"], op=ALU.mult), r=[S["KK"]], w=[S["T1"]])
                yield
                for b0 in range(0, n, 512):
                    bn = min(512, n - b0)
                    p = pP[(b0 // 512) % 2]
                    kb.PE(lambda e, p=p, b0=b0, bn=bn: e.matmul(p[:, 0:bn], lhsT=bones[:], rhs=S["T1"][:, b0:b0 + bn], start=True, stop=True),
                          r=[bones, S["T1"]], w=[p])
                    kb.V(lambda e, p=p, b0=b0, bn=bn: e.tensor_scalar(out=S["T2"][:, b0:b0 + bn], in0=p[:, 0:bn], scalar1=1e-12, scalar2=None, op0=ALU.add),
                         r=[p], w=[S["T2"]])
                yield
                kb.A(lambda e: e.activation(out=s["T2"], in_=s["T2"], func=AF.Sqrt), r=[S["T2"]], w=[S["T2"]])
                yield
                kb.V(lambda e: e.reciprocal(out=s["T2"], in_=s["T2"]), r=[S["T2"]], w=[S["T2"]])
                yield
                kb.V(lambda e: e.tensor_tensor(out=s["KK"], in0=s["KK"], in1=s["T2"], op=ALU.mult), r=[S["KK"], S["T2"]], w=[S["KK"]])
                yield
                kb.V(lambda e: e.tensor_scalar(out=s["T1"], in0=s["AA"], scalar1=-1.0, scalar2=pv["rw_ka"][:, hp:hp + 1], op0=ALU.add, op1=ALU.mult),
                     r=[S["AA"], pv["rw_ka"]], w=[S["T1"]])
                yield
                kb.V(lambda e: e.scalar_tensor_tensor(out=s["KD"], in0=s["T1"], scalar=1.0, in1=s["K"], op0=ALU.add, op1=ALU.mult),
                     r=[S["T1"], S["K"]], w=[S["KD"]])
                yield
                kb.G(lambda e: e.tensor_tensor(out=s["BV"], in0=s["KK"], in1=s["AA"], op=ALU.mult), r=[S["KK"], S["AA"]], w=[S["BV"]])
                yield
                kb.V(lambda e: e.scalar_tensor_tensor(out=s["T1"], in0=s["R"], scalar=pv["rw_rk"][:, hp:hp + 1], in1=s["KD"], op0=ALU.mult, op1=ALU.mult),
                     r=[S["R"], S["KD"], pv["rw_rk"]], w=[S["T1"]])
                yield
                for b0 in range(0, n, 512):
                    bn = min(512, n - b0)
                    p = pP[(b0 // 512) % 2]
                    kb.PE(lambda e, p=p, b0=b0, bn=bn: e.matmul(p[:, 0:bn], lhsT=bones[:], rhs=S["T1"][:, b0:b0 + bn], start=True, stop=True),
                          r=[bones, S["T1"]], w=[p])
                    kb.V(lambda e, p=p, b0=b0, bn=bn: e.tensor_tensor(out=S["T2"][:, b0:b0 + bn], in0=p[:, 0:bn], in1=S["V"][:, b0:b0 + bn], op=ALU.mult),
                         r=[p, S["V"]], w=[S["T2"]])
                yield
                kb.dma(BN[d][hp * 128:(hp + 1) * 128, t0:t0 + n], s["T2"], r=[S["T2"]], w=[bn_bufs[d][hp]])
                v3 = lambda x: x[:, 0:n].rearrange("p (c l) -> p c l", l=RC)
                src, dst = S["LW"], S["CWa"]
                first = True
                yield
                for sft in (1, 2, 4, 8, 16, 32, 64):
                    a_, b_ = v3(src), v3(dst)
                    if not rev:
                        kb.V(lambda e, a_=a_, b_=b_, sft=sft: e.tensor_tensor(out=b_[:, :, sft:], in0=a_[:, :, sft:], in1=a_[:, :, :RC - sft], op=ALU.add),
                             r=[src], w=[dst])
                        kb.G(lambda e, a_=a_, b_=b_, sft=sft: e.tensor_copy(out=b_[:, :, :sft], in_=a_[:, :, :sft]), r=[src], w=[dst])
                    else:
                        kb.V(lambda e, a_=a_, b_=b_, sft=sft: e.tensor_tensor(out=b_[:, :, :RC - sft], in0=a_[:, :, :RC - sft], in1=a_[:, :, sft:], op=ALU.add),
                             r=[src], w=[dst])
                        kb.G(lambda e, a_=a_, b_=b_, sft=sft: e.tensor_copy(out=b_[:, :, RC - sft:], in_=a_[:, :, RC - sft:]), r=[src], w=[dst])
                    if first:
                        src, dst = S["CWa"], S["CWb"]
                        first = False
                    else:
                        src, dst = dst, src
                CW = src
                pos = 0 if rev else RC - 1
                yield
                kb.A(lambda e: e.activation(out=GL[:, 0:nch], in_=v3(CW)[:, :, pos], func=AF.Exp), r=[CW], w=[GL])
                yield
                kb.V(lambda e: e.tensor_tensor(out=s["T1"], in0=CW[:, 0:n], in1=s["LW"], op=ALU.subtract), r=[CW, S["LW"]], w=[S["T1"]])
                yield
                kb.A(lambda e: e.activation(out=s["T1"], in_=s["T1"], func=AF.Exp), r=[S["T1"]], w=[S["T1"]])
                yield
                kb.V(lambda e: e.scalar_tensor_tensor(out=AR[:, 0:nch, 0, :], in0=v3(S["KK"]), scalar=-1.0, in1=v3(S["T1"]), op0=ALU.mult, op1=ALU.mult),
                     r=[S["KK"], S["T1"]], w=[AR])
                yield
                kb.A(lambda e: e.activation(out=s["T2"], in_=CW[:, 0:n], func=AF.Exp), r=[CW], w=[S["T2"]])
                yield
                kb.G(lambda e: e.tensor_tensor(out=AR[:, 0:nch, 1, :], in0=v3(S["R"]), in1=v3(S["T2"]), op=ALU.mult), r=[S["R"], S["T2"]], w=[AR])
                yield
                kb.A(lambda e: e.activation(out=s["T1"], in_=CW[:, 0:n], func=AF.Exp, scale=-1.0), r=[CW], w=[S["T1"]])
                yield
                kb.V(lambda e: e.tensor_tensor(out=s["BT"], in0=s["BV"], in1=s["T1"], op=ALU.mult), r=[S["BV"], S["T1"]], w=[S["BT"]])
                yield
                kb.G(lambda e: e.tensor_tensor(out=s["KT"], in0=s["KD"], in1=s["T1"], op=ALU.mult), r=[S["KD"], S["T1"]], w=[S["KT"]])

            def pre(lc, rev, slot):
                cs = slice(lc * RC, (lc + 1) * RC)
                tk = toks[slot % 2]
                for nm, srcn in (("Btok", "BT"), ("Ktok", "KT"), ("Vtok", "V")):
                    kb.PE(lambda e, srcn=srcn: e.transpose(out=pTr[:, 0:128], in_=S[srcn][:, cs], identity=idf[:]), r=[S[srcn], idf], w=[pTr])
                    kb.VA(cnt[0], lambda e, a, nm=nm: (e.copy(out=tk[nm][:], in_=pTr[:, 0:128]) if a else e.tensor_copy(out=tk[nm][:], in_=pTr[:, 0:128])),
                          r=[pTr], w=[tk[nm]])
                    cnt[0] += 1
                yield
                mT_s = msk["gt_lo" if rev else "gt_up"]; mT_i = msk["ge_lo" if rev else "ge_up"]; mA = msk["gt_up" if rev else "gt_lo"]
                hm = []
                for hh in range(2):
                    m = mats[(slot % 2) * 2 + hh]
                    hs = slice(hh * 64, hh * 64 + 64)
                    p = pP[hh]
                    kb.PE(lambda e, p=p, hs=hs: e.matmul(p[:, 0:256], lhsT=S["BT"][hs, cs], rhs=AR[hs, lc].rearrange("p a l -> p (a l)"), start=True, stop=True),
                          r=[S["BT"], AR], w=[p])
                    kb.V(lambda e, p=p, m=m: e.tensor_tensor(out=m["AabT"][:], in0=p[:, 0:128], in1=mT_s[:], op=ALU.mult), r=[p, mT_s], w=[m["AabT"]])
                    kb.V(lambda e, p=p, m=m: e.tensor_tensor(out=m["PrbT"][:], in0=p[:, 128:256], in1=mT_i[:], op=ALU.mult), r=[p, mT_i], w=[m["PrbT"]])
                    kb.PE(lambda e, p=p, hs=hs: e.matmul(p[:, 256:512], lhsT=S["KT"][hs, cs], rhs=AR[hs, lc].rearrange("p a l -> p (a l)"), start=True, stop=True),
                          r=[S["KT"], AR], w=[p])
                    kb.V(lambda e, p=p, m=m: e.tensor_tensor(out=m["AakT"][:], in0=p[:, 256:384], in1=mT_s[:], op=ALU.mult), r=[p, mT_s], w=[m["AakT"]])
                    kb.V(lambda e, p=p, m=m: e.tensor_tensor(out=m["PrkT"][:], in0=p[:, 384:512], in1=mT_i[:], op=ALU.mult), r=[p, mT_i], w=[m["PrkT"]])
                    pi = pI[hh]
                    kb.PE(lambda e, pi=pi, hs=hs: e.matmul(pi[:], lhsT=AR[hs, lc, 0, :], rhs=S["BT"][hs, cs], start=True, stop=True), r=[S["BT"], AR], w=[pi])
                    kb.V(lambda e, pi=pi, m=m: e.tensor_tensor(out=m["A"][:], in0=pi[:], in1=mA[:], op=ALU.mult), r=[pi, mA], w=[m["A"]])
                    kb.G(lambda e, m=m: e.tensor_tensor(out=m["TT0"][:], in0=m["AabT"][:], in1=idf[:], op=ALU.add), r=[m["AabT"], idf], w=[m["TT0"]])
                    hm.append(m)
                    yield
                cur = [{"B": m["AabT"], "A": m["A"], "TT": m["TT0"]} for m in hm]
                yield
                for lv in range(6):
                    nxt = [{"B": hm[hh]["Bp%d" % (lv % 2)], "A": hm[hh]["Ap%d" % (lv % 2)], "TT": hm[hh]["TT%d" % ((lv + 1) % 2)]} for hh in range(2)]
                    for hh in range(2):
                        c_ = cur[hh]; pi = pI[hh]
                        kb.PE(lambda e, pi=pi, c_=c_: e.matmul(pi[:], lhsT=c_["B"][:], rhs=c_["A"][:], start=True, stop=True), r=[c_["B"], c_["A"]], w=[pi])
                    yield
                    for hh in range(2):
                        c_ = cur[hh]; pi = pI[hh]; A2 = nxt[hh]["A"]
                        kb.A(lambda e, pi=pi, A2=A2: e.copy(out=A2[:], in_=pi[:]), r=[pi], w=[A2])
                        if lv < 5:
                            kb.PE(lambda e, pi=pi, c_=c_: e.matmul(pi[:], lhsT=c_["A"][:], rhs=c_["B"][:], start=True, stop=True), r=[c_["B"], c_["A"]], w=[pi])
                    yield
                    for hh in range(2):
                        c_ = cur[hh]; pi = pI[hh]; A2 = nxt[hh]["A"]; B2 = nxt[hh]["B"]
                        if lv < 5:
                            kb.A(lambda e, pi=pi, B2=B2: e.copy(out=B2[:], in_=pi[:]), r=[pi], w=[B2])
                        kb.PE(lambda e, pi=pi, A2=A2, c_=c_: e.matmul(pi[:], lhsT=A2[:], rhs=c_["TT"][:], start=True, stop=True), r=[A2, c_["TT"]], w=[pi])
                    yield
                    for hh in range(2):
                        c_ = cur[hh]; pi = pI[hh]; TTn = nxt[hh]["TT"]
                        kb.V(lambda e, pi=pi, TTn=TTn, c_=c_: e.tensor_tensor(out=TTn[:], in0=pi[:], in1=c_["TT"][:], op=ALU.add), r=[pi, c_["TT"]], w=[TTn])
                    yield
                    cur = nxt
                return {"tk": tk, "m": hm, "TT": [c_["TT"] for c_ in cur]}

            def chain(lc, c, d, hp, pr, slot):
                tk, hm, TT = pr["tk"], pr["m"], pr["TT"]
                Xt = Xs[slot % 2]; Ut = Us[slot % 2]; ob = obs[slot % 2]
                kb.PE(lambda e: e.matmul(pXU[:], lhsT=AR[:, lc, 0, :], rhs=Hbd[:], start=True, stop=False), r=[AR, Hbd], w=[pXU])
                for hh in range(2):
                    vs = slice(hh * 64, hh * 64 + 64)
                    kb.PE(lambda e, hh=hh, vs=vs: e.matmul(pXU[:, vs], lhsT=hm[hh]["AakT"][:], rhs=tk["Vtok"][:, vs], start=False, stop=(hh == 1)),
                          r=[hm[hh]["AakT"], tk["Vtok"], pXU], w=[pXU])
                yield
                kb.A(lambda e: e.copy(out=Xt[:], in_=pXU[:]), r=[pXU], w=[Xt])
                for hh in range(2):
                    vs = slice(hh * 64, hh * 64 + 64)
                    kb.PE(lambda e, hh=hh, vs=vs: e.matmul(pXU[:, vs], lhsT=TT[hh][:], rhs=Xt[:, vs], start=True, stop=True), r=[TT[hh], Xt], w=[pXU])
                yield
                kb.V(lambda e: e.tensor_copy(out=Ut[:], in_=pXU[:]), r=[pXU], w=[Ut])
                kb.PE(lambda e: e.matmul(pY[:], lhsT=AR[:, lc, 1, :], rhs=Hbd[:], start=True, stop=False), r=[AR, Hbd], w=[pY])
                for hh in range(2):
                    vs = slice(hh * 64, hh * 64 + 64)
                    kb.PE(lambda e, hh=hh, vs=vs: e.matmul(pY[:, vs], lhsT=hm[hh]["PrbT"][:], rhs=Ut[:, vs], start=False, stop=False),
                          r=[hm[hh]["PrbT"], Ut, pY], w=[pY])
                    kb.PE(lambda e, hh=hh, vs=vs: e.matmul(pY[:, vs], lhsT=hm[hh]["PrkT"][:], rhs=tk["Vtok"][:, vs], start=False, stop=(hh == 1)),
                          r=[hm[hh]["PrkT"], tk["Vtok"], pY], w=[pY])
                yield
                kb.A(lambda e: e.copy(out=ob[:], in_=pY[:]), r=[pY], w=[ob])
                kb.dma(OD[d][c * RC:(c + 1) * RC, hp * 128:(hp + 1) * 128], ob[:], r=[ob], w=[od_bufs[d][c]])
                kb.PE(lambda e: e.matmul(pH[:], lhsT=tk["Btok"][:], rhs=Ut[:], start=True, stop=False), r=[tk["Btok"], Ut], w=[pH])
                kb.PE(lambda e: e.matmul(pH[:], lhsT=tk["Ktok"][:], rhs=tk["Vtok"][:], start=False, stop=True), r=[tk["Ktok"], tk["Vtok"], pH], w=[pH])
                yield
                kb.G(lambda e: e.tensor_scalar(out=Hbd[:], in0=Hbd[:], scalar1=GL[:, lc:lc + 1], scalar2=None, op0=ALU.mult), r=[Hbd, GL], w=[Hbd])
                for hh in range(2):
                    vs = slice(hh * 64, hh * 64 + 64)
                    kb.V(lambda e, vs=vs: e.scalar_tensor_tensor(out=Hbd[vs, vs], in0=pH[vs, vs], scalar=GL[vs, lc:lc + 1], in1=Hbd[vs, vs],
                                                                 op0=ALU.mult, op1=ALU.add), r=[pH, GL, Hbd], w=[Hbd])


            def gen(hp):
                kb.V(lambda e: e.memset(Hbd[:], 0.0), w=[Hbd])
                slot = 0
                for run in rw_runs(d == 1):
                    c_lo = min(run); nch = len(run)
                    yield from prep(hp, d, c_lo, nch)
                    prs = {}
                    prs[0] = yield from pre(run[0] - c_lo, d == 1, slot)
                    for i, c in enumerate(run):
                        if i + 1 < nch:
                            prs[i + 1] = yield from pre(run[i + 1] - c_lo, d == 1, slot + i + 1)
                        yield from chain(c - c_lo, c, d, hp, prs.pop(i), slot + i)
                    slot += nch
            return gen

        streams = [make_stream(0, 0), make_stream(1, 4)]
        for hp in range(npair):
            gens = [st_(hp) for st_ in streams]
            live = list(gens)
            while live:
                for g_ in list(live):
                    try:
                        next(g_)
                    except StopIteration:
                        live.remove(g_)


def phase_rwkv_merge(kb, OD, od_bufs, BN, bn_bufs, PT, pt_bufs, prm, Y, y_buf, npair=4):
    with kb.phase():
        idf = kb.identity(F32)
        pBn = [kb.ps([128, 512]) for _ in range(3)]; pG = [kb.ps([128, 512]) for _ in range(3)]
        w3 = rw_consts(kb, idf, pBn[0], prm["rw_mu"])
        W_ = npair * 128; NH_ = npair * 2
        g2 = kb.sb([128, W_]); lnw = kb.sb([128, W_]); lnb = kb.sb([128, W_])
        kb.dma(g2[:], prm["rw_g2"][:, 0:W_], w=[g2])
        kb.dma(lnw[:], prm["rw_ln_w"].rearrange("(o n) -> o n", o=1)[:, 0:W_].partition_broadcast(128), w=[lnw])
        kb.dma(lnb[:], prm["rw_ln_b"].rearrange("(o n) -> o n", o=1)[:, 0:W_].partition_broadcast(128), w=[lnb])
        X = kb.sb([128, 130])
        o0s = [kb.sb([128, NH_, 64]) for _ in range(3)]; o1s = [kb.sb([128, NH_, 64]) for _ in range(3)]
        b0s = [kb.sb([128, npair, 128]) for _ in range(3)]; b1s = [kb.sb([128, npair, 128]) for _ in range(3)]
        gls = [kb.sb([128, 128]) for _ in range(3)]
        sqs = [kb.sb([128, NH_, 64]) for _ in range(3)]
        mus = [kb.sb([128, NH_]) for _ in range(3)]; vrs = [kb.sb([128, NH_]) for _ in range(3)]
        ys = [kb.sb([128, W_]) for _ in range(3)]
        def tile(i):
            t0 = i * 128
            o0 = o0s[i % 3]; o1 = o1s[i % 3]; b0 = b0s[i % 3]; b1 = b1s[i % 3]; gl = gls[i % 3]; sq = sqs[i % 3]
            mu = mus[i % 3]; vr = vrs[i % 3]; y = ys[i % 3]; pb = pBn[i % 3]; pg = pG[i % 3]
            f = lambda x: x[:].rearrange("p h c -> p (h c)")
            kb.dma(f(o0), OD[0][t0:t0 + 128, 0:W_], r=[od_bufs[0][i]], w=[o0])
            kb.dma(f(o1), OD[1][t0:t0 + 128, 0:W_], r=[od_bufs[1][i]], w=[o1])
            kb.dma(b0[:], BN[0][0:W_, t0:t0 + 128].rearrange("(k p) t -> p k t", p=128), r=bn_bufs[0][0:npair], w=[b0])
            kb.dma(b1[:], BN[1][0:W_, t0:t0 + 128].rearrange("(k p) t -> p k t", p=128), r=bn_bufs[1][0:npair], w=[b1])
            yield
            kb.G(lambda e, b0=b0, b1=b1: e.tensor_tensor(out=b0[:], in0=b0[:], in1=b1[:], op=ALU.add), r=[b0, b1], w=[b0])
            for k in range(npair):
                kb.PE(lambda e, k=k, b0=b0, pb=pb: e.transpose(out=pb[:, k * 128:(k + 1) * 128], in_=b0[:, k, :], identity=idf[:]), r=[b0, idf], w=[pb])
            yield
            shiftload(kb, gl[:], X, PT, pt_bufs, RW0 + 14 * 128, t0, 128, w3, 14, gl)
            yield
            kb.A(lambda e, gl=gl: e.activation(out=gl[:], in_=gl[:], func=AF.Exp, scale=-1.0), r=[gl], w=[gl])
            yield
            kb.V(lambda e, gl=gl: e.tensor_scalar(out=gl[:], in0=gl[:], scalar1=1.0, scalar2=None, op0=ALU.add), r=[gl], w=[gl])
            kb.V(lambda e, gl=gl: e.reciprocal(out=gl[:], in_=gl[:]), r=[gl], w=[gl])
            kb.PE(lambda e, gl=gl, pg=pg: e.matmul(pg[:, 0:W_], lhsT=gl[:], rhs=g2[:], start=True, stop=True), r=[gl, g2], w=[pg])
            yield
            kb.G(lambda e, o0=o0, o1=o1: e.tensor_tensor(out=o0[:], in0=o0[:], in1=o1[:], op=ALU.add), r=[o0, o1], w=[o0])
            yield
            kb.V(lambda e, o0=o0, mu=mu: e.reduce_sum(out=mu[:], in_=o0[:], axis=AX.X), r=[o0], w=[mu])
            kb.V(lambda e, mu=mu: e.tensor_scalar(out=mu[:], in0=mu[:], scalar1=1.0 / 64, scalar2=None, op0=ALU.mult), r=[mu], w=[mu])
            kb.V(lambda e, o0=o0, mu=mu: e.tensor_tensor(out=o0[:], in0=o0[:], in1=mu[:].unsqueeze(2).to_broadcast([128, NH_, 64]), op=ALU.subtract),
                 r=[o0, mu], w=[o0])
            yield
            kb.G(lambda e, o0=o0, sq=sq: e.tensor_tensor(out=sq[:], in0=o0[:], in1=o0[:], op=ALU.mult), r=[o0], w=[sq])
            yield
            kb.V(lambda e, sq=sq, vr=vr: e.reduce_sum(out=vr[:], in_=sq[:], axis=AX.X), r=[sq], w=[vr])
            kb.V(lambda e, vr=vr: e.tensor_scalar(out=vr[:], in0=vr[:], scalar1=1.0 / 64, scalar2=64e-5, op0=ALU.mult, op1=ALU.add), r=[vr], w=[vr])
            yield
            kb.A(lambda e, vr=vr: e.activation(out=vr[:], in_=vr[:], func=AF.Ln), r=[vr], w=[vr])
            kb.A(lambda e, vr=vr: e.activation(out=vr[:], in_=vr[:], func=AF.Exp, scale=-0.5), r=[vr], w=[vr])
            yield
            kb.V(lambda e, o0=o0, vr=vr: e.tensor_tensor(out=o0[:], in0=o0[:], in1=vr[:].unsqueeze(2).to_broadcast([128, NH_, 64]), op=ALU.mult),
                 r=[o0, vr], w=[o0])
            yield
            kb.G(lambda e, o0=o0: e.tensor_tensor(out=f(o0), in0=f(o0), in1=lnw[:], op=ALU.mult), r=[o0, lnw], w=[o0])
            kb.G(lambda e, o0=o0: e.tensor_tensor(out=f(o0), in0=f(o0), in1=lnb[:], op=ALU.add), r=[o0, lnb], w=[o0])
            yield
            kb.V(lambda e, o0=o0, pb=pb: e.tensor_tensor(out=f(o0), in0=f(o0), in1=pb[:, 0:W_], op=ALU.add), r=[o0, pb], w=[o0])
            kb.V(lambda e, o0=o0, pg=pg, y=y: e.tensor_tensor(out=y[:], in0=f(o0), in1=pg[:, 0:W_], op=ALU.mult), r=[o0, pg], w=[y])
            kb.dma(Y[t0:t0 + 128, 512:512 + W_], y[:], r=[y], w=[y_buf[i]])
            yield

        for i0_ in range(0, NT, 3):
            run_interleaved([tile(i) for i in range(i0_, min(NT, i0_ + 3))])

EV_BLOCKS = [(0, 128, 0), (128, 128, 512), (256, 128, 1024), (384, 128, 1536), (512, 16, 2048), (528, 128, 2064), (656, 128, 2576),
             (784, 128, 3088), (912, 128, 3600), (1040, 128, 3728), (1168, 128, 3856)]
EV_NC = 1296
OD_BLOCKS = [(0, 128, 0), (128, 128, 512), (256, 128, 1024), (384, 128, 1536), (512, 128, 2048), (640, 128, 2560), (768, 128, 2688),
             (896, 128, 2816), (1024, 64, 2944)]
OD_NC = 1088
GROUPS = [[0, 1, 2, 3], [4, 5, 6, 7]]


def rope_table():
    inv = 10000.0 ** (-np.arange(16, dtype=np.float32) / 16)
    tab = np.zeros((64, 2, 128), np.float32)
    for half, n in ((0, 128), (1, 64)):
        ang = np.arange(n, dtype=np.float32)[None, :] * inv[:, None]
        for q in range(2):
            p0 = half * 32 + q * 16
            tab[p0:p0 + 16, 0, :n] = np.cos(ang)
            tab[p0:p0 + 16, 1, :n] = np.sin(ang)
    return tab


RW_SHAPES = {"rw_mu": [1920], "rw_w0": [2, 512], "rw_w2": [2, 64, 512], "rw_a0": [2, 512], "rw_a2": [2, 64, 512], "rw_g2": [128, 512],
             "rw_kk": [512], "rw_ka": [512], "rw_rk": [512], "rw_ln_w": [512], "rw_ln_b": [512]}
SMALL = {"ml_conv": [3, 1024], "ml_gate_b": [4, 4], "ml_norm": [512], "hg_lb": [2, 512], "hg_norm": [512], "mla_q_norm": [256],
         "mla_w_qb": [256, 768], "mla_kv_norm": [128], "mla_w_kvb": [128, 1024]}


def build_nc():
    nc = bass.Bass("TRN2", target_bir_lowering=False)
    di = lambda n, s: nc.dram_tensor(n, list(s), F32, kind="ExternalInput").ap()
    x = di("x", [8192, D]); ctx = di("ctx", [NCTX, D]); cvec = di("cvec", [2, D])
    ada_w = di("ada_w", [2, D, 6 * D]); ada_b = di("ada_b", [2, 6 * D]); norm_g = di("norm_g", [2, 4, D])
    ow = di("ow", [2, 256, D]); w1c = di("w1c", [2, D, D]); w2c = di("w2c", [2, D, D])
    ev_w = di("ev_w", [D, EV_NC]); od_w = di("od_w", [D, OD_NC])
    prm = {k: di(k, v) for k, v in RW_SHAPES.items()}
    sm = {k: di(k, v) for k, v in SMALL.items()}
    rope_tab = di("rope_tab", [64, 2, 128])
    out = nc.dram_tensor("out", [8192, D], F32, kind="ExternalOutput").ap()
    with contextlib.ExitStack() as st:
        kb = KB(nc, st)
        kb.eps_ap(EPS)
        XS = kb.dram("XS", [T, D]); PT = kb.dram("PT", [3984, T]); Y = kb.dram("Y", [T, D])
        ARi = kb.dram("ARi", [T, D]); ARo = kb.dram("ARo", [T, D])
        modrow = kb.dram("modrow", [2, 4, D])
        OD = [kb.dram("OD%d" % d, [T, 512]) for d in range(2)]
        BN = [kb.dram("BN%d" % d, [512, T]) for d in range(2)]
        xs_buf = [Buf() for _ in range(NT)]; y_buf = [Buf() for _ in range(NT)]; pt_bufs = [Buf() for _ in range(16)]
        ari_b = [Buf() for _ in range(NT)]; aro_b = [Buf() for _ in range(NT)]
        kb.dma(XS[0:NCTX, :], ctx[:, :], w=xs_buf[0:2])
        for i in range(8):
            kb.dma(XS[NCTX + i * 1024:NCTX + (i + 1) * 1024, :], x[i * 1024:(i + 1) * 1024, :], w=xs_buf[2 + i * 8:2 + (i + 1) * 8])
        keep = {k: kb.sb([128, 2, 8]) for k in ("A_pre", "B_pre", "A_mlp", "B_mlp")}
        for l in range(2):
            mb = phase_mod(kb, l, cvec, ada_w, ada_b, norm_g, modrow, keep)
            if l == 0:
                phase_proj(kb, XS, ev_w, EV_NC, EV_BLOCKS, PT, keep, xs_buf, pt_bufs)
                od_b = [[Buf() for _ in range(NCH)] for _ in range(2)]
                phase_mlstm(kb, PT, pt_bufs, sm["ml_conv"], sm["ml_gate_b"], OD, od_b, nh=1)
                phase_gla_merge(kb, OD, od_b, PT, pt_bufs, 1536, AF.Sigmoid, sm["ml_norm"], Y, y_buf, nh=1)
                od2 = [[Buf() for _ in range(NT)] for _ in range(2)]; bn_b = [[Buf() for _ in range(4)] for _ in range(2)]
                phase_rwkv(kb, PT, pt_bufs, prm, OD, od2, BN, bn_b, npair=1)
                phase_rwkv_merge(kb, OD, od2, BN, bn_b, PT, pt_bufs, prm, Y, y_buf, npair=1)
            else:
                phase_proj(kb, XS, od_w, OD_NC, OD_BLOCKS, PT, keep, xs_buf, pt_bufs)
                od_b = [[Buf() for _ in range(NCH)] for _ in range(2)]
                phase_hgrn(kb, PT, pt_bufs, sm["hg_lb"], OD, od_b, nh=1)
                phase_gla_merge(kb, OD, od_b, PT, pt_bufs, 2048, AF.Silu, sm["hg_norm"], Y, y_buf, nh=1)
                phase_mla(kb, PT, pt_bufs, sm["mla_q_norm"], sm["mla_w_qb"], sm["mla_kv_norm"], sm["mla_w_kvb"], rope_tab, Y, y_buf, nh=1)
            last = l == 1
            phase_post_tp(kb, l, XS, Y, out if last else None, ow[l], w1c[l], w2c[l], modrow, keep, xs_buf, y_buf, mb,
                          NCTX if last else 0, ARi, ARo, ari_b, aro_b, GROUPS)
        evs = [b.w for b in xs_buf[2:]]
        for e in ("sync", "gpsimd"):
            kb.P.wait_all(e, evs)
    return nc


def _roll(a, axis_shape, axis, g, base_axis):
    sh = list(a.shape)
    new = sh[:base_axis] + list(axis_shape) + sh[base_axis + 1:]
    return np.ascontiguousarray(np.roll(a.reshape(new), -g, axis=base_axis + axis).reshape(sh))


def kernel(**inputs):
    f = lambda k: np.ascontiguousarray(np.asarray(inputs[k], dtype=np.float32))
    nc = build_nc()
    xs, cs, ctxs, cc = f("x"), f("c"), f("ctx"), f("c_ctx")
    ev, od = f("ev_w_in")[0], f("od_w_in")[0]
    out_w, w1, w2 = f("out_w"), f("mlp_w1"), f("mlp_w2")
    common = {"ada_w": f("ada_w"), "ada_b": f("ada_b"), "norm_g": f("norm_g"), "rope_tab": rope_table(),
              "mla_q_norm": f("mla_q_norm")[0], "mla_kv_norm": f("mla_kv_norm")[0]}
    in_maps = []
    for core in range(8):
        b, g = core // 4, core % 4
        m = dict(common)
        m.update({"x": xs[b], "ctx": ctxs[b], "cvec": np.stack([cs[b], cc])})
        gates = _roll(ev[:, 2048:2064], (4, 4), 1, g, 1)
        s128 = lambda a, o: a[:, o + g * 128:o + (g + 1) * 128]
        m["ev_w"] = np.ascontiguousarray(np.concatenate(
            [s128(ev, 0), s128(ev, 512), s128(ev, 1024), s128(ev, 1536), gates, s128(ev, 2064), s128(ev, 2576), s128(ev, 3088),
             ev[:, 3600:3984]], axis=1))
        m["od_w"] = np.ascontiguousarray(np.concatenate(
            [s128(od, 0), s128(od, 512), s128(od, 1024), s128(od, 1536), s128(od, 2048), od[:, 2560:3008]], axis=1))
        m["ow"] = np.ascontiguousarray(np.stack([np.concatenate([out_w[l][g * 128:(g + 1) * 128], out_w[l][512 + g * 128:512 + (g + 1) * 128]])
                                                 for l in range(2)]))
        m["w1c"] = np.ascontiguousarray(w1[:, :, g * 1024:(g + 1) * 1024])
        m["w2c"] = np.ascontiguousarray(w2[:, g * 1024:(g + 1) * 1024, :])
        m["ml_conv"] = _roll(f("ml_conv")[0], (2, 4, 128), 1, g, 1)
        m["ml_gate_b"] = _roll(f("ml_gate_b")[0], (4,), 0, g, 1)
        m["ml_norm"] = _roll(f("ml_norm")[0], (4, 128), 0, g, 0)
        mu = f("rw_mu")[0]
        m["rw_mu"] = np.ascontiguousarray(np.concatenate([_roll(mu[0:1536], (3, 4, 128), 1, g, 0), mu[1536:]]))
        for k in ("rw_kk", "rw_ka", "rw_rk", "rw_ln_w", "rw_ln_b"):
            m[k] = _roll(f(k)[0], (4, 128), 0, g, 0)
        for k in ("rw_w0", "rw_a0"):
            m[k] = _roll(f(k)[0], (4, 128), 0, g, 1)
        for k in ("rw_w2", "rw_a2"):
            m[k] = _roll(f(k)[0], (4, 128), 0, g, 2)
        m["rw_g2"] = _roll(f("rw_g2")[0], (4, 128), 0, g, 1)
        m["hg_lb"] = _roll(f("hg_lb"), (4, 128), 0, g, 1)
        m["hg_norm"] = _roll(f("hg_norm")[0], (4, 128), 0, g, 0)
        m["mla_w_qb"] = _roll(f("mla_w_qb")[0], (4, 192), 0, g, 1)
        m["mla_w_kvb"] = _roll(f("mla_w_kvb")[0], (4, 256), 0, g, 1)
        in_maps.append(m)
    res = run_bass_kernel_spmd(nc, in_maps, core_ids=list(range(8)))
    return np.stack([res.results[4 * b]["out"] for b in range(2)]).astype(np.float32)
```
